# Optimizing a Trainium2 kernel written in Bass

```python
import jax, jax.numpy as jnp
from jax import lax
import numpy as np

D_MODEL = 1024
BATCH = 32
SEQ = 2048
DEPTH = 4

HEAD_DIM = 64
ROPE_THETA = 10000.0
EPS = 1e-6
DIL_GROUPS = ((128, 1), (512, 4), (2048, 16))
DIL_HEADS_PER_GROUP = D_MODEL // 256
DIL_HEADS = len(DIL_GROUPS) * DIL_HEADS_PER_GROUP
DIL_WIDTH = DIL_HEADS * HEAD_DIM
DIL_OUT_WIDTH = DIL_HEADS_PER_GROUP * HEAD_DIM
CONV_CH = D_MODEL // 2
CONV_WIDTH = 31
DIFF_HEADS = D_MODEL // 256
DIFF_WIDTH = DIFF_HEADS * 2 * HEAD_DIM
ATTN_BLOCK = 128
D_FF = ((8 * D_MODEL // 3 + 127) // 128) * 128
N_BRANCH = 3
IN_SPLIT_SIZES = (DIL_WIDTH, DIL_WIDTH, DIL_WIDTH, DIFF_WIDTH, DIFF_WIDTH, DIFF_WIDTH,
                  2 * CONV_CH, N_BRANCH * D_MODEL)
IN_WIDTH = sum(IN_SPLIT_SIZES)

kernel_name = 'hybrid_gated_dilated_conv_diffattn_macaron'


def _rmsnorm(x, g):
    xf = x.astype(jnp.float32)
    y = xf * lax.rsqrt(jnp.mean(xf * xf, axis=-1, keepdims=True) + EPS)
    return (y * g.astype(jnp.float32)).astype(x.dtype)


def _layernorm(x, g, b):
    xf = x.astype(jnp.float32)
    mu = jnp.mean(xf, axis=-1, keepdims=True)
    var = jnp.mean(jnp.square(xf - mu), axis=-1, keepdims=True)
    y = (xf - mu) * lax.rsqrt(var + EPS)
    return (y * g.astype(jnp.float32) + b.astype(jnp.float32)).astype(x.dtype)


def _swiglu(h, wg, wu, wd):
    return (jax.nn.silu(h @ wg) * (h @ wu)) @ wd


def _rope_tables(seq, dim):
    inv = ROPE_THETA ** (-jnp.arange(0, dim, 2, dtype=jnp.float32) / dim)
    ang = jnp.arange(seq, dtype=jnp.float32)[:, None] * inv[None, :]
    return jnp.cos(ang), jnp.sin(ang)


def _apply_rope(x, cos, sin):
    x1, x2 = jnp.split(x.astype(jnp.float32), 2, axis=-1)
    c, s = cos[:, None, :], sin[:, None, :]
    return jnp.concatenate([x1 * c - x2 * s, x2 * c + x1 * s], axis=-1).astype(x.dtype)


def _banded_window_attention(q, k, v, back):
    n, length, h, hd = q.shape
    blk = back
    nb = -(-length // blk)
    pad = nb * blk - length
    padw = ((0, 0), (0, pad), (0, 0), (0, 0))
    q, k, v = (jnp.pad(t, padw) for t in (q, k, v))
    qb = q.reshape(n, nb, blk, h, hd)
    kb = k.reshape(n, nb, blk, h, hd)
    vb = v.reshape(n, nb, blk, h, hd)
    prev = lambda t: jnp.pad(t, ((0, 0), (1, 0), (0, 0), (0, 0), (0, 0)))[:, :nb]
    kc = jnp.concatenate([prev(kb), kb], axis=2)
    vc = jnp.concatenate([prev(vb), vb], axis=2)
    s = jnp.einsum('nbqhd,nbkhd->nbhqk', qb, kc,
                   preferred_element_type=jnp.float32) * (hd ** -0.5)
    qpos = jnp.arange(blk)[:, None] + blk
    kpos = jnp.arange(2 * blk)[None, :]
    rel = qpos - kpos
    band = (rel >= 0) & (rel <= back)
    has_prev = (jnp.arange(nb) > 0)[:, None, None] | (kpos >= blk)[None]
    mask = band[None] & has_prev
    s = jnp.where(mask[None, :, None], s, -jnp.inf)
    m = jnp.max(s, axis=-1, keepdims=True)
    p = jnp.exp(s - m)
    denom = jnp.sum(p, axis=-1)
    o = jnp.einsum('nbhqk,nbkhd->nbqhd', p, vc.astype(jnp.float32))
    o = o / jnp.moveaxis(denom, 2, 3)[..., None]
    lse = jnp.moveaxis(m[..., 0] + jnp.log(denom), 2, 3)
    o = o.reshape(n, nb * blk, h, hd)[:, :length]
    lse = lse.reshape(n, nb * blk, h)[:, :length]
    return o, lse


def _dilated_group(q, k, v, window, dilation):
    b, s, h, hd = q.shape
    sub_len = s // dilation
    def split_phase(t):
        return t.reshape(b, sub_len, dilation, h, hd).transpose(0, 2, 1, 3, 4).reshape(
            b * dilation, sub_len, h, hd)
    o, lse = _banded_window_attention(split_phase(q), split_phase(k), split_phase(v),
                                      window // dilation)
    o = o.reshape(b, dilation, sub_len, h, hd).transpose(0, 2, 1, 3, 4).reshape(b, s, h, hd)
    lse = lse.reshape(b, dilation, sub_len, h).transpose(0, 2, 1, 3).reshape(b, s, h)
    return o, lse


def _dilated_attention(q, k, v):
    b, s = q.shape[:2]
    outs, lses = [], []
    for g, (window, dilation) in enumerate(DIL_GROUPS):
        hs = slice(g * DIL_HEADS_PER_GROUP, (g + 1) * DIL_HEADS_PER_GROUP)
        o, lse = _dilated_group(q[:, :, hs], k[:, :, hs], v[:, :, hs], window, dilation)
        outs.append(o)
        lses.append(lse)
    wts = jax.nn.softmax(jnp.stack(lses, axis=0), axis=0)
    o = jnp.sum(wts[..., None] * jnp.stack(outs, axis=0), axis=0)
    return o.reshape(b, s, DIL_OUT_WIDTH)


def _conformer_conv(u, conv_w, conv_b, norm_g, norm_b):
    a, gt = jnp.split(u, 2, axis=-1)
    z = a * jax.nn.sigmoid(gt)
    zp = jnp.pad(z, ((0, 0), (CONV_WIDTH - 1, 0), (0, 0)))
    y = lax.conv_general_dilated(zp, conv_w[:, None, :], window_strides=(1,), padding='VALID',
                                 dimension_numbers=('NWC', 'WIO', 'NWC'),
                                 feature_group_count=CONV_CH) + conv_b
    return jax.nn.silu(_layernorm(y, norm_g, norm_b))


def _lambda_init(layer):
    return 0.8 - 0.6 * float(np.exp(-0.3 * layer))


def _diff_attention(q, k, v, lam, subln_g, lambda_init):
    b, s = q.shape[:2]
    scale = HEAD_DIM ** -0.5
    outs = []
    for i in range(s // ATTN_BLOCK):
        lo, hi = i * ATTN_BLOCK, (i + 1) * ATTN_BLOCK
        sc = jnp.einsum('bqhcd,bkhcd->bhcqk', q[:, lo:hi], k[:, :hi],
                        preferred_element_type=jnp.float32) * scale
        causal = (lo + jnp.arange(ATTN_BLOCK))[:, None] >= jnp.arange(hi)[None, :]
        p = jax.nn.softmax(jnp.where(causal, sc, -jnp.inf), axis=-1)
        a = p[:, :, 0] - lam * p[:, :, 1]
        outs.append(jnp.einsum('bhqk,bkhe->bqhe', a, v[:, :hi].astype(jnp.float32)))
    o = jnp.concatenate(outs, axis=1)
    o = _rmsnorm(o, subln_g) * (1.0 - lambda_init)
    return o.reshape(b, s, DIFF_WIDTH)


def setup_inputs(seed: int = 0) -> dict:
    key = jax.random.key(seed)
    ks = iter(jax.random.split(key, 32))
    nrm = lambda shape, scale: scale * jax.random.normal(next(ks), shape, jnp.float32)
    gain = lambda shape: 1.0 + nrm(shape, 0.02)
    L = DEPTH
    return {
        'x': nrm((BATCH, SEQ, D_MODEL), 1.0),
        'ffn1_norm_pre': gain((L, D_MODEL)),
        'ffn1_norm_post': gain((L, D_MODEL)),
        'ffn1_w_gate': nrm((L, D_MODEL, D_FF), D_MODEL ** -0.5),
        'ffn1_w_up': nrm((L, D_MODEL, D_FF), D_MODEL ** -0.5),
        'ffn1_w_down': nrm((L, D_FF, D_MODEL), D_FF ** -0.5),
        'mix_norm_pre': gain((L, D_MODEL)),
        'mix_norm_post': gain((L, D_MODEL)),
        'w_in': nrm((L, D_MODEL, IN_WIDTH), D_MODEL ** -0.5),
        'b_gate': nrm((L, N_BRANCH * D_MODEL), 0.1),
        'conv_w': nrm((L, CONV_WIDTH, CONV_CH), CONV_WIDTH ** -0.5),
        'conv_b': nrm((L, CONV_CH), 0.02),
        'conv_norm_g': gain((L, CONV_CH)),
        'conv_norm_b': nrm((L, CONV_CH), 0.02),
        'lambda_q1': nrm((L, HEAD_DIM), 0.1),
        'lambda_k1': nrm((L, HEAD_DIM), 0.1),
        'lambda_q2': nrm((L, HEAD_DIM), 0.1),
        'lambda_k2': nrm((L, HEAD_DIM), 0.1),
        'diff_subln': gain((L, 2 * HEAD_DIM)),
        'w_proj_a': nrm((L, DIL_OUT_WIDTH, D_MODEL), DIL_OUT_WIDTH ** -0.5),
        'w_proj_b': nrm((L, CONV_CH, D_MODEL), CONV_CH ** -0.5),
        'w_proj_c': nrm((L, DIFF_WIDTH, D_MODEL), DIFF_WIDTH ** -0.5),
        'w_out': nrm((L, D_MODEL, D_MODEL), D_MODEL ** -0.5),
        'ffn2_norm_pre': gain((L, D_MODEL)),
        'ffn2_norm_post': gain((L, D_MODEL)),
        'ffn2_w_gate': nrm((L, D_MODEL, D_FF), D_MODEL ** -0.5),
        'ffn2_w_up': nrm((L, D_MODEL, D_FF), D_MODEL ** -0.5),
        'ffn2_w_down': nrm((L, D_FF, D_MODEL), D_FF ** -0.5),
    }


def reference(x, ffn1_norm_pre, ffn1_norm_post, ffn1_w_gate, ffn1_w_up, ffn1_w_down,
              mix_norm_pre, mix_norm_post, w_in, b_gate, conv_w, conv_b, conv_norm_g, conv_norm_b,
              lambda_q1, lambda_k1, lambda_q2, lambda_k2, diff_subln,
              w_proj_a, w_proj_b, w_proj_c, w_out,
              ffn2_norm_pre, ffn2_norm_post, ffn2_w_gate, ffn2_w_up, ffn2_w_down):
    b, s, _ = x.shape
    cos, sin = _rope_tables(s, HEAD_DIM)
    offsets = [int(o) for o in np.cumsum(IN_SPLIT_SIZES)[:-1]]
    f32 = jnp.float32
    for l in range(DEPTH):
        h = _rmsnorm(x, ffn1_norm_pre[l])
        x = x + 0.5 * _rmsnorm(_swiglu(h, ffn1_w_gate[l], ffn1_w_up[l], ffn1_w_down[l]),
                               ffn1_norm_post[l])

        h = _rmsnorm(x, mix_norm_pre[l])
        proj = h @ w_in[l]
        qa, ka, va, qc, kc, vc, conv_in, gate_pre = jnp.split(proj, offsets, axis=-1)

        qa = _apply_rope(qa.reshape(b, s, DIL_HEADS, HEAD_DIM), cos, sin)
        ka = _apply_rope(ka.reshape(b, s, DIL_HEADS, HEAD_DIM), cos, sin)
        va = va.reshape(b, s, DIL_HEADS, HEAD_DIM)
        oa = _dilated_attention(qa, ka, va).astype(x.dtype)

        ob = _conformer_conv(conv_in, conv_w[l], conv_b[l], conv_norm_g[l], conv_norm_b[l])

        lam_init = _lambda_init(l)
        lam = (jnp.exp(jnp.sum(lambda_q1[l].astype(f32) * lambda_k1[l].astype(f32)))
               - jnp.exp(jnp.sum(lambda_q2[l].astype(f32) * lambda_k2[l].astype(f32)))
               + lam_init)
        qc = _apply_rope(qc.reshape(b, s, 2 * DIFF_HEADS, HEAD_DIM), cos, sin).reshape(
            b, s, DIFF_HEADS, 2, HEAD_DIM)
        kc = _apply_rope(kc.reshape(b, s, 2 * DIFF_HEADS, HEAD_DIM), cos, sin).reshape(
            b, s, DIFF_HEADS, 2, HEAD_DIM)
        vc = vc.reshape(b, s, DIFF_HEADS, 2 * HEAD_DIM)
        oc = _diff_attention(qc, kc, vc, lam, diff_subln[l], lam_init).astype(x.dtype)

        ya = oa @ w_proj_a[l]
        yb = ob @ w_proj_b[l]
        yc = oc @ w_proj_c[l]
        gates = jax.nn.sigmoid(gate_pre + b_gate[l]).reshape(b, s, N_BRANCH, D_MODEL)
        merged = gates[:, :, 0] * ya + gates[:, :, 1] * yb + gates[:, :, 2] * yc
        x = x + _rmsnorm(merged @ w_out[l], mix_norm_post[l])

        h = _rmsnorm(x, ffn2_norm_pre[l])
        x = x + 0.5 * _rmsnorm(_swiglu(h, ffn2_w_gate[l], ffn2_w_up[l], ffn2_w_down[l]),
                               ffn2_norm_post[l])
    return x
```

```python
import contextlib
import numpy as np
import concourse.bass as bass
import concourse.mybir as mybir
from concourse.bass_utils import run_bass_kernel_spmd

F32 = mybir.dt.float32
BF16 = mybir.dt.bfloat16
AF = mybir.ActivationFunctionType
ALU = mybir.AluOpType

PE, ACT, DVE, POOL, SP = "pe", "act", "dve", "pool", "sp"

D = 1024
SEQ = 2048
DFF = 2816
NFF = 22
NCORES = 8
SEQ_PER_CORE = 4
DEPTH = 4
EPS = 1e-6
DIL = (1, 4, 16)

GU_SZ = NFF * 128 * 2 * 8 * 128
D_SZ = 8 * 128 * NFF * 128
WA_SZ = 6 * 128 * 3 * 8 * 128
WB_SZ = 4 * 128 * 2 * 8 * 128
WC_SZ = 4 * 128 * 3 * 8 * 128
GG_SZ = 8 * 128 * 3 * 8 * 128
GP_SZ = 8 * 128 * 10 * 128
WO_SZ = 8 * 128 * 8 * 128
OFF = {}
_o = 0
for _n, _s in (("gu1", GU_SZ), ("d1", D_SZ), ("wa", WA_SZ), ("wb", WB_SZ), ("wc", WC_SZ),
               ("gg", GG_SZ), ("gp", GP_SZ), ("wo", WO_SZ), ("gu2", GU_SZ), ("d2", D_SZ)):
    OFF[_n] = (_o, _s)
    _o += _s
WL_SZ = _o
assert WL_SZ % 128 == 0

PC_NORM = 0
PC_BG = 48
PC_CW = 72
PC_CB = 196
PC_CG = 200
PC_CNB = 204
PC_SUB = 208
PC_N = 212
LAMC = 4 * 64


def _lam_init(layer):
    return 0.8 - 0.6 * float(np.exp(-0.3 * layer))


class Buf:
    __slots__ = ("name", "lw", "rd", "sem", "cnt", "excl")

    def __init__(self, name, excl=False):
        self.name = name
        self.excl = excl
        self.lw = None
        self.rd = {}
        self.sem = None
        self.cnt = 0


class Prog:
    def __init__(self, nc):
        self.nc = nc
        self.ops = []
        self.stack = contextlib.ExitStack()
        self.n_dma_sems = 0
        self.maxops = None
        self.force = False

    def sbuf(self, name, shape, dt):
        return self.stack.enter_context(self.nc.sbuf_tensor(name, shape, dt))

    def psum(self, name, shape, dt):
        return self.stack.enter_context(self.nc.psum_tensor(name, shape, dt))

    def _deps(self, eng, reads, writes, is_dma, nodep=False):
        deps = set()
        idx = len(self.ops)
        key = ("dma", idx) if is_dma else eng
        xreads = [b for b in reads if b.excl]
        for b in reads:
            if b.lw is not None:
                deps.add(b.lw)
        if not nodep:
            for b in list(writes) + xreads:
                if b.lw is not None:
                    deps.add(b.lw)
                deps.update(b.rd.values())
        for b in reads:
            if not b.excl:
                b.rd[key] = idx
        for b in list(writes) + xreads:
            b.lw = idx
            b.rd = {}
        deps.discard(idx)
        return deps

    def op(self, eng, fn, reads=(), writes=()):
        if self.maxops is not None and len(self.ops) >= self.maxops and not self.force:
            return
        deps = self._deps(eng, reads, writes, False)
        self.ops.append([eng, fn, deps, False, None, 0, False, 0, frozenset(reads), frozenset(writes)])

    def dma(self, queue, out_ap, in_ap, reads=(), writes=(), semb=None, nodep=False):
        if self.maxops is not None and len(self.ops) >= self.maxops and not self.force:
            return
        if semb is None:
            semb = writes[0] if writes else reads[0]
        if semb.sem is None:
            semb.sem = self.n_dma_sems
            self.n_dma_sems += 1
        deps = self._deps(queue, reads, writes, True, nodep)
        semb.cnt += 16
        fn = lambda e, o=out_ap, i=in_ap: e.dma_start(out=o, in_=i)
        self.ops.append([queue, fn, deps, True, semb, semb.cnt, False, 0, frozenset(reads), frozenset(writes)])

    def emit(self, final_wait_bufs=()):
        nc = self.nc
        ops = self.ops
        for o in ops:
            for d in o[2]:
                if not ops[d][3]:
                    ops[d][6] = True
        counts = {PE: 0, ACT: 0, DVE: 0, POOL: 0, SP: 0}
        for o in ops:
            if o[6]:
                counts[o[0]] += 1
                o[7] = counts[o[0]]
        st = self.stack
        esem = {e: st.enter_context(nc.semaphore("s_" + e)) for e in (PE, ACT, DVE, POOL, SP)}
        dsem = [st.enter_context(nc.semaphore("d%d" % i)) for i in range(self.n_dma_sems)]
        per_eng = {PE: [], ACT: [], DVE: [], POOL: [], SP: []}
        for i, o in enumerate(ops):
            per_eng[o[0]].append(i)
        self.n_waits = 0

        def run(eng_name, e):
            known = {}
            for i in per_eng[eng_name]:
                o = ops[i]
                need = {}
                for d in o[2]:
                    od = ops[d]
                    if od[3]:
                        key = ("d", od[4].sem)
                        val = od[5]
                    else:
                        if od[0] == eng_name:
                            if eng_name == PE:
                                continue
                            if not (od[9] & o[8]):
                                continue
                        key = ("e", od[0])
                        val = od[7]
                    if known.get(key, 0) >= val:
                        continue
                    if need.get(key, 0) < val:
                        need[key] = val
                for key, val in need.items():
                    sem = dsem[key[1]] if key[0] == "d" else esem[key[1]]
                    e.wait_ge(sem, val)
                    known[key] = val
                    self.n_waits += 1
                ins = o[1](e)
                if o[3]:
                    ins.then_inc(dsem[o[4].sem], 16)
                elif o[6]:
                    ins.then_inc(esem[eng_name], 1)
            if eng_name == SP:
                for b in final_wait_bufs:
                    e.wait_ge(dsem[b.sem], b.cnt)

        with nc.Block() as block:
            @block.tensor
            def _(e):
                run(PE, e)

            @block.scalar
            def _(e):
                run(ACT, e)

            @block.vector
            def _(e):
                run(DVE, e)

            @block.gpsimd
            def _(e):
                run(POOL, e)

            @block.sync
            def _(e):
                run(SP, e)
        st.close()


def build_program(n_layers, n_seq, lam_inits, stages=("ffn1", "mixer", "ffn2"), dbg=False):
    nc = bass.Bass("TRN2", target_bir_lowering=False)
    P = Prog(nc)
    import os as _os
    if _os.environ.get("KCUT"):
        P.maxops = int(_os.environ["KCUT"])
    xin = nc.dram_tensor("xin", [n_seq, 8, 128, SEQ], F32, kind="ExternalInput").ap()
    wts = nc.dram_tensor("wts", [n_layers, 128, WL_SZ // 128], F32, kind="ExternalInput").ap()
    prm = nc.dram_tensor("prm", [128, n_layers * PC_N], F32, kind="ExternalInput").ap()
    lamrows = nc.dram_tensor("lamrows", [128, n_layers * LAMC], F32, kind="ExternalInput").ap()
    rope = nc.dram_tensor("rope", [4, 128, 2, 512], F32, kind="ExternalInput").ap()
    cmat = nc.dram_tensor("cmat", [128, 7 * 128 + 256], F32, kind="ExternalInput").ap()
    yout = nc.dram_tensor("yout", [n_seq, 8, 128, SEQ], F32, kind="ExternalOutput").ap()
    if dbg:
        dbg_out = nc.dram_tensor("dbg_out", [10, 128, SEQ], BF16, kind="ExternalOutput").ap()
        Bdbg = Buf("dbg")
    wsc = nc.dram_tensor("wsc", [n_layers, 128, WL_SZ // 128], BF16, kind="Internal").ap()
    wflat = [wsc[l].rearrange("p x -> (p x)") for l in range(n_layers)]

    xT = P.sbuf("xT", [128, 8, SEQ], F32)
    Bx = [[Buf("x%d_%d" % (c, t)) for t in range(4)] for c in range(8)]
    NSLOT, SLOT = 3, 3072
    ring = [P.sbuf("ring%d" % i, [128, SLOT], BF16) for i in range(NSLOT)]
    Bring = [Buf("ring%d" % i) for i in range(NSLOT)]
    prm_sb = P.sbuf("prm_sb", [128, n_layers * PC_N], F32)
    Bprm = Buf("prm")
    cm = P.sbuf("cm", [128, 7 * 128 + 256], BF16)
    Bcm = Buf("cm")
    identf = P.sbuf("identf", [128, 128], F32)
    Bidf = Buf("identf")
    lam_sb = P.sbuf("lam_sb", [128, 2 * n_layers], F32)
    Blam = Buf("lam")
    rstd_t = [P.sbuf("rstd%d" % i, [128, 512], F32) for i in range(2)]
    Brstd = [Buf("rstd%d" % i) for i in range(2)]
    rstd_mix = P.sbuf("rstd_mix", [128, SEQ], F32)
    Brmix = [Buf("rmix%d" % t) for t in range(4)]
    rstd_pre = [rstd_mix[:, 0:512], rstd_mix[:, 512:1024]]
    Brpre = [Brmix[0], Brmix[1]]
    SCR = 106 * 1024
    scr = P.sbuf("scr", [128, SCR // 2], BF16)

    def carve(off_bytes, shape, dt):
        n = int(np.prod(shape))
        esz = 4 if dt == F32 else 2
        assert off_bytes % 4 == 0 and off_bytes + n * esz <= SCR, (off_bytes, n * esz, SCR)
        ap = scr[:, off_bytes // 2: off_bytes // 2 + n * esz // 2]
        if dt == F32:
            ap = ap.bitcast(F32)
        if len(shape) == 2:
            ap = ap.rearrange("p (a b) -> p a b", a=shape[0])
        elif len(shape) == 3:
            ap = ap.rearrange("p (a b c) -> p a b c", a=shape[0], b=shape[1])
        return ap

    dummy = P.sbuf("dummy_bar", [128, 2], F32)
    phase = {"bar": None}

    def phase_begin(bufs):
        for b in bufs:
            b.lw = phase["bar"]
            b.rd = {}

    def phase_end(bufs):
        bufs = list(bufs)
        P.op(DVE, lambda e: e.memset(dummy[:, 0:1], 0.0), reads=bufs, writes=bufs)
        phase["bar"] = len(P.ops) - 1

    eps_vals = [float(EPS)] + [float(EPS / (1.0 - float(li)) ** 2) for li in lam_inits]
    eps_sb = P.sbuf("eps_sb", [128, len(eps_vals)], F32)
    Beps = Buf("eps")
    epsc = {}
    for i, v in enumerate(eps_vals):
        epsc[v] = eps_sb[:, i:i + 1]
        P.op(DVE, lambda e, i=i, v=v: e.memset(eps_sb[:, i:i + 1], v), writes=[Beps])
    PS = [P.psum("ps%d" % i, [128, 512], F32) for i in range(8)]
    BPS = [Buf("ps%d" % i, excl=True) for i in range(8)]

    AVG1024 = cm[:, 0:128]
    AVG512 = cm[:, 128:256]
    AVG128 = cm[:, 256:384]
    ONES = cm[:, 384:512]
    RMT = cm[:, 512:640]
    MASK = cm[:, 896:1152]

    allx = [Bx[c][t] for c in range(8) for t in range(4)]
    P.dma(POOL, xT[:], xin[0].rearrange("c p t -> p c t"), writes=allx, semb=allx[0])
    P.dma(POOL, cm[:], cmat[:, :], writes=[Bcm])
    P.dma(SP, identf[:], cmat[:, 640:768], writes=[Bidf])
    P.dma(SP, prm_sb[:], prm[:, :], writes=[Bprm])
    lr = carve(0, [n_layers * LAMC], F32)
    Blr = Buf("lr")
    P.dma(SP, lr, lamrows[:, :], writes=[Blr])
    lt = carve(n_layers * LAMC * 4, [n_layers * 128 + 4 * n_layers], F32)
    Blt = Buf("lt")
    for l in range(n_layers):
        b0 = l * LAMC
        prod = lt[:, l * 128: l * 128 + 128]
        sums = lt[:, n_layers * 128 + 4 * l: n_layers * 128 + 4 * l + 4]
        P.op(DVE, lambda e, o=prod[:, 0:64], a=lr[:, b0:b0 + 64], b=lr[:, b0 + 64:b0 + 128]:
             e.tensor_tensor(o, a, b, ALU.mult), reads=[Blr], writes=[Blt])
        P.op(DVE, lambda e, o=prod[:, 64:128], a=lr[:, b0 + 128:b0 + 192], b=lr[:, b0 + 192:b0 + 256]:
             e.tensor_tensor(o, a, b, ALU.mult), reads=[Blr], writes=[Blt])
        P.op(DVE, lambda e, o=sums[:, 0:1], i=prod[:, 0:64]:
             e.reduce_sum(o, i, mybir.AxisListType.X), reads=[Blt], writes=[Blt])
        P.op(DVE, lambda e, o=sums[:, 1:2], i=prod[:, 64:128]:
             e.reduce_sum(o, i, mybir.AxisListType.X), reads=[Blt], writes=[Blt])
        P.op(ACT, lambda e, o=sums[:, 2:4], i=sums[:, 0:2]:
             e.activation(out=o, in_=i, func=AF.Exp), reads=[Blt], writes=[Blt])
        P.op(DVE, lambda e, o=lam_sb[:, l:l + 1], a=sums[:, 2:3], b=sums[:, 3:4]:
             e.tensor_tensor(o, a, b, ALU.subtract), reads=[Blt], writes=[Blam])
        P.op(DVE, lambda e, o=lam_sb[:, l:l + 1], li=float(lam_inits[l]):
             e.tensor_scalar(o, o, li, None, ALU.add), reads=[Blam], writes=[Blam])
        P.op(DVE, lambda e, o=lam_sb[:, n_layers + l:n_layers + l + 1], i=lam_sb[:, l:l + 1]:
             e.tensor_scalar(o, i, -1.0, None, ALU.mult), reads=[Blam], writes=[Blam])

    phase_end([Blr, Blt])
    Bw = {}
    WCOLS = WL_SZ // 128
    for l in range(n_layers):
        for name, (o, s) in OFF.items():
            Bw[(l, name)] = Buf("w%d%s" % (l, name))
    wflat32 = [wts[l].rearrange("p x -> (p x)") for l in range(n_layers)]
    for l in range(n_layers):
        for name, (o, s) in OFF.items():
            cols = s // 128
            src = wflat32[l][o:o + s].rearrange("(p x) -> p x", p=128)
            dst = wflat[l][o:o + s].rearrange("(p x) -> p x", p=128)
            step = 8192
            for c0 in range(0, cols, step):
                c1 = min(cols, c0 + step)
                P.dma(POOL, dst[:, c0:c1], src[:, c0:c1], writes=[Bw[(l, name)]], semb=Bw[(l, name)], nodep=True)

    wstate = {"i": 0}

    def wload(l, name, elem_off, per_part, npanel=1):
        i = wstate["i"]
        wstate["i"] += 1
        slot = ring[i % NSLOT]
        sb = Bring[i % NSLOT]
        o, s = OFF[name]
        assert per_part <= SLOT and elem_off + 128 * per_part <= s
        if npanel == 1:
            src = wflat[l][o + elem_off: o + elem_off + 128 * per_part].rearrange("(p x) -> p x", p=128)
            dst = slot[:, 0:per_part]
        else:
            src = wflat[l][o + elem_off: o + elem_off + 128 * per_part].rearrange(
                "(f p x) -> p f x", f=npanel, p=128)
            dst = slot[:, 0:per_part].rearrange("p (f x) -> p f x", f=npanel)
        P.dma(SP, dst, src, reads=[Bw[(l, name)]], writes=[sb], semb=sb)
        return slot, sb

    sqk = {"i": 0, "r": 0, "t": 0}

    def pcol(l, col):
        return prm_sb[:, l * PC_N + col: l * PC_N + col + 1]

    def mm(out, lhsT, rhs, start, stop, reads, writes):
        P.op(PE, lambda e: e.matmul(out, lhsT, rhs, start=start, stop=stop, skip_group_check=True),
             reads=reads, writes=writes)

    def stats_accum(sqbufs, Bsq, src_ap, src_bufs, ps_stat, Bps_stat, avg, first, last, scale=1.0, N=512):
        k = sqk["i"] % 2
        sqk["i"] += 1
        sq = sqbufs[k]
        P.op(ACT, lambda e: e.activation(out=sq[:, 0:N], in_=src_ap, func=AF.Square, scale=scale),
             reads=src_bufs, writes=[Bsq[k]])
        mm(ps_stat[:, 0:N], avg, sq[:, 0:N], first, last, [Bsq[k], Bcm], [Bps_stat])

    def rstd_from(ps_stat, Bps_stat, out_ap, out_bufs, eps=EPS, N=512, lnexp=True):
        if lnexp:
            P.op(ACT, lambda e: e.activation(out=out_ap, in_=ps_stat[:, 0:N], func=AF.Ln, bias=epsc[float(eps)]),
                 reads=[Bps_stat, Beps], writes=out_bufs)
            P.op(ACT, lambda e: e.activation(out=out_ap, in_=out_ap, func=AF.Exp, scale=-0.5),
                 reads=out_bufs, writes=out_bufs)
            return
        P.op(ACT, lambda e: e.activation(out=out_ap, in_=ps_stat[:, 0:N], func=AF.Sqrt, bias=epsc[float(eps)]),
             reads=[Bps_stat, Beps], writes=out_bufs)
        P.op(DVE, lambda e: e.reciprocal(out_ap, out_ap), reads=out_bufs, writes=out_bufs)

    def ffn(l, which, lay):
        gname, dname = ("gu1", "d1") if which == 0 else ("gu2", "d2")
        ncol_pre = PC_NORM + (0 if which == 0 else 32)
        ncol_post = ncol_pre + 8
        hT, Bh, actT, Bact, yT, By, sqb, Bsq, tmpA, BtA, tmpB, BtB = lay

        def prenorm(half, ps_i=7):
            for tgi in range(2):
                tg = half * 2 + tgi
                tok = slice(tg * 512, tg * 512 + 512)
                r = sqk["r"] % 2
                sqk["r"] += 1
                for c in range(8):
                    stats_accum(sqb, Bsq, xT[:, c, tok], [Bx[c][tg]], PS[ps_i], BPS[ps_i], AVG1024, c == 0, c == 7)
                rstd_from(PS[ps_i], BPS[ps_i], rstd_pre[tgi][:], [Brpre[tgi]])
                for c in range(8):
                    P.op(DVE, lambda e, c=c, tok=tok, r=tgi, tgi=tgi: e.scalar_tensor_tensor(
                        out=hT[:, c, tgi * 512: tgi * 512 + 512], in0=xT[:, c, tok], scalar=pcol(l, ncol_pre + c),
                        in1=rstd_pre[tgi][:], op0=ALU.mult, op1=ALU.mult),
                        reads=[Bx[c][tg], Brpre[tgi], Bprm], writes=[Bh[c][tgi]])

        def gateup(half):
            for j in range(NFF):
                slot, sb = wload(l, gname, j * 128 * 2048, 2048)
                sv = slot[:, 0:2048].rearrange("p (g c n) -> p g c n", g=2, c=8)
                for tgi in range(2):
                    tg = half * 2 + tgi
                    tok = slice(tg * 512, tg * 512 + 512)
                    pg = (2 * j + tgi) % 2
                    psg, psu = PS[pg * 2], PS[pg * 2 + 1]
                    Bg, Bu = BPS[pg * 2], BPS[pg * 2 + 1]
                    for gu, (pp, Bp) in enumerate(((psg, Bg), (psu, Bu))):
                        for c in range(8):
                            mm(pp[:], sv[:, gu, c, :], hT[:, c, tgi * 512: tgi * 512 + 512], c == 0, c == 7,
                               [sb, Bh[c][tgi]], [Bp])
                    k = sqk["t"] % 2
                    sqk["t"] += 1
                    P.op(ACT, lambda e, k=k, psg=psg: e.activation(out=tmpA[k], in_=psg[:], func=AF.Silu),
                         reads=[Bg], writes=[BtA[k]])
                    P.op(DVE, lambda e, k=k, psu=psu, j=j, tgi=tgi: e.tensor_tensor(
                        actT[:, j, tgi * 512: tgi * 512 + 512], tmpA[k], psu[:], ALU.mult),
                        reads=[BtA[k], Bu], writes=[Bact[j][tgi]])

        def down(half):
            for f in range(8):
                slot, sb = wload(l, dname, f * 128 * NFF * 128, NFF * 128)
                sv = slot[:, 0:NFF * 128].rearrange("p (c n) -> p c n", c=NFF)
                for tgi in range(2):
                    pd = PS[4 + tgi]
                    Bpd = BPS[4 + tgi]
                    for cc in range(NFF):
                        mm(pd[:], sv[:, cc, :], actT[:, cc, tgi * 512: tgi * 512 + 512], cc == 0, cc == NFF - 1,
                           [sb, Bact[cc][tgi]], [Bpd])
                    P.op(DVE, lambda e, f=f, pd=pd, tgi=tgi: e.tensor_copy(yT[:, f, tgi, :], pd[:]),
                         reads=[Bpd], writes=[By[f][tgi]])
                    stats_accum(sqb, Bsq, pd[:], [Bpd], PS[6 + tgi], BPS[6 + tgi], AVG1024, f == 0, f == 7)

        def down_resid(half):
            for tgi in range(2):
                tg = half * 2 + tgi
                tok = slice(tg * 512, tg * 512 + 512)
                r = sqk["r"] % 2
                sqk["r"] += 1
                rstd_from(PS[6 + tgi], BPS[6 + tgi], rstd_t[r][:], [Brstd[r]])
                for f in range(8):
                    k = sqk["t"] % 2
                    sqk["t"] += 1
                    P.op(DVE, lambda e, f=f, r=r, k=k, tgi=tgi: e.scalar_tensor_tensor(
                        out=tmpB[k], in0=yT[:, f, tgi, :], scalar=pcol(l, ncol_post + f), in1=rstd_t[r][:],
                        op0=ALU.mult, op1=ALU.mult), reads=[By[f][tgi], Brstd[r], Bprm], writes=[BtB[k]])
                    P.op(DVE, lambda e, f=f, k=k, tok=tok: e.scalar_tensor_tensor(
                        out=xT[:, f, tok], in0=tmpB[k], scalar=0.5, in1=xT[:, f, tok],
                        op0=ALU.mult, op1=ALU.add), reads=[BtB[k], Bx[f][tg]], writes=[Bx[f][tg]])

        prenorm(0)
        gateup(0)
        down(0)
        prenorm(1, ps_i=3)
        down_resid(0)
        gateup(1)
        down(1)
        down_resid(1)

    def ffn_layout():
        o = 0
        hT = carve(o, [8, 1024], BF16); o += 8 * 1024 * 2
        actT = carve(o, [NFF, 1024], BF16); o += NFF * 1024 * 2
        yT = carve(o, [8, 2, 512], F32); o += 8 * 1024 * 4
        sqb = [carve(o + i * 1024, [512], BF16) for i in range(2)]; o += 2048
        tmpA = [carve(o + i * 2048, [512], F32) for i in range(2)]; o += 4096
        tmpB = [carve(o + i * 2048, [512], F32) for i in range(2)]; o += 4096
        return (hT, [[Buf("fh%d_%d" % (c, t)) for t in range(2)] for c in range(8)],
                actT, [[Buf("act%d_%d" % (j, t)) for t in range(2)] for j in range(NFF)],
                yT, [[Buf("y%d_%d" % (f, t)) for t in range(2)] for f in range(8)],
                sqb, [Buf("sq0"), Buf("sq1")], tmpA, [Buf("tA0"), Buf("tA1")], tmpB, [Buf("tB0"), Buf("tB1")])

    def run_pipeline(iters, skew=2):
        n = len(iters)
        for i in range(n + skew):
            if i < n:
                iters[i][0]()
            if i >= skew:
                iters[i - skew][1]()

    def mixer(l):
        hT = carve(0, [8, SEQ], BF16)
        Bh = [[Buf("mh%d_%d" % (c, t)) for t in range(4)] for c in range(8)]
        oa = carve(32768, [2, SEQ], BF16)
        Boa = [[Buf("oa%d_%d" % (c, t)) for t in range(4)] for c in range(2)]
        sqb = [carve(40960 + i * 1024, [512], BF16) for i in range(2)]
        Bsq = [Buf("msq0"), Buf("msq1")]
        ob = carve(43008, [4, SEQ], BF16)
        Bob = [[Buf("ob%d_%d" % (c, t)) for t in range(4)] for c in range(4)]
        oc = carve(59392, [4, SEQ], BF16)
        Boc = [[Buf("oc%d_%d" % (c, t)) for t in range(4)] for c in range(4)]
        flat = lambda ll: [b for row in ll for b in row]
        wide = flat(Bh) + flat(Boa) + flat(Bob) + flat(Boc) + Bsq
        phase_begin(wide)

        for tg in range(4):
            tok = slice(tg * 512, tg * 512 + 512)
            for c in range(8):
                stats_accum(sqb, Bsq, xT[:, c, tok], [Bx[c][tg]], PS[7], BPS[7], AVG1024, c == 0, c == 7)
            rstd_from(PS[7], BPS[7], rstd_mix[:, tok], [Brmix[tg]])
            for c in range(8):
                P.op(DVE, lambda e, c=c, tok=tok: e.scalar_tensor_tensor(
                    out=hT[:, c, tok], in0=xT[:, c, tok], scalar=pcol(l, PC_NORM + 16 + c),
                    in1=rstd_mix[:, tok], op0=ALU.mult, op1=ALU.mult),
                    reads=[Bx[c][tg], Brmix[tg], Bprm], writes=[Bh[c][tg]])

        ropek = {"i": 0}

        def rope_stages(wv, sb, tg, tabref, do_load, sets, dst_ap, dst_bufs, perm_d):
            k = ropek["i"] % 2
            ropek["i"] += 1
            qbf, Bqbf, t1, Bt1, t2, Bt2 = sets[k]
            tok = slice(tg * 512, tg * 512 + 512)
            pq, Bq = (PS[0], BPS[0]) if k == 0 else (PS[3], BPS[3])
            pr, Br = (PS[1], BPS[1]) if k == 0 else (PS[4], BPS[4])

            def st1():
                if do_load:
                    tabref["tb"], tabref["Btb"] = load_tab(tg)
                for c in range(8):
                    mm(pq[:], wv[:, c, :], hT[:, c, tok], c == 0, c == 7, [sb, Bh[c][tg]], [Bq])
                P.op(ACT, lambda e: e.activation(out=qbf, in_=pq[:], func=AF.Copy), reads=[Bq], writes=[Bqbf])

            def st2():
                tabs, Btab = tabref["tb"], tabref["Btb"]
                mm(pr[:], RMT, qbf, True, True, [Bqbf, Bcm], [Br])
                P.op(DVE, lambda e: e.tensor_tensor(t1, pq[:], tabs[:, 0, :], ALU.mult), reads=[Bq, Btab], writes=[Bt1])
                P.op(DVE, lambda e: e.tensor_tensor(t2, pr[:], tabs[:, 1, :], ALU.mult), reads=[Br, Btab], writes=[Bt2])
                if perm_d == 1:
                    P.op(DVE, lambda e: e.tensor_tensor(dst_ap, t1, t2, ALU.add), reads=[Bt1, Bt2], writes=dst_bufs)
                else:
                    a_ = t1.rearrange("p (i r) -> p r i", r=perm_d)
                    b_ = t2.rearrange("p (i r) -> p r i", r=perm_d)
                    P.op(DVE, lambda e: e.tensor_tensor(dst_ap, a_, b_, ALU.add), reads=[Bt1, Bt2], writes=dst_bufs)
            return [st1, st2]

        o = 43008
        tabs = [carve(o + i * 4096, [2, 512], F32) for i in range(2)]; o += 8192
        Btab = [Buf("tab0"), Buf("tab1")]
        qg = carve(o, [SEQ], BF16); o += SEQ * 2
        kg = carve(o, [SEQ], BF16); o += SEQ * 2
        vg = carve(o, [16, 128], BF16); o += 16 * 128 * 2
        Bqg, Bkg, Bvg = Buf("qg"), Buf("kg"), Buf("vg")
        Oacc = carve(o, [SEQ], F32); o += SEQ * 4
        Dacc = carve(o, [SEQ], F32); o += SEQ * 4
        BOacc, BDacc = Buf("Oacc"), Buf("Dacc")
        qbf = carve(o, [512], BF16); o += 1024
        Bqbf = Buf("qbf")
        t1 = carve(o, [512], F32); o += 2048
        t2 = carve(o, [512], F32); o += 2048
        Bt1, Bt2 = Buf("t1"), Buf("t2")
        PT = [carve(o + i * 1024, [512], BF16) for i in range(3)]; o += 3072
        BPT = [Buf("pt0"), Buf("pt1"), Buf("pt2")]
        qbf2 = carve(o, [512], BF16); o += 1024
        t1b = carve(o, [512], F32); o += 2048
        t2b = carve(o, [512], F32); o += 2048
        Bqbf2, Bt1b, Bt2b = Buf("qbf2"), Buf("t1b"), Buf("t2b")
        setsA = [(qbf, Bqbf, t1, Bt1, t2, Bt2), (qbf2, Bqbf2, t1b, Bt1b, t2b, Bt2b)]
        tabk = {"i": 0}
        ptk = {"i": 0}
        localA = Btab + [Bqg, Bkg, Bvg, BOacc, BDacc, Bqbf, Bt1, Bt2, Bqbf2, Bt1b, Bt2b] + BPT
        phase_begin(localA)

        def load_tab(tg):
            k = tabk["i"] % 2
            tabk["i"] += 1
            P.dma(SP, tabs[k], rope[tg], writes=[Btab[k]])
            return tabs[k], Btab[k]

        for jp in range(2):
            for g in range(3):
                d = DIL[g]
                L = SEQ // d
                nb = L // 128
                slot, sb = wload(l, "wa", (jp * 3 + g) * 128 * 3072, 3072)
                wv = slot[:, 0:3072].rearrange("p (q c n) -> p q c n", q=3, c=8)
                allh = lambda c: [Bh[c][t] for t in range(4)]
                rits = []
                for tg in range(4):
                    tref = {}
                    for which, (dstT, Bdst) in enumerate(((qg, Bqg), (kg, Bkg))):
                        if d == 1:
                            dst = dstT[:, tg * 512: tg * 512 + 512]
                        else:
                            i0 = tg * 512 // d
                            dst = dstT.rearrange("p (r i) -> p r i", r=d)[:, :, i0: i0 + 512 // d]
                        rits.append(rope_stages(wv[:, which], sb, tg, tref, which == 0, setsA, dst, [Bdst], d))
                run_pipeline(rits, 1)
                for b0 in range(0, 16, 4):
                    pv, Bpv = PS[2], BPS[2]
                    for bi in range(4):
                        blk = b0 + bi
                        r, kb = blk // nb, blk % nb
                        for c in range(8):
                            if d == 1:
                                lhs = hT[:, c, kb * 128: kb * 128 + 128]
                            else:
                                lhs = hT[:, c, :].rearrange("p (i r) -> p r i", r=d)[:, r, kb * 128: kb * 128 + 128]
                            mm(pv[:, bi * 128: bi * 128 + 128], lhs, wv[:, 2, c, :], c == 0, c == 7,
                               [sb] + allh(c), [Bpv])
                    P.op(ACT, lambda e, b0=b0, pv=pv: e.activation(
                        out=vg[:, b0:b0 + 4, :], in_=pv[:].rearrange("p (a n) -> p a n", a=4), func=AF.Copy),
                        reads=[Bpv], writes=[Bvg])
                units = []
                for r in range(d):
                    for q0 in range(0, nb, 4):
                        units.append((r, q0, min(nb, q0 + 4)))
                per_tile = max(1, 512 // (128 * min(nb, 4)))
                iters = []
                for ti, u0 in enumerate(range(0, len(units), per_tile)):
                    tile_units = units[u0:u0 + per_tile]
                    po, Bpo = (PS[3], BPS[3]) if ti % 2 == 0 else (PS[0], BPS[0])
                    pdn, Bpdn = (PS[4], BPS[4]) if ti % 2 == 0 else (PS[1], BPS[1])
                    colbase = 0
                    tile_iters = []
                    for (r, q0, q1) in tile_units:
                        nq = (q1 - q0) * 128
                        for hh in range(2):
                            hr = slice(hh * 64, hh * 64 + 64)
                            kbs = list(range(max(q0 - 1, 0), q1))
                            for ki, kb in enumerate(kbs):
                                qlo = max(kb, q0)
                                qhi = min(kb + 1, q1 - 1)
                                n = (qhi - qlo + 1) * 128
                                off = colbase + (qlo - q0) * 128
                                bi3 = ptk["i"] % 3
                                ptk["i"] += 1
                                pss, Bpss = PS[5 + bi3], BPS[5 + bi3]
                                pt, Bpt = PT[bi3], BPT[bi3]
                                first = (ki == 0)
                                last = (ki == len(kbs) - 1)
                                blk = r * nb + kb

                                def st1(pss=pss, Bpss=Bpss, pt=pt, Bpt=Bpt, hr=hr, r=r, kb=kb, qlo=qlo, qhi=qhi, n=n):
                                    mm(pss[:, 0:n], kg[hr, r * L + kb * 128: r * L + kb * 128 + 128],
                                       qg[hr, r * L + qlo * 128: r * L + qlo * 128 + n], True, True, [Bkg, Bqg], [Bpss])
                                    P.op(ACT, lambda e: e.activation(out=pt[:, 0:n], in_=pss[:, 0:n], func=AF.Exp, scale=0.125),
                                         reads=[Bpss], writes=[Bpt])
                                    if qlo == kb:
                                        P.op(DVE, lambda e: e.tensor_tensor(pt[:, 0:128], pt[:, 0:128], MASK[:, 0:128], ALU.mult),
                                             reads=[Bpt, Bcm], writes=[Bpt])
                                    if qhi == kb + 1:
                                        P.op(DVE, lambda e: e.tensor_tensor(pt[:, n - 128:n], pt[:, n - 128:n], MASK[:, 128:256], ALU.mult),
                                             reads=[Bpt, Bcm], writes=[Bpt])

                                def st2(pt=pt, Bpt=Bpt, hr=hr, hh=hh, off=off, n=n, first=first, last=last, blk=blk,
                                        po=po, Bpo=Bpo, pdn=pdn, Bpdn=Bpdn):
                                    mm(po[hr, off:off + n], vg[:, blk, hh * 64: hh * 64 + 64], pt[:, 0:n], first, last,
                                       [Bvg, Bpt], [Bpo])
                                    mm(pdn[hr, off:off + n], ONES[:, 0:64], pt[:, 0:n], first, last, [Bcm, Bpt], [Bpdn])
                                tile_iters.append([st1, st2])
                        colbase += nq
                    ncols = colbase
                    (r0, q00, q01) = tile_units[0]
                    if d == 1:
                        dsto = Oacc[:, q00 * 128: q00 * 128 + ncols]
                        dstd = Dacc[:, q00 * 128: q00 * 128 + ncols]
                        so, sd = po[:, 0:ncols], pdn[:, 0:ncols]
                    else:
                        nu = len(tile_units)
                        nq = ncols // nu
                        if nu == 1:
                            dsto = Oacc.rearrange("p (i r) -> p r i", r=d)[:, r0, q00 * 128: q00 * 128 + nq]
                            dstd = Dacc.rearrange("p (i r) -> p r i", r=d)[:, r0, q00 * 128: q00 * 128 + nq]
                            so, sd = po[:, 0:ncols], pdn[:, 0:ncols]
                        else:
                            dsto = Oacc.rearrange("p (i r) -> p r i", r=d)[:, r0:r0 + nu, q00 * 128: q00 * 128 + nq]
                            dstd = Dacc.rearrange("p (i r) -> p r i", r=d)[:, r0:r0 + nu, q00 * 128: q00 * 128 + nq]
                            so = po[:, 0:ncols].rearrange("p (u n) -> p u n", u=nu)
                            sd = pdn[:, 0:ncols].rearrange("p (u n) -> p u n", u=nu)

                    def evac(g=g, dsto=dsto, dstd=dstd, so=so, sd=sd, Bpo=Bpo, Bpdn=Bpdn):
                        if g == 0:
                            P.op(ACT, lambda e: e.activation(out=dsto, in_=so, func=AF.Copy), reads=[Bpo], writes=[BOacc])
                            P.op(DVE, lambda e: e.tensor_copy(dstd, sd), reads=[Bpdn], writes=[BDacc])
                        else:
                            P.op(DVE, lambda e: e.tensor_tensor(dsto, so, dsto, ALU.add), reads=[Bpo, BOacc], writes=[BOacc])
                            P.op(DVE, lambda e: e.tensor_tensor(dstd, sd, dstd, ALU.add), reads=[Bpdn, BDacc], writes=[BDacc])
                    last2 = tile_iters[-1][1]
                    tile_iters[-1][1] = (lambda f=last2, ev=evac: (f(), ev()))
                    iters.extend(tile_iters)
                run_pipeline(iters, 2)
            for tg in range(4):
                tok = slice(tg * 512, tg * 512 + 512)
                P.op(ACT, lambda e, tok=tok: e.activation(out=Dacc[:, tok], in_=Dacc[:, tok], func=AF.Ln), reads=[BDacc], writes=[BDacc])
                P.op(ACT, lambda e, tok=tok: e.activation(out=Dacc[:, tok], in_=Dacc[:, tok], func=AF.Exp, scale=-1.0), reads=[BDacc], writes=[BDacc])
                P.op(DVE, lambda e, tok=tok, jp=jp: e.tensor_tensor(oa[:, jp, tok], Oacc[:, tok], Dacc[:, tok], ALU.mult),
                     reads=[BOacc, BDacc], writes=[Boa[jp][tg]])

        phase_end(localA)
        o = 59392
        zT = carve(o, [4, 32 + SEQ], BF16); o += 4 * (32 + SEQ) * 2
        Bz = [Buf("z%d" % c) for c in range(4)]
        dg = carve(o, [31, 128], BF16); o += 31 * 128 * 2
        Bdg = Buf("dg")
        BdgA, BdgB = Buf("dgA"), Buf("dgB")
        y32 = carve(o, [4, 512], F32); o += 4 * 512 * 4
        By32 = [Buf("y32_%d" % c) for c in range(4)]
        ybf = [carve(o + i * 1024, [512], BF16) for i in range(2)]; o += 2048
        Bybf = [Buf("ybf0"), Buf("ybf1")]
        sgB = [carve(o + i * 2048, [512], F32) for i in range(2)]; o += 4096
        BsgB = [Buf("sg0"), Buf("sg1")]
        mean_sb = carve(o, [512], F32); o += 2048
        Bmean = Buf("mean")
        var_sb = carve(o, [512], F32); o += 2048
        Bvar = Buf("var")
        localB = Bz + [Bdg, BdgA, BdgB] + By32 + Bybf + BsgB + [Bmean, Bvar]
        phase_begin(localB)
        for cc in range(4):
            P.op(DVE, lambda e, cc=cc: e.memset(zT[:, cc, 0:32], 0.0), writes=[Bz[cc]])
        kk = {"i": 0}
        for cc in range(4):
            slot, sb = wload(l, "wb", cc * 128 * 2048, 2048)
            wv = slot[:, 0:2048].rearrange("p (q c n) -> p q c n", q=2, c=8)
            for tg in range(4):
                tok = slice(tg * 512, tg * 512 + 512)
                pa, Bpa = PS[0], BPS[0]
                pg_, Bpg = PS[1], BPS[1]
                for c in range(8):
                    mm(pa[:], wv[:, 0, c, :], hT[:, c, tok], c == 0, c == 7, [sb, Bh[c][tg]], [Bpa])
                for c in range(8):
                    mm(pg_[:], wv[:, 1, c, :], hT[:, c, tok], c == 0, c == 7, [sb, Bh[c][tg]], [Bpg])
                k = kk["i"] % 2
                kk["i"] += 1
                P.op(ACT, lambda e, k=k, pg_=pg_: e.activation(out=sgB[k], in_=pg_[:], func=AF.Sigmoid),
                     reads=[Bpg], writes=[BsgB[k]])
                P.op(DVE, lambda e, k=k, pa=pa, cc=cc, tg=tg: e.tensor_tensor(
                    zT[:, cc, 32 + tg * 512: 32 + tg * 512 + 512], pa[:], sgB[k], ALU.mult),
                    reads=[Bpa, BsgB[k]], writes=[Bz[cc]])
        for tg in range(4):
            tok = slice(tg * 512, tg * 512 + 512)
            pm, Bpm = PS[2], BPS[2]
            pq2, Bpq2 = PS[3], BPS[3]
            for cc in range(4):
                for j in range(16):
                    P.op(DVE, lambda e, j=j, cc=cc: e.tensor_scalar(
                        dg[:, j, :], identf[:], pcol(l, PC_CW + cc * 31 + j), None, ALU.mult),
                        reads=[Bidf, Bprm], writes=[BdgA])
                for j in range(16, 31):
                    P.op(ACT, lambda e, j=j, cc=cc: e.activation(
                        out=dg[:, j, :], in_=identf[:], func=AF.Copy, scale=pcol(l, PC_CW + cc * 31 + j)),
                        reads=[Bidf, Bprm], writes=[BdgB])
                py, Bpy = PS[4 + (cc % 2)], BPS[4 + (cc % 2)]
                for j in range(31):
                    s0 = 2 + tg * 512 + j
                    mm(py[:], dg[:, j, :], zT[:, cc, s0:s0 + 512], j == 0, j == 30,
                       [BdgA if j < 16 else BdgB, Bz[cc]], [Bpy])
                P.op(ACT, lambda e, cc=cc, py=py: e.activation(
                    out=y32[:, cc, :], in_=py[:], func=AF.Identity, bias=pcol(l, PC_CB + cc)),
                    reads=[Bpy, Bprm], writes=[By32[cc]])
                k = kk["i"] % 2
                kk["i"] += 1
                P.op(ACT, lambda e, cc=cc, k=k: e.activation(out=ybf[k], in_=y32[:, cc, :], func=AF.Copy),
                     reads=[By32[cc]], writes=[Bybf[k]])
                mm(pm[:], AVG512, ybf[k], cc == 0, cc == 3, [Bybf[k], Bcm], [Bpm])
                stats_accum(sqb, Bsq, y32[:, cc, :], [By32[cc]], pq2, Bpq2, AVG512, cc == 0, cc == 3)
            P.op(ACT, lambda e, pm=pm: e.activation(out=mean_sb, in_=pm[:], func=AF.Copy), reads=[Bpm], writes=[Bmean])
            P.op(DVE, lambda e: e.tensor_tensor(var_sb, mean_sb, mean_sb, ALU.mult), reads=[Bmean], writes=[Bvar])
            P.op(DVE, lambda e, pq2=pq2: e.tensor_tensor(var_sb, pq2[:], var_sb, ALU.subtract),
                 reads=[Bpq2, Bvar], writes=[Bvar])
            P.op(ACT, lambda e: e.activation(out=var_sb, in_=var_sb, func=AF.Ln, bias=epsc[float(EPS)]),
                 reads=[Bvar, Beps], writes=[Bvar])
            P.op(ACT, lambda e: e.activation(out=var_sb, in_=var_sb, func=AF.Exp, scale=-0.5),
                 reads=[Bvar], writes=[Bvar])
            for cc in range(4):
                k = kk["i"] % 2
                kk["i"] += 1
                P.op(DVE, lambda e, cc=cc, k=k: e.tensor_tensor(sgB[k], y32[:, cc, :], mean_sb, ALU.subtract),
                     reads=[By32[cc], Bmean], writes=[BsgB[k]])
                P.op(DVE, lambda e, k=k: e.tensor_tensor(sgB[k], sgB[k], var_sb, ALU.mult),
                     reads=[BsgB[k], Bvar], writes=[BsgB[k]])
                P.op(ACT, lambda e, cc=cc, k=k, tok=tok: e.activation(
                    out=ob[:, cc, tok], in_=sgB[k], func=AF.Silu, bias=pcol(l, PC_CNB + cc), scale=pcol(l, PC_CG + cc)),
                    reads=[BsgB[k], Bprm], writes=[Bob[cc][tg]])

        if dbg and _os.environ.get("KDBGZ"):
            P.dma(SP, dbg_out[6:10].rearrange("c p t -> p c t"), zT[:, :, 32:32 + SEQ], reads=Bz, writes=[Bdbg], semb=Bdbg)
        phase_end(localB)
        o = 75776
        tabs = [carve(o + i * 4096, [2, 512], F32) for i in range(2)]; o += 8192
        qh = carve(o, [SEQ], BF16); o += SEQ * 2
        kh = carve(o, [SEQ], BF16); o += SEQ * 2
        vh = carve(o, [16, 128], BF16); o += 16 * 128 * 2
        Bqh = [Buf("qh%d" % t) for t in range(4)]
        Bkh = [Buf("kh%d" % t) for t in range(4)]
        Bvh = Buf("vh")
        qbf = carve(o, [512], BF16); o += 1024
        t1 = carve(o, [512], F32); o += 2048
        t2 = carve(o, [512], F32); o += 2048
        PT = [carve(o + i * 1024, [512], BF16) for i in range(3)]; o += 3072
        d0 = carve(o, [512], F32); o += 2048
        d1 = carve(o, [512], F32); o += 2048
        Bd0, Bd1 = Buf("d0"), Buf("d1")
        lam_scale = 1.0 / (1.0 - float(lam_inits[l]))
        localC = Btab + Bqh + Bkh + [Bvh, Bqbf, Bt1, Bt2, Bd0, Bd1] + BPT
        phase_begin(localC)
        for h in range(4):
            slot, sb = wload(l, "wc", h * 128 * 3072, 3072)
            wv = slot[:, 0:3072].rearrange("p (q c n) -> p q c n", q=3, c=8)
            rits = []
            setsC = [(qbf, Bqbf, t1, Bt1, t2, Bt2), (PT[2], BPT[2], d0, Bd0, d1, Bd1)]
            for tg in range(4):
                tok = slice(tg * 512, tg * 512 + 512)
                tref = {}
                rits.append(rope_stages(wv[:, 0], sb, tg, tref, True, setsC, qh[:, tok], [Bqh[tg]], 1))
                rits.append(rope_stages(wv[:, 1], sb, tg, tref, False, setsC, kh[:, tok], [Bkh[tg]], 1))
            run_pipeline(rits, 1)
            for b0 in range(0, 16, 4):
                pv, Bpv = PS[2], BPS[2]
                for bi in range(4):
                    kb = b0 + bi
                    for c in range(8):
                        mm(pv[:, bi * 128: bi * 128 + 128], hT[:, c, kb * 128: kb * 128 + 128], wv[:, 2, c, :],
                           c == 0, c == 7, [sb, Bh[c][kb // 4]], [Bpv])
                P.op(ACT, lambda e, b0=b0, pv=pv: e.activation(
                    out=vh[:, b0:b0 + 4, :], in_=pv[:].rearrange("p (a n) -> p a n", a=4), func=AF.Copy),
                    reads=[Bpv], writes=[Bvh])
            iters = []
            pending = []
            for Q in range(4):
                tok = slice(Q * 512, Q * 512 + 512)
                for m in range(2):
                    hr = slice(m * 64, m * 64 + 64)
                    po, Bpo = PS[3 + m], BPS[3 + m]
                    pdn, Bpdn = PS[0 + m], BPS[0 + m]
                    nkb = 4 * Q + 4
                    for kb in range(nkb):
                        off = 0 if kb < 4 * Q else (kb - 4 * Q) * 128
                        n = 512 - off
                        bi3 = ptk["i"] % 3
                        ptk["i"] += 1
                        pss, Bpss = PS[5 + bi3], BPS[5 + bi3]
                        pt, Bpt = PT[bi3], BPT[bi3]

                        def st1(pss=pss, Bpss=Bpss, pt=pt, Bpt=Bpt, hr=hr, kb=kb, Q=Q, off=off, n=n):
                            mm(pss[:, 0:n], kh[hr, kb * 128: kb * 128 + 128], qh[hr, Q * 512 + off: Q * 512 + 512],
                               True, True, [Bkh[kb // 4], Bqh[Q]], [Bpss])
                            P.op(ACT, lambda e: e.activation(out=pt[:, 0:n], in_=pss[:, 0:n], func=AF.Exp, scale=0.125),
                                 reads=[Bpss], writes=[Bpt])
                            if kb >= 4 * Q:
                                P.op(DVE, lambda e: e.tensor_tensor(pt[:, 0:128], pt[:, 0:128], MASK[:, 0:128], ALU.mult),
                                     reads=[Bpt, Bcm], writes=[Bpt])

                        def st2(pt=pt, Bpt=Bpt, kb=kb, nkb=nkb, off=off, n=n, po=po, Bpo=Bpo, pdn=pdn, Bpdn=Bpdn):
                            mm(po[:, off:512], vh[:, kb, :], pt[:, 0:n], kb == 0, kb == nkb - 1, [Bvh, Bpt], [Bpo])
                            mm(pdn[:, off:512], ONES, pt[:, 0:n], kb == 0, kb == nkb - 1, [Bcm, Bpt], [Bpdn])
                        iters.append([st1, st2])

                def combine(h=h, tok=tok, Q=Q):
                    P.op(ACT, lambda e: e.activation(out=d0, in_=PS[0][:], func=AF.Ln), reads=[BPS[0]], writes=[Bd0])
                    P.op(DVE, lambda e: e.tensor_copy(t1, PS[3][:]), reads=[BPS[3]], writes=[Bt1])
                    P.op(ACT, lambda e: e.activation(out=d1, in_=PS[1][:], func=AF.Ln), reads=[BPS[1]], writes=[Bd1])
                    P.op(DVE, lambda e: e.tensor_copy(t2, PS[4][:]), reads=[BPS[4]], writes=[Bt2])
                    P.op(ACT, lambda e: e.activation(out=d0, in_=d0, func=AF.Exp, scale=-1.0), reads=[Bd0], writes=[Bd0])
                    P.op(ACT, lambda e: e.activation(out=d1, in_=d1, func=AF.Exp, scale=-1.0), reads=[Bd1], writes=[Bd1])
                    P.op(DVE, lambda e: e.tensor_tensor(t1, t1, d0, ALU.mult), reads=[Bt1, Bd0], writes=[Bt1])
                    P.op(DVE, lambda e: e.tensor_tensor(t2, t2, d1, ALU.mult), reads=[Bt2, Bd1], writes=[Bt2])
                    P.op(DVE, lambda e: e.scalar_tensor_tensor(
                        out=t1, in0=t2, scalar=lam_sb[:, n_layers + l: n_layers + l + 1], in1=t1,
                        op0=ALU.mult, op1=ALU.add), reads=[Bt1, Bt2, Blam], writes=[Bt1])

                def combine2(h=h, tok=tok, Q=Q):
                    stats_accum(sqb, Bsq, t1, [Bt1], PS[2], BPS[2], AVG128, True, True, scale=lam_scale)
                    rstd_from(PS[2], BPS[2], d0, [Bd0], eps=float(EPS / (1.0 - float(lam_inits[l])) ** 2), lnexp=True)
                    P.op(DVE, lambda e: e.scalar_tensor_tensor(
                        out=oc[:, h, tok], in0=t1, scalar=pcol(l, PC_SUB), in1=d0, op0=ALU.mult, op1=ALU.mult),
                        reads=[Bt1, Bd0, Bprm], writes=[Boc[h][Q]])
                last2 = iters[-1][1]
                iters[-1][1] = (lambda f=last2, cb=combine: (f(), cb()))
                pending.append((len(iters) - 1, combine2))
            for (idx, cb2) in pending:
                j = min(idx + 7, len(iters) - 1)
                f0 = iters[j][1]
                iters[j][1] = (lambda f=f0, cb=cb2: (f(), cb()))
            run_pipeline(iters, 2)

        if dbg:
            P.dma(SP, dbg_out[0:2].rearrange("c p t -> p c t"), oa, reads=flat(Boa), writes=[Bdbg], semb=Bdbg)
            P.dma(SP, dbg_out[2:6].rearrange("c p t -> p c t"), ob, reads=flat(Bob), writes=[Bdbg], semb=Bdbg)
            if not _os.environ.get("KDBGZ"):
                P.dma(SP, dbg_out[6:10].rearrange("c p t -> p c t"), oc, reads=flat(Boc), writes=[Bdbg], semb=Bdbg)
        phase_end(localC + flat(Bh))
        htg = carve(0, [8, 2, 512], BF16)
        Bhtg = [[Buf("htg%d_%d" % (c, t)) for t in range(2)] for c in range(8)]
        mg = carve(16384, [8, 2, 512], BF16)
        Bmg = [[Buf("mg%d_%d" % (c, t)) for t in range(2)] for c in range(8)]
        o = 75776
        sg = [carve(o + i * 2048, [512], F32) for i in range(2)]; o += 4096
        Bsg = [Buf("gsg0"), Buf("gsg1")]
        macc = [carve(o + i * 2048, [512], F32) for i in range(2)]; o += 4096
        Bmacc = [Buf("macc0"), Buf("macc1")]
        tt = [carve(o + i * 2048, [512], F32) for i in range(2)]; o += 4096
        Btt = [Buf("tt0"), Buf("tt1")]
        yT = carve(o, [8, 512], F32); o += 16384
        By = [Buf("my%d" % f) for f in range(8)]
        localG = flat(Bhtg) + flat(Bmg) + By + Bsg + Bmacc + Btt
        phase_begin(localG)
        obr = (oa, ob, oc)
        Bobr = (Boa, Bob, Boc)
        nkc = (2, 4, 4)
        poff = (0, 2, 6)
        for tp in range(2):
            for ti in range(2):
                tg = tp * 2 + ti
                tok = slice(tg * 512, tg * 512 + 512)
                for c in range(8):
                    P.op(DVE, lambda e, c=c, tok=tok, ti=ti: e.scalar_tensor_tensor(
                        out=htg[:, c, ti, :], in0=xT[:, c, tok], scalar=pcol(l, PC_NORM + 16 + c),
                        in1=rstd_mix[:, tok], op0=ALU.mult, op1=ALU.mult),
                        reads=[Bx[c][tg], Brmix[tg], Bprm], writes=[Bhtg[c][ti]])
            for f in range(8):
                slotg, sbg = wload(l, "gg", f * 128 * 3072, 3072)
                gv = slotg[:, 0:3072].rearrange("p (b c n) -> p b c n", b=3, c=8)
                slotp, sbp = wload(l, "gp", f * 128 * 1280, 1280)
                pv_ = slotp[:, 0:1280].rearrange("p (c n) -> p c n", c=10)
                for ti in range(2):
                    tg = tp * 2 + ti
                    tok = slice(tg * 512, tg * 512 + 512)
                    for b in range(3):
                        pgt, Bpgt = PS[(2 * b) % 6], BPS[(2 * b) % 6]
                        pyb, Bpyb = PS[(2 * b + 1) % 6], BPS[(2 * b + 1) % 6]
                        for c in range(8):
                            mm(pgt[:], gv[:, b, c, :], htg[:, c, ti, :], c == 0, c == 7, [sbg, Bhtg[c][ti]], [Bpgt])
                        for c in range(nkc[b]):
                            mm(pyb[:], pv_[:, poff[b] + c, :], obr[b][:, c, tok], c == 0, c == nkc[b] - 1,
                               [sbp, Bobr[b][c][tg]], [Bpyb])
                        k = kk["i"] % 2
                        kk["i"] += 1
                        P.op(ACT, lambda e, k=k, pgt=pgt, b=b, f=f: e.activation(
                            out=sg[k], in_=pgt[:], func=AF.Sigmoid, bias=pcol(l, PC_BG + b * 8 + f)),
                            reads=[Bpgt, Bprm], writes=[Bsg[k]])
                        if b == 0:
                            P.op(DVE, lambda e, k=k, pyb=pyb, ti=ti: e.tensor_tensor(macc[ti], sg[k], pyb[:], ALU.mult),
                                 reads=[Bsg[k], Bpyb], writes=[Bmacc[ti]])
                        else:
                            P.op(DVE, lambda e, k=k, pyb=pyb: e.tensor_tensor(tt[k], sg[k], pyb[:], ALU.mult),
                                 reads=[Bsg[k], Bpyb], writes=[Btt[k]])
                            if b == 1:
                                P.op(DVE, lambda e, k=k, ti=ti: e.tensor_tensor(macc[ti], macc[ti], tt[k], ALU.add),
                                     reads=[Bmacc[ti], Btt[k]], writes=[Bmacc[ti]])
                            else:
                                P.op(DVE, lambda e, k=k, f=f, ti=ti: e.tensor_tensor(mg[:, f, ti, :], macc[ti], tt[k], ALU.add),
                                     reads=[Bmacc[ti], Btt[k]], writes=[Bmg[f][ti]])
            for ti in range(2):
                tg = tp * 2 + ti
                tok = slice(tg * 512, tg * 512 + 512)
                for f0, nf in ((0, 3), (3, 3), (6, 2)):
                    slot, sb = wload(l, "wo", f0 * 128 * 1024, nf * 1024, npanel=nf)
                    wv = slot[:, 0:nf * 1024].rearrange("p (f c n) -> p f c n", f=nf, c=8)
                    for fi in range(nf):
                        f = f0 + fi
                        pw, Bpw = PS[f % 2], BPS[f % 2]
                        for c in range(8):
                            mm(pw[:], wv[:, fi, c, :], mg[:, c, ti, :], c == 0, c == 7, [sb, Bmg[c][ti]], [Bpw])
                        P.op(DVE, lambda e, f=f, pw=pw: e.tensor_copy(yT[:, f, :], pw[:]), reads=[Bpw], writes=[By[f]])
                        stats_accum(sqb, Bsq, pw[:], [Bpw], PS[6], BPS[6], AVG1024, f == 0, f == 7)
                r = sqk["r"] % 2
                sqk["r"] += 1
                rstd_from(PS[6], BPS[6], rstd_t[r][:], [Brstd[r]])
                for f in range(8):
                    k = kk["i"] % 2
                    kk["i"] += 1
                    P.op(DVE, lambda e, f=f, r=r, k=k: e.scalar_tensor_tensor(
                        out=tt[k], in0=yT[:, f, :], scalar=pcol(l, PC_NORM + 24 + f), in1=rstd_t[r][:],
                        op0=ALU.mult, op1=ALU.mult), reads=[By[f], Brstd[r], Bprm], writes=[Btt[k]])
                    P.op(DVE, lambda e, f=f, k=k, tok=tok: e.tensor_tensor(xT[:, f, tok], xT[:, f, tok], tt[k], ALU.add),
                         reads=[Btt[k], Bx[f][tg]], writes=[Bx[f][tg]])
        phase_end(localG + flat(Boa) + flat(Bob) + flat(Boc) + Bsq)

    Bout = Buf("yout")
    flay = ffn_layout()
    fbufs = []
    for item in flay[1::2]:
        for b in item:
            fbufs.extend(b if isinstance(b, list) else [b])
    for s in range(n_seq):
        if s > 0:
            P.dma(POOL, xT[:], xin[s].rearrange("c p t -> p c t"), writes=allx, semb=allx[0])
        for l in range(n_layers):
            if "ffn1" in stages:
                phase_begin(fbufs)
                ffn(l, 0, flay)
                phase_end(fbufs)
            if "mixer" in stages:
                mixer(l)
            if "ffn2" in stages:
                phase_begin(fbufs)
                ffn(l, 1, flay)
                phase_end(fbufs)
        P.force = True
        P.dma(POOL, yout[s].rearrange("c p t -> p c t"), xT[:], reads=allx, writes=[Bout], semb=Bout)
        P.force = False
    P.emit(final_wait_bufs=[Bout] + ([Bdbg] if dbg else []))
    return nc, P


def _panels(W, col0, ncols):
    K = W.shape[0]
    sub = W[:, col0:col0 + ncols].reshape(K // 128, 128, ncols // 128, 128)
    return np.ascontiguousarray(sub.transpose(2, 1, 0, 3))


def pack_layer_weights(inp, l):
    out = np.empty(WL_SZ, np.float32)

    def put(name, arr):
        o, s = OFF[name]
        assert arr.size == s, (name, arr.size, s)
        out[o:o + s] = arr.reshape(-1)

    for idx, pre in ((1, "ffn1"), (2, "ffn2")):
        g = _panels(inp[pre + "_w_gate"][l], 0, DFF)
        u = _panels(inp[pre + "_w_up"][l], 0, DFF)
        put("gu%d" % idx, np.stack([g, u], axis=2))
        put("d%d" % idx, _panels(inp[pre + "_w_down"][l], 0, D))
    win = inp["w_in"][l]
    wp = _panels(win, 0, 7936)
    wa = np.empty((6, 128, 3, 8, 128), np.float32)
    for jp in range(2):
        for g in range(3):
            for q in range(3):
                wa[jp * 3 + g, :, q] = wp[q * 6 + g * 2 + jp]
    put("wa", wa)
    wb = np.empty((4, 128, 2, 8, 128), np.float32)
    for cc in range(4):
        wb[cc, :, 0] = wp[30 + cc]
        wb[cc, :, 1] = wp[34 + cc]
    put("wb", wb)
    wc = np.empty((4, 128, 3, 8, 128), np.float32)
    for h in range(4):
        for q in range(3):
            wc[h, :, q] = wp[18 + q * 4 + h]
    put("wc", wc)
    gg = np.empty((8, 128, 3, 8, 128), np.float32)
    for f in range(8):
        for b in range(3):
            gg[f, :, b] = wp[38 + b * 8 + f]
    put("gg", gg)
    pa = _panels(inp["w_proj_a"][l], 0, D)
    pb = _panels(inp["w_proj_b"][l], 0, D)
    pc = _panels(inp["w_proj_c"][l], 0, D)
    put("gp", np.concatenate([pa, pb, pc], axis=2))
    put("wo", _panels(inp["w_out"][l], 0, D))
    return out.reshape(128, WL_SZ // 128)


def pack_params(inp, layers):
    nl = len(layers)
    prm = np.zeros((128, nl * PC_N), np.float32)
    lam = np.zeros((128, nl * LAMC), np.float32)
    for i, l in enumerate(layers):
        b = i * PC_N
        for k, name in enumerate(("ffn1_norm_pre", "ffn1_norm_post", "mix_norm_pre", "mix_norm_post",
                                  "ffn2_norm_pre", "ffn2_norm_post")):
            prm[:, b + PC_NORM + 8 * k: b + PC_NORM + 8 * k + 8] = inp[name][l].reshape(8, 128).T
        prm[:, b + PC_BG: b + PC_BG + 24] = inp["b_gate"][l].reshape(24, 128).T
        cw = inp["conv_w"][l]
        for cc in range(4):
            prm[:, b + PC_CW + cc * 31: b + PC_CW + cc * 31 + 31] = cw[:, cc * 128:(cc + 1) * 128].T
        prm[:, b + PC_CB: b + PC_CB + 4] = inp["conv_b"][l].reshape(4, 128).T
        prm[:, b + PC_CG: b + PC_CG + 4] = inp["conv_norm_g"][l].reshape(4, 128).T
        prm[:, b + PC_CNB: b + PC_CNB + 4] = inp["conv_norm_b"][l].reshape(4, 128).T
        prm[:, b + PC_SUB] = inp["diff_subln"][l]
        for k, name in enumerate(("lambda_q1", "lambda_k1", "lambda_q2", "lambda_k2")):
            lam[:, i * LAMC + 64 * k: i * LAMC + 64 * k + 64] = inp[name][l][None, :]
    return prm, lam


def const_tables():
    inv = (10000.0 ** (-np.arange(0, 64, 2, dtype=np.float32) / 64)).astype(np.float32)
    ang = np.arange(SEQ, dtype=np.float32)[None, :] * inv[:, None]
    cos = np.cos(ang).astype(np.float32)
    sin = np.sin(ang).astype(np.float32)
    C = np.tile(cos, (4, 1))
    S_ = np.tile(sin, (4, 1))
    rope = np.stack([C.reshape(128, 4, 512), S_.reshape(128, 4, 512)], axis=2)
    rope = np.ascontiguousarray(rope.transpose(1, 0, 2, 3))
    cm = np.zeros((128, 7 * 128 + 256), np.float32)
    cm[:, 0:128] = 1.0 / 1024
    cm[:, 128:256] = 1.0 / 512
    cm[:, 256:384] = 1.0 / 128
    cm[:, 384:512] = 1.0
    R = np.zeros((128, 128), np.float32)
    for blk in (0, 64):
        for dd in range(32):
            R[blk + dd + 32, blk + dd] = -1.0
            R[blk + dd, blk + dd + 32] = 1.0
    cm[:, 512:640] = R
    cm[:, 640:768] = np.eye(128, dtype=np.float32)
    i = np.arange(128)[:, None]
    j = np.arange(128)[None, :]
    cm[:, 896:1024] = (j >= i).astype(np.float32)
    cm[:, 1024:1152] = (j <= i).astype(np.float32)
    return rope, cm


_CACHE = {}


def _get_prog(n_layers, n_seq, lam_inits):
    key = (n_layers, n_seq, tuple(lam_inits))
    if key not in _CACHE:
        _CACHE[key] = build_program(n_layers, n_seq, lam_inits)[0]
    return _CACHE[key]


MODE = "split"


def kernel(**inputs):
    inp = {k: np.asarray(v) for k, v in inputs.items()}
    x = inp["x"]
    B = x.shape[0]
    xT = np.ascontiguousarray(x.reshape(NCORES, SEQ_PER_CORE, SEQ, 8, 128).transpose(0, 1, 3, 4, 2))
    layers = list(range(DEPTH))
    rope, cm = const_tables()
    if MODE == "fused":
        wts = np.stack([pack_layer_weights(inp, l) for l in layers], axis=0)
        prm, lam = pack_params(inp, layers)
        nc = _get_prog(DEPTH, SEQ_PER_CORE, [_lam_init(l) for l in layers])
        in_maps = [{"xin": xT[c], "wts": wts, "prm": prm, "lamrows": lam, "rope": rope, "cmat": cm}
                   for c in range(NCORES)]
        res = run_bass_kernel_spmd(nc, in_maps, core_ids=list(range(NCORES)))
        yT = np.stack([res.results[c]["yout"] for c in range(NCORES)], axis=0)
    else:
        yT = xT
        for l in layers:
            wts = pack_layer_weights(inp, l)[None]
            prm, lam = pack_params(inp, [l])
            nc = _get_prog(1, 1, [_lam_init(l)])
            nxt = np.empty_like(yT)
            for h in range(SEQ_PER_CORE):
                in_maps = [{"xin": np.ascontiguousarray(yT[c, h:h + 1]), "wts": wts, "prm": prm,
                            "lamrows": lam, "rope": rope, "cmat": cm} for c in range(NCORES)]
                res = run_bass_kernel_spmd(nc, in_maps, core_ids=list(range(NCORES)))
                for c in range(NCORES):
                    nxt[c, h:h + 1] = res.results[c]["yout"]
            yT = nxt
    y = yT.transpose(0, 1, 4, 2, 3).reshape(B, SEQ, D)
    return np.ascontiguousarray(y.astype(np.float32))
```

```python
import contextlib
import numpy as np
import concourse.bass as bass
import concourse.mybir as mybir
from concourse.bass_utils import run_bass_kernel_spmd

F32 = mybir.dt.float32
BF16 = mybir.dt.bfloat16
AF = mybir.ActivationFunctionType
ALU = mybir.AluOpType

PE, ACT, DVE, POOL, SP = "pe", "act", "dve", "pool", "sp"

D = 1024
SEQ = 2048
DFF = 2816
NFF = 22
NCORES = 8
SEQ_PER_CORE = 4
DEPTH = 4
EPS = 1e-6
DIL = (1, 4, 16)

GU_SZ = NFF * 128 * 2 * 8 * 128
D_SZ = 8 * 128 * NFF * 128
WA_SZ = 6 * 128 * 3 * 8 * 128
WB_SZ = 4 * 128 * 2 * 8 * 128
WC_SZ = 4 * 128 * 3 * 8 * 128
GG_SZ = 8 * 128 * 3 * 8 * 128
GP_SZ = 8 * 128 * 10 * 128
WO_SZ = 8 * 128 * 8 * 128
OFF = {}
_o = 0
for _n, _s in (("gu1", GU_SZ), ("d1", D_SZ), ("wa", WA_SZ), ("wb", WB_SZ), ("wc", WC_SZ),
               ("gg", GG_SZ), ("gp", GP_SZ), ("wo", WO_SZ), ("gu2", GU_SZ), ("d2", D_SZ)):
    OFF[_n] = (_o, _s)
    _o += _s
WL_SZ = _o
assert WL_SZ % 128 == 0

PC_NORM = 0
PC_BG = 48
PC_CW = 72
PC_CB = 196
PC_CG = 200
PC_CNB = 204
PC_SUB = 208
PC_N = 212
LAMC = 4 * 64


def _lam_init(layer):
    return 0.8 - 0.6 * float(np.exp(-0.3 * layer))


class Buf:
    __slots__ = ("name", "lw", "rd", "sem", "cnt", "excl")

    def __init__(self, name, excl=False):
        self.name = name
        self.excl = excl
        self.lw = None
        self.rd = {}
        self.sem = None
        self.cnt = 0


class Prog:
    def __init__(self, nc):
        self.nc = nc
        self.ops = []
        self.stack = contextlib.ExitStack()
        self.n_dma_sems = 0
        self.maxops = None
        self.force = False

    def sbuf(self, name, shape, dt):
        return self.stack.enter_context(self.nc.sbuf_tensor(name, shape, dt))

    def psum(self, name, shape, dt):
        return self.stack.enter_context(self.nc.psum_tensor(name, shape, dt))

    def _deps(self, eng, reads, writes, is_dma, nodep=False):
        deps = set()
        idx = len(self.ops)
        key = ("dma", idx) if is_dma else eng
        xreads = [b for b in reads if b.excl]
        for b in reads:
            if b.lw is not None:
                deps.add(b.lw)
        if not nodep:
            for b in list(writes) + xreads:
                if b.lw is not None:
                    deps.add(b.lw)
                deps.update(b.rd.values())
        for b in reads:
            if not b.excl:
                b.rd[key] = idx
        for b in list(writes) + xreads:
            b.lw = idx
            b.rd = {}
        deps.discard(idx)
        return deps

    def op(self, eng, fn, reads=(), writes=()):
        if self.maxops is not None and len(self.ops) >= self.maxops and not self.force:
            return
        deps = self._deps(eng, reads, writes, False)
        self.ops.append([eng, fn, deps, False, None, 0, False, 0, frozenset(reads), frozenset(writes)])

    def dma(self, queue, out_ap, in_ap, reads=(), writes=(), semb=None, nodep=False):
        if self.maxops is not None and len(self.ops) >= self.maxops and not self.force:
            return
        if semb is None:
            semb = writes[0] if writes else reads[0]
        if semb.sem is None:
            semb.sem = self.n_dma_sems
            self.n_dma_sems += 1
        deps = self._deps(queue, reads, writes, True, nodep)
        semb.cnt += 16
        fn = lambda e, o=out_ap, i=in_ap: e.dma_start(out=o, in_=i)
        self.ops.append([queue, fn, deps, True, semb, semb.cnt, False, 0, frozenset(reads), frozenset(writes)])

    def emit(self, final_wait_bufs=()):
        nc = self.nc
        ops = self.ops
        for o in ops:
            for d in o[2]:
                if not ops[d][3]:
                    ops[d][6] = True
        counts = {PE: 0, ACT: 0, DVE: 0, POOL: 0, SP: 0}
        for o in ops:
            if o[6]:
                counts[o[0]] += 1
                o[7] = counts[o[0]]
        st = self.stack
        esem = {e: st.enter_context(nc.semaphore("s_" + e)) for e in (PE, ACT, DVE, POOL, SP)}
        dsem = [st.enter_context(nc.semaphore("d%d" % i)) for i in range(self.n_dma_sems)]
        per_eng = {PE: [], ACT: [], DVE: [], POOL: [], SP: []}
        for i, o in enumerate(ops):
            per_eng[o[0]].append(i)
        self.n_waits = 0

        def run(eng_name, e):
            known = {}
            for i in per_eng[eng_name]:
                o = ops[i]
                need = {}
                for d in o[2]:
                    od = ops[d]
                    if od[3]:
                        key = ("d", od[4].sem)
                        val = od[5]
                    else:
                        if od[0] == eng_name:
                            if eng_name == PE:
                                continue
                            if not (od[9] & o[8]):
                                continue
                        key = ("e", od[0])
                        val = od[7]
                    if known.get(key, 0) >= val:
                        continue
                    if need.get(key, 0) < val:
                        need[key] = val
                for key, val in need.items():
                    sem = dsem[key[1]] if key[0] == "d" else esem[key[1]]
                    e.wait_ge(sem, val)
                    known[key] = val
                    self.n_waits += 1
                ins = o[1](e)
                if o[3]:
                    ins.then_inc(dsem[o[4].sem], 16)
                elif o[6]:
                    ins.then_inc(esem[eng_name], 1)
            if eng_name == SP:
                for b in final_wait_bufs:
                    e.wait_ge(dsem[b.sem], b.cnt)

        with nc.Block() as block:
            @block.tensor
            def _(e):
                run(PE, e)

            @block.scalar
            def _(e):
                run(ACT, e)

            @block.vector
            def _(e):
                run(DVE, e)

            @block.gpsimd
            def _(e):
                run(POOL, e)

            @block.sync
            def _(e):
                run(SP, e)
        st.close()


def build_program(n_layers, n_seq, lam_inits, stages=("ffn1", "mixer", "ffn2"), dbg=False):
    nc = bass.Bass("TRN2", target_bir_lowering=False)
    P = Prog(nc)
    import os as _os
    if _os.environ.get("KCUT"):
        P.maxops = int(_os.environ["KCUT"])
    xin = nc.dram_tensor("xin", [n_seq, 8, 128, SEQ], F32, kind="ExternalInput").ap()
    wts = nc.dram_tensor("wts", [n_layers, 128, WL_SZ // 128], F32, kind="ExternalInput").ap()
    prm = nc.dram_tensor("prm", [128, n_layers * PC_N], F32, kind="ExternalInput").ap()
    lamrows = nc.dram_tensor("lamrows", [128, n_layers * LAMC], F32, kind="ExternalInput").ap()
    rope = nc.dram_tensor("rope", [4, 128, 2, 512], F32, kind="ExternalInput").ap()
    cmat = nc.dram_tensor("cmat", [128, 7 * 128 + 256], F32, kind="ExternalInput").ap()
    yout = nc.dram_tensor("yout", [n_seq, 8, 128, SEQ], F32, kind="ExternalOutput").ap()
    if dbg:
        dbg_out = nc.dram_tensor("dbg_out", [10, 128, SEQ], BF16, kind="ExternalOutput").ap()
        Bdbg = Buf("dbg")
    wsc = nc.dram_tensor("wsc", [n_layers, 128, WL_SZ // 128], BF16, kind="Internal").ap()
    wflat = [wsc[l].rearrange("p x -> (p x)") for l in range(n_layers)]

    xT = P.sbuf("xT", [128, 8, SEQ], F32)
    Bx = [[Buf("x%d_%d" % (c, t)) for t in range(4)] for c in range(8)]
    NSLOT, SLOT = 3, 3072
    ring = [P.sbuf("ring%d" % i, [128, SLOT], BF16) for i in range(NSLOT)]
    Bring = [Buf("ring%d" % i) for i in range(NSLOT)]
    prm_sb = P.sbuf("prm_sb", [128, n_layers * PC_N], F32)
    Bprm = Buf("prm")
    cm = P.sbuf("cm", [128, 7 * 128 + 256], BF16)
    Bcm = Buf("cm")
    identf = P.sbuf("identf", [128, 128], F32)
    Bidf = Buf("identf")
    lam_sb = P.sbuf("lam_sb", [128, 2 * n_layers], F32)
    Blam = Buf("lam")
    rstd_t = [P.sbuf("rstd%d" % i, [128, 512], F32) for i in range(2)]
    Brstd = [Buf("rstd%d" % i) for i in range(2)]
    rstd_mix = P.sbuf("rstd_mix", [128, SEQ], F32)
    Brmix = [Buf("rmix%d" % t) for t in range(4)]
    rstd_pre = [rstd_mix[:, 0:512], rstd_mix[:, 512:1024]]
    Brpre = [Brmix[0], Brmix[1]]
    SCR = 106 * 1024
    scr = P.sbuf("scr", [128, SCR // 2], BF16)

    def carve(off_bytes, shape, dt):
        n = int(np.prod(shape))
        esz = 4 if dt == F32 else 2
        assert off_bytes % 4 == 0 and off_bytes + n * esz <= SCR, (off_bytes, n * esz, SCR)
        ap = scr[:, off_bytes // 2: off_bytes // 2 + n * esz // 2]
        if dt == F32:
            ap = ap.bitcast(F32)
        if len(shape) == 2:
            ap = ap.rearrange("p (a b) -> p a b", a=shape[0])
        elif len(shape) == 3:
            ap = ap.rearrange("p (a b c) -> p a b c", a=shape[0], b=shape[1])
        return ap

    dummy = P.sbuf("dummy_bar", [128, 2], F32)
    phase = {"bar": None}

    def phase_begin(bufs):
        for b in bufs:
            b.lw = phase["bar"]
            b.rd = {}

    def phase_end(bufs):
        bufs = list(bufs)
        P.op(DVE, lambda e: e.memset(dummy[:, 0:1], 0.0), reads=bufs, writes=bufs)
        phase["bar"] = len(P.ops) - 1

    eps_vals = [float(EPS)] + [float(EPS / (1.0 - float(li)) ** 2) for li in lam_inits]
    eps_sb = P.sbuf("eps_sb", [128, len(eps_vals)], F32)
    Beps = Buf("eps")
    epsc = {}
    for i, v in enumerate(eps_vals):
        epsc[v] = eps_sb[:, i:i + 1]
        P.op(DVE, lambda e, i=i, v=v: e.memset(eps_sb[:, i:i + 1], v), writes=[Beps])
    PS = [P.psum("ps%d" % i, [128, 512], F32) for i in range(8)]
    BPS = [Buf("ps%d" % i, excl=True) for i in range(8)]

    AVG1024 = cm[:, 0:128]
    AVG512 = cm[:, 128:256]
    AVG128 = cm[:, 256:384]
    ONES = cm[:, 384:512]
    RMT = cm[:, 512:640]
    MASK = cm[:, 896:1152]

    allx = [Bx[c][t] for c in range(8) for t in range(4)]
    P.dma(POOL, xT[:], xin[0].rearrange("c p t -> p c t"), writes=allx, semb=allx[0])
    P.dma(POOL, cm[:], cmat[:, :], writes=[Bcm])
    P.dma(SP, identf[:], cmat[:, 640:768], writes=[Bidf])
    P.dma(SP, prm_sb[:], prm[:, :], writes=[Bprm])
    lr = carve(0, [n_layers * LAMC], F32)
    Blr = Buf("lr")
    P.dma(SP, lr, lamrows[:, :], writes=[Blr])
    lt = carve(n_layers * LAMC * 4, [n_layers * 128 + 4 * n_layers], F32)
    Blt = Buf("lt")
    for l in range(n_layers):
        b0 = l * LAMC
        prod = lt[:, l * 128: l * 128 + 128]
        sums = lt[:, n_layers * 128 + 4 * l: n_layers * 128 + 4 * l + 4]
        P.op(DVE, lambda e, o=prod[:, 0:64], a=lr[:, b0:b0 + 64], b=lr[:, b0 + 64:b0 + 128]:
             e.tensor_tensor(o, a, b, ALU.mult), reads=[Blr], writes=[Blt])
        P.op(DVE, lambda e, o=prod[:, 64:128], a=lr[:, b0 + 128:b0 + 192], b=lr[:, b0 + 192:b0 + 256]:
             e.tensor_tensor(o, a, b, ALU.mult), reads=[Blr], writes=[Blt])
        P.op(DVE, lambda e, o=sums[:, 0:1], i=prod[:, 0:64]:
             e.reduce_sum(o, i, mybir.AxisListType.X), reads=[Blt], writes=[Blt])
        P.op(DVE, lambda e, o=sums[:, 1:2], i=prod[:, 64:128]:
             e.reduce_sum(o, i, mybir.AxisListType.X), reads=[Blt], writes=[Blt])
        P.op(ACT, lambda e, o=sums[:, 2:4], i=sums[:, 0:2]:
             e.activation(out=o, in_=i, func=AF.Exp), reads=[Blt], writes=[Blt])
        P.op(DVE, lambda e, o=lam_sb[:, l:l + 1], a=sums[:, 2:3], b=sums[:, 3:4]:
             e.tensor_tensor(o, a, b, ALU.subtract), reads=[Blt], writes=[Blam])
        P.op(DVE, lambda e, o=lam_sb[:, l:l + 1], li=float(lam_inits[l]):
             e.tensor_scalar(o, o, li, None, ALU.add), reads=[Blam], writes=[Blam])
        P.op(DVE, lambda e, o=lam_sb[:, n_layers + l:n_layers + l + 1], i=lam_sb[:, l:l + 1]:
             e.tensor_scalar(o, i, -1.0, None, ALU.mult), reads=[Blam], writes=[Blam])

    phase_end([Blr, Blt])
    Bw = {}
    WCOLS = WL_SZ // 128
    for l in range(n_layers):
        for name, (o, s) in OFF.items():
            Bw[(l, name)] = Buf("w%d%s" % (l, name))
    wflat32 = [wts[l].rearrange("p x -> (p x)") for l in range(n_layers)]
    for l in range(n_layers):
        for name, (o, s) in OFF.items():
            cols = s // 128
            src = wflat32[l][o:o + s].rearrange("(p x) -> p x", p=128)
            dst = wflat[l][o:o + s].rearrange("(p x) -> p x", p=128)
            step = 8192
            for c0 in range(0, cols, step):
                c1 = min(cols, c0 + step)
                P.dma(POOL, dst[:, c0:c1], src[:, c0:c1], writes=[Bw[(l, name)]], semb=Bw[(l, name)], nodep=True)

    wstate = {"i": 0}

    def wload(l, name, elem_off, per_part, npanel=1):
        i = wstate["i"]
        wstate["i"] += 1
        slot = ring[i % NSLOT]
        sb = Bring[i % NSLOT]
        o, s = OFF[name]
        assert per_part <= SLOT and elem_off + 128 * per_part <= s
        if npanel == 1:
            src = wflat[l][o + elem_off: o + elem_off + 128 * per_part].rearrange("(p x) -> p x", p=128)
            dst = slot[:, 0:per_part]
        else:
            src = wflat[l][o + elem_off: o + elem_off + 128 * per_part].rearrange(
                "(f p x) -> p f x", f=npanel, p=128)
            dst = slot[:, 0:per_part].rearrange("p (f x) -> p f x", f=npanel)
        P.dma(SP, dst, src, reads=[Bw[(l, name)]], writes=[sb], semb=sb)
        return slot, sb

    sqk = {"i": 0, "r": 0, "t": 0}

    def pcol(l, col):
        return prm_sb[:, l * PC_N + col: l * PC_N + col + 1]

    def mm(out, lhsT, rhs, start, stop, reads, writes):
        P.op(PE, lambda e: e.matmul(out, lhsT, rhs, start=start, stop=stop, skip_group_check=True),
             reads=reads, writes=writes)

    def stats_accum(sqbufs, Bsq, src_ap, src_bufs, ps_stat, Bps_stat, avg, first, last, scale=1.0, N=512):
        k = sqk["i"] % 2
        sqk["i"] += 1
        sq = sqbufs[k]
        P.op(ACT, lambda e: e.activation(out=sq[:, 0:N], in_=src_ap, func=AF.Square, scale=scale),
             reads=src_bufs, writes=[Bsq[k]])
        mm(ps_stat[:, 0:N], avg, sq[:, 0:N], first, last, [Bsq[k], Bcm], [Bps_stat])

    def rstd_from(ps_stat, Bps_stat, out_ap, out_bufs, eps=EPS, N=512, lnexp=True):
        if lnexp:
            P.op(ACT, lambda e: e.activation(out=out_ap, in_=ps_stat[:, 0:N], func=AF.Ln, bias=epsc[float(eps)]),
                 reads=[Bps_stat, Beps], writes=out_bufs)
            P.op(ACT, lambda e: e.activation(out=out_ap, in_=out_ap, func=AF.Exp, scale=-0.5),
                 reads=out_bufs, writes=out_bufs)
            return
        P.op(ACT, lambda e: e.activation(out=out_ap, in_=ps_stat[:, 0:N], func=AF.Sqrt, bias=epsc[float(eps)]),
             reads=[Bps_stat, Beps], writes=out_bufs)
        P.op(DVE, lambda e: e.reciprocal(out_ap, out_ap), reads=out_bufs, writes=out_bufs)

    def ffn(l, which, lay):
        gname, dname = ("gu1", "d1") if which == 0 else ("gu2", "d2")
        ncol_pre = PC_NORM + (0 if which == 0 else 32)
        ncol_post = ncol_pre + 8
        hT, Bh, actT, Bact, yT, By, sqb, Bsq, tmpA, BtA, tmpB, BtB = lay

        def prenorm(half, ps_i=7):
            for tgi in range(2):
                tg = half * 2 + tgi
                tok = slice(tg * 512, tg * 512 + 512)
                r = sqk["r"] % 2
                sqk["r"] += 1
                for c in range(8):
                    stats_accum(sqb, Bsq, xT[:, c, tok], [Bx[c][tg]], PS[ps_i], BPS[ps_i], AVG1024, c == 0, c == 7)
                rstd_from(PS[ps_i], BPS[ps_i], rstd_pre[tgi][:], [Brpre[tgi]])
                for c in range(8):
                    P.op(DVE, lambda e, c=c, tok=tok, r=tgi, tgi=tgi: e.scalar_tensor_tensor(
                        out=hT[:, c, tgi * 512: tgi * 512 + 512], in0=xT[:, c, tok], scalar=pcol(l, ncol_pre + c),
                        in1=rstd_pre[tgi][:], op0=ALU.mult, op1=ALU.mult),
                        reads=[Bx[c][tg], Brpre[tgi], Bprm], writes=[Bh[c][tgi]])

        def gateup(half):
            for j in range(NFF):
                slot, sb = wload(l, gname, j * 128 * 2048, 2048)
                sv = slot[:, 0:2048].rearrange("p (g c n) -> p g c n", g=2, c=8)
                for tgi in range(2):
                    tg = half * 2 + tgi
                    tok = slice(tg * 512, tg * 512 + 512)
                    pg = (2 * j + tgi) % 2
                    psg, psu = PS[pg * 2], PS[pg * 2 + 1]
                    Bg, Bu = BPS[pg * 2], BPS[pg * 2 + 1]
                    for gu, (pp, Bp) in enumerate(((psg, Bg), (psu, Bu))):
                        for c in range(8):
                            mm(pp[:], sv[:, gu, c, :], hT[:, c, tgi * 512: tgi * 512 + 512], c == 0, c == 7,
                               [sb, Bh[c][tgi]], [Bp])
                    k = sqk["t"] % 2
                    sqk["t"] += 1
                    P.op(ACT, lambda e, k=k, psg=psg: e.activation(out=tmpA[k], in_=psg[:], func=AF.Silu),
                         reads=[Bg], writes=[BtA[k]])
                    P.op(DVE, lambda e, k=k, psu=psu, j=j, tgi=tgi: e.tensor_tensor(
                        actT[:, j, tgi * 512: tgi * 512 + 512], tmpA[k], psu[:], ALU.mult),
                        reads=[BtA[k], Bu], writes=[Bact[j][tgi]])

        def down(half):
            for f in range(8):
                slot, sb = wload(l, dname, f * 128 * NFF * 128, NFF * 128)
                sv = slot[:, 0:NFF * 128].rearrange("p (c n) -> p c n", c=NFF)
                for tgi in range(2):
                    pd = PS[4 + tgi]
                    Bpd = BPS[4 + tgi]
                    for cc in range(NFF):
                        mm(pd[:], sv[:, cc, :], actT[:, cc, tgi * 512: tgi * 512 + 512], cc == 0, cc == NFF - 1,
                           [sb, Bact[cc][tgi]], [Bpd])
                    P.op(DVE, lambda e, f=f, pd=pd, tgi=tgi: e.tensor_copy(yT[:, f, tgi, :], pd[:]),
                         reads=[Bpd], writes=[By[f][tgi]])
                    stats_accum(sqb, Bsq, pd[:], [Bpd], PS[6 + tgi], BPS[6 + tgi], AVG1024, f == 0, f == 7)

        def down_resid(half):
            for tgi in range(2):
                tg = half * 2 + tgi
                tok = slice(tg * 512, tg * 512 + 512)
                r = sqk["r"] % 2
                sqk["r"] += 1
                rstd_from(PS[6 + tgi], BPS[6 + tgi], rstd_t[r][:], [Brstd[r]])
                for f in range(8):
                    k = sqk["t"] % 2
                    sqk["t"] += 1
                    P.op(DVE, lambda e, f=f, r=r, k=k, tgi=tgi: e.scalar_tensor_tensor(
                        out=tmpB[k], in0=yT[:, f, tgi, :], scalar=pcol(l, ncol_post + f), in1=rstd_t[r][:],
                        op0=ALU.mult, op1=ALU.mult), reads=[By[f][tgi], Brstd[r], Bprm], writes=[BtB[k]])
                    P.op(DVE, lambda e, f=f, k=k, tok=tok: e.scalar_tensor_tensor(
                        out=xT[:, f, tok], in0=tmpB[k], scalar=0.5, in1=xT[:, f, tok],
                        op0=ALU.mult, op1=ALU.add), reads=[BtB[k], Bx[f][tg]], writes=[Bx[f][tg]])

        prenorm(0)
        gateup(0)
        down(0)
        prenorm(1, ps_i=3)
        down_resid(0)
        gateup(1)
        down(1)
        down_resid(1)

    def ffn_layout():
        o = 0
        hT = carve(o, [8, 1024], BF16); o += 8 * 1024 * 2
        actT = carve(o, [NFF, 1024], BF16); o += NFF * 1024 * 2
        yT = carve(o, [8, 2, 512], F32); o += 8 * 1024 * 4
        sqb = [carve(o + i * 1024, [512], BF16) for i in range(2)]; o += 2048
        tmpA = [carve(o + i * 2048, [512], F32) for i in range(2)]; o += 4096
        tmpB = [carve(o + i * 2048, [512], F32) for i in range(2)]; o += 4096
        return (hT, [[Buf("fh%d_%d" % (c, t)) for t in range(2)] for c in range(8)],
                actT, [[Buf("act%d_%d" % (j, t)) for t in range(2)] for j in range(NFF)],
                yT, [[Buf("y%d_%d" % (f, t)) for t in range(2)] for f in range(8)],
                sqb, [Buf("sq0"), Buf("sq1")], tmpA, [Buf("tA0"), Buf("tA1")], tmpB, [Buf("tB0"), Buf("tB1")])

    def run_pipeline(iters, skew=2):
        n = len(iters)
        for i in range(n + skew):
            if i < n:
                iters[i][0]()
            if i >= skew:
                iters[i - skew][1]()

    def mixer(l):
        hT = carve(0, [8, SEQ], BF16)
        Bh = [[Buf("mh%d_%d" % (c, t)) for t in range(4)] for c in range(8)]
        oa = carve(32768, [2, SEQ], BF16)
        Boa = [[Buf("oa%d_%d" % (c, t)) for t in range(4)] for c in range(2)]
        sqb = [carve(40960 + i * 1024, [512], BF16) for i in range(2)]
        Bsq = [Buf("msq0"), Buf("msq1")]
        ob = carve(43008, [4, SEQ], BF16)
        Bob = [[Buf("ob%d_%d" % (c, t)) for t in range(4)] for c in range(4)]
        oc = carve(59392, [4, SEQ], BF16)
        Boc = [[Buf("oc%d_%d" % (c, t)) for t in range(4)] for c in range(4)]
        flat = lambda ll: [b for row in ll for b in row]
        wide = flat(Bh) + flat(Boa) + flat(Bob) + flat(Boc) + Bsq
        phase_begin(wide)

        for tg in range(4):
            tok = slice(tg * 512, tg * 512 + 512)
            for c in range(8):
                stats_accum(sqb, Bsq, xT[:, c, tok], [Bx[c][tg]], PS[7], BPS[7], AVG1024, c == 0, c == 7)
            rstd_from(PS[7], BPS[7], rstd_mix[:, tok], [Brmix[tg]])
            for c in range(8):
                P.op(DVE, lambda e, c=c, tok=tok: e.scalar_tensor_tensor(
                    out=hT[:, c, tok], in0=xT[:, c, tok], scalar=pcol(l, PC_NORM + 16 + c),
                    in1=rstd_mix[:, tok], op0=ALU.mult, op1=ALU.mult),
                    reads=[Bx[c][tg], Brmix[tg], Bprm], writes=[Bh[c][tg]])

        ropek = {"i": 0}

        def rope_stages(wv, sb, tg, tabref, do_load, sets, dst_ap, dst_bufs, perm_d):
            k = ropek["i"] % 2
            ropek["i"] += 1
            qbf, Bqbf, t1, Bt1, t2, Bt2 = sets[k]
            tok = slice(tg * 512, tg * 512 + 512)
            pq, Bq = (PS[0], BPS[0]) if k == 0 else (PS[3], BPS[3])
            pr, Br = (PS[1], BPS[1]) if k == 0 else (PS[4], BPS[4])

            def st1():
                if do_load:
                    tabref["tb"], tabref["Btb"] = load_tab(tg)
                for c in range(8):
                    mm(pq[:], wv[:, c, :], hT[:, c, tok], c == 0, c == 7, [sb, Bh[c][tg]], [Bq])
                P.op(ACT, lambda e: e.activation(out=qbf, in_=pq[:], func=AF.Copy), reads=[Bq], writes=[Bqbf])

            def st2():
                tabs, Btab = tabref["tb"], tabref["Btb"]
                mm(pr[:], RMT, qbf, True, True, [Bqbf, Bcm], [Br])
                P.op(DVE, lambda e: e.tensor_tensor(t1, pq[:], tabs[:, 0, :], ALU.mult), reads=[Bq, Btab], writes=[Bt1])
                P.op(DVE, lambda e: e.tensor_tensor(t2, pr[:], tabs[:, 1, :], ALU.mult), reads=[Br, Btab], writes=[Bt2])
                if perm_d == 1:
                    P.op(DVE, lambda e: e.tensor_tensor(dst_ap, t1, t2, ALU.add), reads=[Bt1, Bt2], writes=dst_bufs)
                else:
                    a_ = t1.rearrange("p (i r) -> p r i", r=perm_d)
                    b_ = t2.rearrange("p (i r) -> p r i", r=perm_d)
                    P.op(DVE, lambda e: e.tensor_tensor(dst_ap, a_, b_, ALU.add), reads=[Bt1, Bt2], writes=dst_bufs)
            return [st1, st2]

        o = 43008
        tabs = [carve(o + i * 4096, [2, 512], F32) for i in range(2)]; o += 8192
        Btab = [Buf("tab0"), Buf("tab1")]
        qg = carve(o, [SEQ], BF16); o += SEQ * 2
        kg = carve(o, [SEQ], BF16); o += SEQ * 2
        vg = carve(o, [16, 128], BF16); o += 16 * 128 * 2
        Bqg, Bkg, Bvg = Buf("qg"), Buf("kg"), Buf("vg")
        Oacc = carve(o, [SEQ], F32); o += SEQ * 4
        Dacc = carve(o, [SEQ], F32); o += SEQ * 4
        BOacc, BDacc = Buf("Oacc"), Buf("Dacc")
        qbf = carve(o, [512], BF16); o += 1024
        Bqbf = Buf("qbf")
        t1 = carve(o, [512], F32); o += 2048
        t2 = carve(o, [512], F32); o += 2048
        Bt1, Bt2 = Buf("t1"), Buf("t2")
        PT = [carve(o + i * 1024, [512], BF16) for i in range(3)]; o += 3072
        BPT = [Buf("pt0"), Buf("pt1"), Buf("pt2")]
        qbf2 = carve(o, [512], BF16); o += 1024
        t1b = carve(o, [512], F32); o += 2048
        t2b = carve(o, [512], F32); o += 2048
        Bqbf2, Bt1b, Bt2b = Buf("qbf2"), Buf("t1b"), Buf("t2b")
        setsA = [(qbf, Bqbf, t1, Bt1, t2, Bt2), (qbf2, Bqbf2, t1b, Bt1b, t2b, Bt2b)]
        tabk = {"i": 0}
        ptk = {"i": 0}
        localA = Btab + [Bqg, Bkg, Bvg, BOacc, BDacc, Bqbf, Bt1, Bt2, Bqbf2, Bt1b, Bt2b] + BPT
        phase_begin(localA)

        def load_tab(tg):
            k = tabk["i"] % 2
            tabk["i"] += 1
            P.dma(SP, tabs[k], rope[tg], writes=[Btab[k]])
            return tabs[k], Btab[k]

        for jp in range(2):
            for g in range(3):
                d = DIL[g]
                L = SEQ // d
                nb = L // 128
                slot, sb = wload(l, "wa", (jp * 3 + g) * 128 * 3072, 3072)
                wv = slot[:, 0:3072].rearrange("p (q c n) -> p q c n", q=3, c=8)
                allh = lambda c: [Bh[c][t] for t in range(4)]
                rits = []
                for tg in range(4):
                    tref = {}
                    for which, (dstT, Bdst) in enumerate(((qg, Bqg), (kg, Bkg))):
                        if d == 1:
                            dst = dstT[:, tg * 512: tg * 512 + 512]
                        else:
                            i0 = tg * 512 // d
                            dst = dstT.rearrange("p (r i) -> p r i", r=d)[:, :, i0: i0 + 512 // d]
                        rits.append(rope_stages(wv[:, which], sb, tg, tref, which == 0, setsA, dst, [Bdst], d))
                run_pipeline(rits, 1)
                for b0 in range(0, 16, 4):
                    pv, Bpv = PS[2], BPS[2]
                    for bi in range(4):
                        blk = b0 + bi
                        r, kb = blk // nb, blk % nb
                        for c in range(8):
                            if d == 1:
                                lhs = hT[:, c, kb * 128: kb * 128 + 128]
                            else:
                                lhs = hT[:, c, :].rearrange("p (i r) -> p r i", r=d)[:, r, kb * 128: kb * 128 + 128]
                            mm(pv[:, bi * 128: bi * 128 + 128], lhs, wv[:, 2, c, :], c == 0, c == 7,
                               [sb] + allh(c), [Bpv])
                    P.op(ACT, lambda e, b0=b0, pv=pv: e.activation(
                        out=vg[:, b0:b0 + 4, :], in_=pv[:].rearrange("p (a n) -> p a n", a=4), func=AF.Copy),
                        reads=[Bpv], writes=[Bvg])
                units = []
                for r in range(d):
                    for q0 in range(0, nb, 4):
                        units.append((r, q0, min(nb, q0 + 4)))
                per_tile = max(1, 512 // (128 * min(nb, 4)))
                iters = []
                for ti, u0 in enumerate(range(0, len(units), per_tile)):
                    tile_units = units[u0:u0 + per_tile]
                    po, Bpo = (PS[3], BPS[3]) if ti % 2 == 0 else (PS[0], BPS[0])
                    pdn, Bpdn = (PS[4], BPS[4]) if ti % 2 == 0 else (PS[1], BPS[1])
                    colbase = 0
                    tile_iters = []
                    for (r, q0, q1) in tile_units:
                        nq = (q1 - q0) * 128
                        for hh in range(2):
                            hr = slice(hh * 64, hh * 64 + 64)
                            kbs = list(range(max(q0 - 1, 0), q1))
                            for ki, kb in enumerate(kbs):
                                qlo = max(kb, q0)
                                qhi = min(kb + 1, q1 - 1)
                                n = (qhi - qlo + 1) * 128
                                off = colbase + (qlo - q0) * 128
                                bi3 = ptk["i"] % 3
                                ptk["i"] += 1
                                pss, Bpss = PS[5 + bi3], BPS[5 + bi3]
                                pt, Bpt = PT[bi3], BPT[bi3]
                                first = (ki == 0)
                                last = (ki == len(kbs) - 1)
                                blk = r * nb + kb

                                def st1(pss=pss, Bpss=Bpss, pt=pt, Bpt=Bpt, hr=hr, r=r, kb=kb, qlo=qlo, qhi=qhi, n=n):
                                    mm(pss[:, 0:n], kg[hr, r * L + kb * 128: r * L + kb * 128 + 128],
                                       qg[hr, r * L + qlo * 128: r * L + qlo * 128 + n], True, True, [Bkg, Bqg], [Bpss])
                                    P.op(ACT, lambda e: e.activation(out=pt[:, 0:n], in_=pss[:, 0:n], func=AF.Exp, scale=0.125),
                                         reads=[Bpss], writes=[Bpt])
                                    if qlo == kb:
                                        P.op(DVE, lambda e: e.tensor_tensor(pt[:, 0:128], pt[:, 0:128], MASK[:, 0:128], ALU.mult),
                                             reads=[Bpt, Bcm], writes=[Bpt])
                                    if qhi == kb + 1:
                                        P.op(DVE, lambda e: e.tensor_tensor(pt[:, n - 128:n], pt[:, n - 128:n], MASK[:, 128:256], ALU.mult),
                                             reads=[Bpt, Bcm], writes=[Bpt])

                                def st2(pt=pt, Bpt=Bpt, hr=hr, hh=hh, off=off, n=n, first=first, last=last, blk=blk,
                                        po=po, Bpo=Bpo, pdn=pdn, Bpdn=Bpdn):
                                    mm(po[hr, off:off + n], vg[:, blk, hh * 64: hh * 64 + 64], pt[:, 0:n], first, last,
                                       [Bvg, Bpt], [Bpo])
                                    mm(pdn[hr, off:off + n], ONES[:, 0:64], pt[:, 0:n], first, last, [Bcm, Bpt], [Bpdn])
                                tile_iters.append([st1, st2])
                        colbase += nq
                    ncols = colbase
                    (r0, q00, q01) = tile_units[0]
                    if d == 1:
                        dsto = Oacc[:, q00 * 128: q00 * 128 + ncols]
                        dstd = Dacc[:, q00 * 128: q00 * 128 + ncols]
                        so, sd = po[:, 0:ncols], pdn[:, 0:ncols]
                    else:
                        nu = len(tile_units)
                        nq = ncols // nu
                        if nu == 1:
                            dsto = Oacc.rearrange("p (i r) -> p r i", r=d)[:, r0, q00 * 128: q00 * 128 + nq]
                            dstd = Dacc.rearrange("p (i r) -> p r i", r=d)[:, r0, q00 * 128: q00 * 128 + nq]
                            so, sd = po[:, 0:ncols], pdn[:, 0:ncols]
                        else:
                            dsto = Oacc.rearrange("p (i r) -> p r i", r=d)[:, r0:r0 + nu, q00 * 128: q00 * 128 + nq]
                            dstd = Dacc.rearrange("p (i r) -> p r i", r=d)[:, r0:r0 + nu, q00 * 128: q00 * 128 + nq]
                            so = po[:, 0:ncols].rearrange("p (u n) -> p u n", u=nu)
                            sd = pdn[:, 0:ncols].rearrange("p (u n) -> p u n", u=nu)

                    def evac(g=g, dsto=dsto, dstd=dstd, so=so, sd=sd, Bpo=Bpo, Bpdn=Bpdn):
                        if g == 0:
                            P.op(ACT, lambda e: e.activation(out=dsto, in_=so, func=AF.Copy), reads=[Bpo], writes=[BOacc])
                            P.op(DVE, lambda e: e.tensor_copy(dstd, sd), reads=[Bpdn], writes=[BDacc])
                        else:
                            P.op(DVE, lambda e: e.tensor_tensor(dsto, so, dsto, ALU.add), reads=[Bpo, BOacc], writes=[BOacc])
                            P.op(DVE, lambda e: e.tensor_tensor(dstd, sd, dstd, ALU.add), reads=[Bpdn, BDacc], writes=[BDacc])
                    last2 = tile_iters[-1][1]
                    tile_iters[-1][1] = (lambda f=last2, ev=evac: (f(), ev()))
                    iters.extend(tile_iters)
                run_pipeline(iters, 2)
            for tg in range(4):
                tok = slice(tg * 512, tg * 512 + 512)
                P.op(ACT, lambda e, tok=tok: e.activation(out=Dacc[:, tok], in_=Dacc[:, tok], func=AF.Ln), reads=[BDacc], writes=[BDacc])
                P.op(ACT, lambda e, tok=tok: e.activation(out=Dacc[:, tok], in_=Dacc[:, tok], func=AF.Exp, scale=-1.0), reads=[BDacc], writes=[BDacc])
                P.op(DVE, lambda e, tok=tok, jp=jp: e.tensor_tensor(oa[:, jp, tok], Oacc[:, tok], Dacc[:, tok], ALU.mult),
                     reads=[BOacc, BDacc], writes=[Boa[jp][tg]])

        phase_end(localA)
        o = 59392
        zT = carve(o, [4, 32 + SEQ], BF16); o += 4 * (32 + SEQ) * 2
        Bz = [Buf("z%d" % c) for c in range(4)]
        dg = carve(o, [31, 128], BF16); o += 31 * 128 * 2
        Bdg = Buf("dg")
        BdgA, BdgB = Buf("dgA"), Buf("dgB")
        y32 = carve(o, [4, 512], F32); o += 4 * 512 * 4
        By32 = [Buf("y32_%d" % c) for c in range(4)]
        ybf = [carve(o + i * 1024, [512], BF16) for i in range(2)]; o += 2048
        Bybf = [Buf("ybf0"), Buf("ybf1")]
        sgB = [carve(o + i * 2048, [512], F32) for i in range(2)]; o += 4096
        BsgB = [Buf("sg0"), Buf("sg1")]
        mean_sb = carve(o, [512], F32); o += 2048
        Bmean = Buf("mean")
        var_sb = carve(o, [512], F32); o += 2048
        Bvar = Buf("var")
        localB = Bz + [Bdg, BdgA, BdgB] + By32 + Bybf + BsgB + [Bmean, Bvar]
        phase_begin(localB)
        for cc in range(4):
            P.op(DVE, lambda e, cc=cc: e.memset(zT[:, cc, 0:32], 0.0), writes=[Bz[cc]])
        kk = {"i": 0}
        for cc in range(4):
            slot, sb = wload(l, "wb", cc * 128 * 2048, 2048)
            wv = slot[:, 0:2048].rearrange("p (q c n) -> p q c n", q=2, c=8)
            for tg in range(4):
                tok = slice(tg * 512, tg * 512 + 512)
                pa, Bpa = PS[0], BPS[0]
                pg_, Bpg = PS[1], BPS[1]
                for c in range(8):
                    mm(pa[:], wv[:, 0, c, :], hT[:, c, tok], c == 0, c == 7, [sb, Bh[c][tg]], [Bpa])
                for c in range(8):
                    mm(pg_[:], wv[:, 1, c, :], hT[:, c, tok], c == 0, c == 7, [sb, Bh[c][tg]], [Bpg])
                k = kk["i"] % 2
                kk["i"] += 1
                P.op(ACT, lambda e, k=k, pg_=pg_: e.activation(out=sgB[k], in_=pg_[:], func=AF.Sigmoid),
                     reads=[Bpg], writes=[BsgB[k]])
                P.op(DVE, lambda e, k=k, pa=pa, cc=cc, tg=tg: e.tensor_tensor(
                    zT[:, cc, 32 + tg * 512: 32 + tg * 512 + 512], pa[:], sgB[k], ALU.mult),
                    reads=[Bpa, BsgB[k]], writes=[Bz[cc]])
        for tg in range(4):
            tok = slice(tg * 512, tg * 512 + 512)
            pm, Bpm = PS[2], BPS[2]
            pq2, Bpq2 = PS[3], BPS[3]
            for cc in range(4):
                for j in range(16):
                    P.op(DVE, lambda e, j=j, cc=cc: e.tensor_scalar(
                        dg[:, j, :], identf[:], pcol(l, PC_CW + cc * 31 + j), None, ALU.mult),
                        reads=[Bidf, Bprm], writes=[BdgA])
                for j in range(16, 31):
                    P.op(ACT, lambda e, j=j, cc=cc: e.activation(
                        out=dg[:, j, :], in_=identf[:], func=AF.Copy, scale=pcol(l, PC_CW + cc * 31 + j)),
                        reads=[Bidf, Bprm], writes=[BdgB])
                py, Bpy = PS[4 + (cc % 2)], BPS[4 + (cc % 2)]
                for j in range(31):
                    s0 = 2 + tg * 512 + j
                    mm(py[:], dg[:, j, :], zT[:, cc, s0:s0 + 512], j == 0, j == 30,
                       [BdgA if j < 16 else BdgB, Bz[cc]], [Bpy])
                P.op(ACT, lambda e, cc=cc, py=py: e.activation(
                    out=y32[:, cc, :], in_=py[:], func=AF.Identity, bias=pcol(l, PC_CB + cc)),
                    reads=[Bpy, Bprm], writes=[By32[cc]])
                k = kk["i"] % 2
                kk["i"] += 1
                P.op(ACT, lambda e, cc=cc, k=k: e.activation(out=ybf[k], in_=y32[:, cc, :], func=AF.Copy),
                     reads=[By32[cc]], writes=[Bybf[k]])
                mm(pm[:], AVG512, ybf[k], cc == 0, cc == 3, [Bybf[k], Bcm], [Bpm])
                stats_accum(sqb, Bsq, y32[:, cc, :], [By32[cc]], pq2, Bpq2, AVG512, cc == 0, cc == 3)
            P.op(ACT, lambda e, pm=pm: e.activation(out=mean_sb, in_=pm[:], func=AF.Copy), reads=[Bpm], writes=[Bmean])
            P.op(DVE, lambda e: e.tensor_tensor(var_sb, mean_sb, mean_sb, ALU.mult), reads=[Bmean], writes=[Bvar])
            P.op(DVE, lambda e, pq2=pq2: e.tensor_tensor(var_sb, pq2[:], var_sb, ALU.subtract),
                 reads=[Bpq2, Bvar], writes=[Bvar])
            P.op(ACT, lambda e: e.activation(out=var_sb, in_=var_sb, func=AF.Ln, bias=epsc[float(EPS)]),
                 reads=[Bvar, Beps], writes=[Bvar])
            P.op(ACT, lambda e: e.activation(out=var_sb, in_=var_sb, func=AF.Exp, scale=-0.5),
                 reads=[Bvar], writes=[Bvar])
            for cc in range(4):
                k = kk["i"] % 2
                kk["i"] += 1
                P.op(DVE, lambda e, cc=cc, k=k: e.tensor_tensor(sgB[k], y32[:, cc, :], mean_sb, ALU.subtract),
                     reads=[By32[cc], Bmean], writes=[BsgB[k]])
                P.op(DVE, lambda e, k=k: e.tensor_tensor(sgB[k], sgB[k], var_sb, ALU.mult),
                     reads=[BsgB[k], Bvar], writes=[BsgB[k]])
                P.op(ACT, lambda e, cc=cc, k=k, tok=tok: e.activation(
                    out=ob[:, cc, tok], in_=sgB[k], func=AF.Silu, bias=pcol(l, PC_CNB + cc), scale=pcol(l, PC_CG + cc)),
                    reads=[BsgB[k], Bprm], writes=[Bob[cc][tg]])

        if dbg and _os.environ.get("KDBGZ"):
            P.dma(SP, dbg_out[6:10].rearrange("c p t -> p c t"), zT[:, :, 32:32 + SEQ], reads=Bz, writes=[Bdbg], semb=Bdbg)
        phase_end(localB)
        o = 75776
        tabs = [carve(o + i * 4096, [2, 512], F32) for i in range(2)]; o += 8192
        qh = carve(o, [SEQ], BF16); o += SEQ * 2
        kh = carve(o, [SEQ], BF16); o += SEQ * 2
        vh = carve(o, [16, 128], BF16); o += 16 * 128 * 2
        Bqh = [Buf("qh%d" % t) for t in range(4)]
        Bkh = [Buf("kh%d" % t) for t in range(4)]
        Bvh = Buf("vh")
        qbf = carve(o, [512], BF16); o += 1024
        t1 = carve(o, [512], F32); o += 2048
        t2 = carve(o, [512], F32); o += 2048
        PT = [carve(o + i * 1024, [512], BF16) for i in range(3)]; o += 3072
        d0 = carve(o, [512], F32); o += 2048
        d1 = carve(o, [512], F32); o += 2048
        Bd0, Bd1 = Buf("d0"), Buf("d1")
        lam_scale = 1.0 / (1.0 - float(lam_inits[l]))
        localC = Btab + Bqh + Bkh + [Bvh, Bqbf, Bt1, Bt2, Bd0, Bd1] + BPT
        phase_begin(localC)
        for h in range(4):
            slot, sb = wload(l, "wc", h * 128 * 3072, 3072)
            wv = slot[:, 0:3072].rearrange("p (q c n) -> p q c n", q=3, c=8)
            rits = []
            setsC = [(qbf, Bqbf, t1, Bt1, t2, Bt2), (PT[2], BPT[2], d0, Bd0, d1, Bd1)]
            for tg in range(4):
                tok = slice(tg * 512, tg * 512 + 512)
                tref = {}
                rits.append(rope_stages(wv[:, 0], sb, tg, tref, True, setsC, qh[:, tok], [Bqh[tg]], 1))
                rits.append(rope_stages(wv[:, 1], sb, tg, tref, False, setsC, kh[:, tok], [Bkh[tg]], 1))
            run_pipeline(rits, 1)
            for b0 in range(0, 16, 4):
                pv, Bpv = PS[2], BPS[2]
                for bi in range(4):
                    kb = b0 + bi
                    for c in range(8):
                        mm(pv[:, bi * 128: bi * 128 + 128], hT[:, c, kb * 128: kb * 128 + 128], wv[:, 2, c, :],
                           c == 0, c == 7, [sb, Bh[c][kb // 4]], [Bpv])
                P.op(ACT, lambda e, b0=b0, pv=pv: e.activation(
                    out=vh[:, b0:b0 + 4, :], in_=pv[:].rearrange("p (a n) -> p a n", a=4), func=AF.Copy),
                    reads=[Bpv], writes=[Bvh])
            iters = []
            pending = []
            for Q in range(4):
                tok = slice(Q * 512, Q * 512 + 512)
                for m in range(2):
                    hr = slice(m * 64, m * 64 + 64)
                    po, Bpo = PS[3 + m], BPS[3 + m]
                    pdn, Bpdn = PS[0 + m], BPS[0 + m]
                    nkb = 4 * Q + 4
                    for kb in range(nkb):
                        off = 0 if kb < 4 * Q else (kb - 4 * Q) * 128
                        n = 512 - off
                        bi3 = ptk["i"] % 3
                        ptk["i"] += 1
                        pss, Bpss = PS[5 + bi3], BPS[5 + bi3]
                        pt, Bpt = PT[bi3], BPT[bi3]

                        def st1(pss=pss, Bpss=Bpss, pt=pt, Bpt=Bpt, hr=hr, kb=kb, Q=Q, off=off, n=n):
                            mm(pss[:, 0:n], kh[hr, kb * 128: kb * 128 + 128], qh[hr, Q * 512 + off: Q * 512 + 512],
                               True, True, [Bkh[kb // 4], Bqh[Q]], [Bpss])
                            P.op(ACT, lambda e: e.activation(out=pt[:, 0:n], in_=pss[:, 0:n], func=AF.Exp, scale=0.125),
                                 reads=[Bpss], writes=[Bpt])
                            if kb >= 4 * Q:
                                P.op(DVE, lambda e: e.tensor_tensor(pt[:, 0:128], pt[:, 0:128], MASK[:, 0:128], ALU.mult),
                                     reads=[Bpt, Bcm], writes=[Bpt])

                        def st2(pt=pt, Bpt=Bpt, kb=kb, nkb=nkb, off=off, n=n, po=po, Bpo=Bpo, pdn=pdn, Bpdn=Bpdn):
                            mm(po[:, off:512], vh[:, kb, :], pt[:, 0:n], kb == 0, kb == nkb - 1, [Bvh, Bpt], [Bpo])
                            mm(pdn[:, off:512], ONES, pt[:, 0:n], kb == 0, kb == nkb - 1, [Bcm, Bpt], [Bpdn])
                        iters.append([st1, st2])

                def combine(h=h, tok=tok, Q=Q):
                    P.op(ACT, lambda e: e.activation(out=d0, in_=PS[0][:], func=AF.Ln), reads=[BPS[0]], writes=[Bd0])
                    P.op(DVE, lambda e: e.tensor_copy(t1, PS[3][:]), reads=[BPS[3]], writes=[Bt1])
                    P.op(ACT, lambda e: e.activation(out=d1, in_=PS[1][:], func=AF.Ln), reads=[BPS[1]], writes=[Bd1])
                    P.op(DVE, lambda e: e.tensor_copy(t2, PS[4][:]), reads=[BPS[4]], writes=[Bt2])
                    P.op(ACT, lambda e: e.activation(out=d0, in_=d0, func=AF.Exp, scale=-1.0), reads=[Bd0], writes=[Bd0])
                    P.op(ACT, lambda e: e.activation(out=d1, in_=d1, func=AF.Exp, scale=-1.0), reads=[Bd1], writes=[Bd1])
                    P.op(DVE, lambda e: e.tensor_tensor(t1, t1, d0, ALU.mult), reads=[Bt1, Bd0], writes=[Bt1])
                    P.op(DVE, lambda e: e.tensor_tensor(t2, t2, d1, ALU.mult), reads=[Bt2, Bd1], writes=[Bt2])
                    P.op(DVE, lambda e: e.scalar_tensor_tensor(
                        out=t1, in0=t2, scalar=lam_sb[:, n_layers + l: n_layers + l + 1], in1=t1,
                        op0=ALU.mult, op1=ALU.add), reads=[Bt1, Bt2, Blam], writes=[Bt1])

                def combine2(h=h, tok=tok, Q=Q):
                    stats_accum(sqb, Bsq, t1, [Bt1], PS[2], BPS[2], AVG128, True, True, scale=lam_scale)
                    rstd_from(PS[2], BPS[2], d0, [Bd0], eps=float(EPS / (1.0 - float(lam_inits[l])) ** 2), lnexp=True)
                    P.op(DVE, lambda e: e.scalar_tensor_tensor(
                        out=oc[:, h, tok], in0=t1, scalar=pcol(l, PC_SUB), in1=d0, op0=ALU.mult, op1=ALU.mult),
                        reads=[Bt1, Bd0, Bprm], writes=[Boc[h][Q]])
                last2 = iters[-1][1]
                iters[-1][1] = (lambda f=last2, cb=combine: (f(), cb()))
                pending.append((len(iters) - 1, combine2))
            for (idx, cb2) in pending:
                j = min(idx + 7, len(iters) - 1)
                f0 = iters[j][1]
                iters[j][1] = (lambda f=f0, cb=cb2: (f(), cb()))
            run_pipeline(iters, 2)

        if dbg:
            P.dma(SP, dbg_out[0:2].rearrange("c p t -> p c t"), oa, reads=flat(Boa), writes=[Bdbg], semb=Bdbg)
            P.dma(SP, dbg_out[2:6].rearrange("c p t -> p c t"), ob, reads=flat(Bob), writes=[Bdbg], semb=Bdbg)
            if not _os.environ.get("KDBGZ"):
                P.dma(SP, dbg_out[6:10].rearrange("c p t -> p c t"), oc, reads=flat(Boc), writes=[Bdbg], semb=Bdbg)
        phase_end(localC + flat(Bh))
        htg = carve(0, [8, 2, 512], BF16)
        Bhtg = [[Buf("htg%d_%d" % (c, t)) for t in range(2)] for c in range(8)]
        mg = carve(16384, [8, 2, 512], BF16)
        Bmg = [[Buf("mg%d_%d" % (c, t)) for t in range(2)] for c in range(8)]
        o = 75776
        sg = [carve(o + i * 2048, [512], F32) for i in range(2)]; o += 4096
        Bsg = [Buf("gsg0"), Buf("gsg1")]
        macc = [carve(o + i * 2048, [512], F32) for i in range(2)]; o += 4096
        Bmacc = [Buf("macc0"), Buf("macc1")]
        tt = [carve(o + i * 2048, [512], F32) for i in range(2)]; o += 4096
        Btt = [Buf("tt0"), Buf("tt1")]
        yT = carve(o, [8, 512], F32); o += 16384
        By = [Buf("my%d" % f) for f in range(8)]
        localG = flat(Bhtg) + flat(Bmg) + By + Bsg + Bmacc + Btt
        phase_begin(localG)
        obr = (oa, ob, oc)
        Bobr = (Boa, Bob, Boc)
        nkc = (2, 4, 4)
        poff = (0, 2, 6)
        for tp in range(2):
            for ti in range(2):
                tg = tp * 2 + ti
                tok = slice(tg * 512, tg * 512 + 512)
                for c in range(8):
                    P.op(DVE, lambda e, c=c, tok=tok, ti=ti: e.scalar_tensor_tensor(
                        out=htg[:, c, ti, :], in0=xT[:, c, tok], scalar=pcol(l, PC_NORM + 16 + c),
                        in1=rstd_mix[:, tok], op0=ALU.mult, op1=ALU.mult),
                        reads=[Bx[c][tg], Brmix[tg], Bprm], writes=[Bhtg[c][ti]])
            for f in range(8):
                slotg, sbg = wload(l, "gg", f * 128 * 3072, 3072)
                gv = slotg[:, 0:3072].rearrange("p (b c n) -> p b c n", b=3, c=8)
                slotp, sbp = wload(l, "gp", f * 128 * 1280, 1280)
                pv_ = slotp[:, 0:1280].rearrange("p (c n) -> p c n", c=10)
                for ti in range(2):
                    tg = tp * 2 + ti
                    tok = slice(tg * 512, tg * 512 + 512)
                    for b in range(3):
                        pgt, Bpgt = PS[(2 * b) % 6], BPS[(2 * b) % 6]
                        pyb, Bpyb = PS[(2 * b + 1) % 6], BPS[(2 * b + 1) % 6]
                        for c in range(8):
                            mm(pgt[:], gv[:, b, c, :], htg[:, c, ti, :], c == 0, c == 7, [sbg, Bhtg[c][ti]], [Bpgt])
                        for c in range(nkc[b]):
                            mm(pyb[:], pv_[:, poff[b] + c, :], obr[b][:, c, tok], c == 0, c == nkc[b] - 1,
                               [sbp, Bobr[b][c][tg]], [Bpyb])
                        k = kk["i"] % 2
                        kk["i"] += 1
                        P.op(ACT, lambda e, k=k, pgt=pgt, b=b, f=f: e.activation(
                            out=sg[k], in_=pgt[:], func=AF.Sigmoid, bias=pcol(l, PC_BG + b * 8 + f)),
                            reads=[Bpgt, Bprm], writes=[Bsg[k]])
                        if b == 0:
                            P.op(DVE, lambda e, k=k, pyb=pyb, ti=ti: e.tensor_tensor(macc[ti], sg[k], pyb[:], ALU.mult),
                                 reads=[Bsg[k], Bpyb], writes=[Bmacc[ti]])
                        else:
                            P.op(DVE, lambda e, k=k, pyb=pyb: e.tensor_tensor(tt[k], sg[k], pyb[:], ALU.mult),
                                 reads=[Bsg[k], Bpyb], writes=[Btt[k]])
                            if b == 1:
                                P.op(DVE, lambda e, k=k, ti=ti: e.tensor_tensor(macc[ti], macc[ti], tt[k], ALU.add),
                                     reads=[Bmacc[ti], Btt[k]], writes=[Bmacc[ti]])
                            else:
                                P.op(DVE, lambda e, k=k, f=f, ti=ti: e.tensor_tensor(mg[:, f, ti, :], macc[ti], tt[k], ALU.add),
                                     reads=[Bmacc[ti], Btt[k]], writes=[Bmg[f][ti]])
            for ti in range(2):
                tg = tp * 2 + ti
                tok = slice(tg * 512, tg * 512 + 512)
                for f0, nf in ((0, 3), (3, 3), (6, 2)):
                    slot, sb = wload(l, "wo", f0 * 128 * 1024, nf * 1024, npanel=nf)
                    wv = slot[:, 0:nf * 1024].rearrange("p (f c n) -> p f c n", f=nf, c=8)
                    for fi in range(nf):
                        f = f0 + fi
                        pw, Bpw = PS[f % 2], BPS[f % 2]
                        for c in range(8):
                            mm(pw[:], wv[:, fi, c, :], mg[:, c, ti, :], c == 0, c == 7, [sb, Bmg[c][ti]], [Bpw])
                        P.op(DVE, lambda e, f=f, pw=pw: e.tensor_copy(yT[:, f, :], pw[:]), reads=[Bpw], writes=[By[f]])
                        stats_accum(sqb, Bsq, pw[:], [Bpw], PS[6], BPS[6], AVG1024, f == 0, f == 7)
                r = sqk["r"] % 2
                sqk["r"] += 1
                rstd_from(PS[6], BPS[6], rstd_t[r][:], [Brstd[r]])
                for f in range(8):
                    k = kk["i"] % 2
                    kk["i"] += 1
                    P.op(DVE, lambda e, f=f, r=r, k=k: e.scalar_tensor_tensor(
                        out=tt[k], in0=yT[:, f, :], scalar=pcol(l, PC_NORM + 24 + f), in1=rstd_t[r][:],
                        op0=ALU.mult, op1=ALU.mult), reads=[By[f], Brstd[r], Bprm], writes=[Btt[k]])
                    P.op(DVE, lambda e, f=f, k=k, tok=tok: e.tensor_tensor(xT[:, f, tok], xT[:, f, tok], tt[k], ALU.add),
                         reads=[Btt[k], Bx[f][tg]], writes=[Bx[f][tg]])
        phase_end(localG + flat(Boa) + flat(Bob) + flat(Boc) + Bsq)

    Bout = Buf("yout")
    flay = ffn_layout()
    fbufs = []
    for item in flay[1::2]:
        for b in item:
            fbufs.extend(b if isinstance(b, list) else [b])
    for s in range(n_seq):
        if s > 0:
            P.dma(POOL, xT[:], xin[s].rearrange("c p t -> p c t"), writes=allx, semb=allx[0])
        for l in range(n_layers):
            if "ffn1" in stages:
                phase_begin(fbufs)
                ffn(l, 0, flay)
                phase_end(fbufs)
            if "mixer" in stages:
                mixer(l)
            if "ffn2" in stages:
                phase_begin(fbufs)
                ffn(l, 1, flay)
                phase_end(fbufs)
        P.force = True
        P.dma(POOL, yout[s].rearrange("c p t -> p c t"), xT[:], reads=allx, writes=[Bout], semb=Bout)
        P.force = False
    P.emit(final_wait_bufs=[Bout] + ([Bdbg] if dbg else []))
    return nc, P


def _panels(W, col0, ncols):
    K = W.shape[0]
    sub = W[:, col0:col0 + ncols].reshape(K // 128, 128, ncols // 128, 128)
    return np.ascontiguousarray(sub.transpose(2, 1, 0, 3))


def pack_layer_weights(inp, l):
    out = np.empty(WL_SZ, np.float32)

    def put(name, arr):
        o, s = OFF[name]
        assert arr.size == s, (name, arr.size, s)
        out[o:o + s] = arr.reshape(-1)

    for idx, pre in ((1, "ffn1"), (2, "ffn2")):
        g = _panels(inp[pre + "_w_gate"][l], 0, DFF)
        u = _panels(inp[pre + "_w_up"][l], 0, DFF)
        put("gu%d" % idx, np.stack([g, u], axis=2))
        put("d%d" % idx, _panels(inp[pre + "_w_down"][l], 0, D))
    win = inp["w_in"][l]
    wp = _panels(win, 0, 7936)
    wa = np.empty((6, 128, 3, 8, 128), np.float32)
    for jp in range(2):
        for g in range(3):
            for q in range(3):
                wa[jp * 3 + g, :, q] = wp[q * 6 + g * 2 + jp]
    put("wa", wa)
    wb = np.empty((4, 128, 2, 8, 128), np.float32)
    for cc in range(4):
        wb[cc, :, 0] = wp[30 + cc]
        wb[cc, :, 1] = wp[34 + cc]
    put("wb", wb)
    wc = np.empty((4, 128, 3, 8, 128), np.float32)
    for h in range(4):
        for q in range(3):
            wc[h, :, q] = wp[18 + q * 4 + h]
    put("wc", wc)
    gg = np.empty((8, 128, 3, 8, 128), np.float32)
    for f in range(8):
        for b in range(3):
            gg[f, :, b] = wp[38 + b * 8 + f]
    put("gg", gg)
    pa = _panels(inp["w_proj_a"][l], 0, D)
    pb = _panels(inp["w_proj_b"][l], 0, D)
    pc = _panels(inp["w_proj_c"][l], 0, D)
    put("gp", np.concatenate([pa, pb, pc], axis=2))
    put("wo", _panels(inp["w_out"][l], 0, D))
    return out.reshape(128, WL_SZ // 128)


def pack_params(inp, layers):
    nl = len(layers)
    prm = np.zeros((128, nl * PC_N), np.float32)
    lam = np.zeros((128, nl * LAMC), np.float32)
    for i, l in enumerate(layers):
        b = i * PC_N
        for k, name in enumerate(("ffn1_norm_pre", "ffn1_norm_post", "mix_norm_pre", "mix_norm_post",
                                  "ffn2_norm_pre", "ffn2_norm_post")):
            prm[:, b + PC_NORM + 8 * k: b + PC_NORM + 8 * k + 8] = inp[name][l].reshape(8, 128).T
        prm[:, b + PC_BG: b + PC_BG + 24] = inp["b_gate"][l].reshape(24, 128).T
        cw = inp["conv_w"][l]
        for cc in range(4):
            prm[:, b + PC_CW + cc * 31: b + PC_CW + cc * 31 + 31] = cw[:, cc * 128:(cc + 1) * 128].T
        prm[:, b + PC_CB: b + PC_CB + 4] = inp["conv_b"][l].reshape(4, 128).T
        prm[:, b + PC_CG: b + PC_CG + 4] = inp["conv_norm_g"][l].reshape(4, 128).T
        prm[:, b + PC_CNB: b + PC_CNB + 4] = inp["conv_norm_b"][l].reshape(4, 128).T
        prm[:, b + PC_SUB] = inp["diff_subln"][l]
        for k, name in enumerate(("lambda_q1", "lambda_k1", "lambda_q2", "lambda_k2")):
            lam[:, i * LAMC + 64 * k: i * LAMC + 64 * k + 64] = inp[name][l][None, :]
    return prm, lam


def const_tables():
    inv = (10000.0 ** (-np.arange(0, 64, 2, dtype=np.float32) / 64)).astype(np.float32)
    ang = np.arange(SEQ, dtype=np.float32)[None, :] * inv[:, None]
    cos = np.cos(ang).astype(np.float32)
    sin = np.sin(ang).astype(np.float32)
    C = np.tile(cos, (4, 1))
    S_ = np.tile(sin, (4, 1))
    rope = np.stack([C.reshape(128, 4, 512), S_.reshape(128, 4, 512)], axis=2)
    rope = np.ascontiguousarray(rope.transpose(1, 0, 2, 3))
    cm = np.zeros((128, 7 * 128 + 256), np.float32)
    cm[:, 0:128] = 1.0 / 1024
    cm[:, 128:256] = 1.0 / 512
    cm[:, 256:384] = 1.0 / 128
    cm[:, 384:512] = 1.0
    R = np.zeros((128, 128), np.float32)
    for blk in (0, 64):
        for dd in range(32):
            R[blk + dd + 32, blk + dd] = -1.0
            R[blk + dd, blk + dd + 32] = 1.0
    cm[:, 512:640] = R
    cm[:, 640:768] = np.eye(128, dtype=np.float32)
    i = np.arange(128)[:, None]
    j = np.arange(128)[None, :]
    cm[:, 896:1024] = (j >= i).astype(np.float32)
    cm[:, 1024:1152] = (j <= i).astype(np.float32)
    return rope, cm


_CACHE = {}


def _get_prog(n_layers, n_seq, lam_inits):
    key = (n_layers, n_seq, tuple(lam_inits))
    if key not in _CACHE:
        _CACHE[key] = build_program(n_layers, n_seq, lam_inits)[0]
    return _CACHE[key]


MODE = "fused"


def kernel(**inputs):
    inp = {k: np.asarray(v) for k, v in inputs.items()}
    x = inp["x"]
    B = x.shape[0]
    xT = np.ascontiguousarray(x.reshape(NCORES, SEQ_PER_CORE, SEQ, 8, 128).transpose(0, 1, 3, 4, 2))
    layers = list(range(DEPTH))
    rope, cm = const_tables()
    if MODE == "fused":
        wts = np.stack([pack_layer_weights(inp, l) for l in layers], axis=0)
        prm, lam = pack_params(inp, layers)
        nc = _get_prog(DEPTH, SEQ_PER_CORE, [_lam_init(l) for l in layers])
        in_maps = [{"xin": xT[c], "wts": wts, "prm": prm, "lamrows": lam, "rope": rope, "cmat": cm}
                   for c in range(NCORES)]
        res = run_bass_kernel_spmd(nc, in_maps, core_ids=list(range(NCORES)))
        yT = np.stack([res.results[c]["yout"] for c in range(NCORES)], axis=0)
    else:
        yT = xT
        for l in layers:
            wts = pack_layer_weights(inp, l)[None]
            prm, lam = pack_params(inp, [l])
            nc = _get_prog(1, 1, [_lam_init(l)])
            nxt = np.empty_like(yT)
            for h in range(SEQ_PER_CORE):
                in_maps = [{"xin": np.ascontiguousarray(yT[c, h:h + 1]), "wts": wts, "prm": prm,
                            "lamrows": lam, "rope": rope, "cmat": cm} for c in range(NCORES)]
                res = run_bass_kernel_spmd(nc, in_maps, core_ids=list(range(NCORES)))
                for c in range(NCORES):
                    nxt[c, h:h + 1] = res.results[c]["yout"]
            yT = nxt
    y = yT.transpose(0, 1, 4, 2, 3).reshape(B, SEQ, D)
    return np.ascontiguousarray(y.astype(np.float32))
```

```python
import contextlib
import numpy as np
import concourse.bass as bass
import concourse.mybir as mybir
from concourse.bass_utils import run_bass_kernel_spmd

F32 = mybir.dt.float32
BF16 = mybir.dt.bfloat16
AF = mybir.ActivationFunctionType
ALU = mybir.AluOpType

PE, ACT, DVE, POOL, SP = "pe", "act", "dve", "pool", "sp"

D = 1024
SEQ = 2048
DFF = 2816
NFF = 22
NCORES = 8
SEQ_PER_CORE = 4
DEPTH = 4
EPS = 1e-6
DIL = (1, 4, 16)

GU_SZ = NFF * 128 * 2 * 8 * 128
D_SZ = 8 * 128 * NFF * 128
WA_SZ = 6 * 128 * 3 * 8 * 128
WB_SZ = 4 * 128 * 2 * 8 * 128
WC_SZ = 4 * 128 * 3 * 8 * 128
GG_SZ = 8 * 128 * 3 * 8 * 128
GP_SZ = 8 * 128 * 10 * 128
WO_SZ = 8 * 128 * 8 * 128
OFF = {}
_o = 0
for _n, _s in (("gu1", GU_SZ), ("d1", D_SZ), ("wa", WA_SZ), ("wb", WB_SZ), ("wc", WC_SZ),
               ("gg", GG_SZ), ("gp", GP_SZ), ("wo", WO_SZ), ("gu2", GU_SZ), ("d2", D_SZ)):
    OFF[_n] = (_o, _s)
    _o += _s
WL_SZ = _o
assert WL_SZ % 128 == 0

PC_NORM = 0
PC_BG = 48
PC_CW = 72
PC_CB = 196
PC_CG = 200
PC_CNB = 204
PC_SUB = 208
PC_N = 212
LAMC = 4 * 64


def _lam_init(layer):
    return 0.8 - 0.6 * float(np.exp(-0.3 * layer))


class Buf:
    __slots__ = ("name", "lw", "rd", "sem", "cnt", "excl")

    def __init__(self, name, excl=False):
        self.name = name
        self.excl = excl
        self.lw = None
        self.rd = {}
        self.sem = None
        self.cnt = 0


class Prog:
    def __init__(self, nc):
        self.nc = nc
        self.ops = []
        self.stack = contextlib.ExitStack()
        self.n_dma_sems = 0
        self.maxops = None
        self.force = False

    def sbuf(self, name, shape, dt):
        return self.stack.enter_context(self.nc.sbuf_tensor(name, shape, dt))

    def psum(self, name, shape, dt):
        return self.stack.enter_context(self.nc.psum_tensor(name, shape, dt))

    def _deps(self, eng, reads, writes, is_dma, nodep=False):
        deps = set()
        idx = len(self.ops)
        key = ("dma", idx) if is_dma else eng
        xreads = [b for b in reads if b.excl]
        for b in reads:
            if b.lw is not None:
                deps.add(b.lw)
        if not nodep:
            for b in list(writes) + xreads:
                if b.lw is not None:
                    deps.add(b.lw)
                deps.update(b.rd.values())
        for b in reads:
            if not b.excl:
                b.rd[key] = idx
        for b in list(writes) + xreads:
            b.lw = idx
            b.rd = {}
        deps.discard(idx)
        return deps

    def op(self, eng, fn, reads=(), writes=()):
        if self.maxops is not None and len(self.ops) >= self.maxops and not self.force:
            return
        deps = self._deps(eng, reads, writes, False)
        self.ops.append([eng, fn, deps, False, None, 0, False, 0, frozenset(reads), frozenset(writes)])

    def dma(self, queue, out_ap, in_ap, reads=(), writes=(), semb=None, nodep=False):
        if self.maxops is not None and len(self.ops) >= self.maxops and not self.force:
            return
        if semb is None:
            semb = writes[0] if writes else reads[0]
        if semb.sem is None:
            semb.sem = self.n_dma_sems
            self.n_dma_sems += 1
        deps = self._deps(queue, reads, writes, True, nodep)
        semb.cnt += 16
        fn = lambda e, o=out_ap, i=in_ap: e.dma_start(out=o, in_=i)
        self.ops.append([queue, fn, deps, True, semb, semb.cnt, False, 0, frozenset(reads), frozenset(writes)])

    def emit(self, final_wait_bufs=()):
        nc = self.nc
        ops = self.ops
        for o in ops:
            for d in o[2]:
                if not ops[d][3]:
                    ops[d][6] = True
        counts = {PE: 0, ACT: 0, DVE: 0, POOL: 0, SP: 0}
        for o in ops:
            if o[6]:
                counts[o[0]] += 1
                o[7] = counts[o[0]]
        st = self.stack
        esem = {e: st.enter_context(nc.semaphore("s_" + e)) for e in (PE, ACT, DVE, POOL, SP)}
        dsem = [st.enter_context(nc.semaphore("d%d" % i)) for i in range(self.n_dma_sems)]
        per_eng = {PE: [], ACT: [], DVE: [], POOL: [], SP: []}
        for i, o in enumerate(ops):
            per_eng[o[0]].append(i)
        self.n_waits = 0

        def run(eng_name, e):
            known = {}
            for i in per_eng[eng_name]:
                o = ops[i]
                need = {}
                for d in o[2]:
                    od = ops[d]
                    if od[3]:
                        key = ("d", od[4].sem)
                        val = od[5]
                    else:
                        if od[0] == eng_name:
                            if eng_name == PE:
                                continue
                            if not ((od[9] & o[8]) or (od[8] & o[9])):
                                continue
                        key = ("e", od[0])
                        val = od[7]
                    if known.get(key, 0) >= val:
                        continue
                    if need.get(key, 0) < val:
                        need[key] = val
                for key, val in need.items():
                    sem = dsem[key[1]] if key[0] == "d" else esem[key[1]]
                    e.wait_ge(sem, val)
                    known[key] = val
                    self.n_waits += 1
                ins = o[1](e)
                if o[3]:
                    ins.then_inc(dsem[o[4].sem], 16)
                elif o[6]:
                    ins.then_inc(esem[eng_name], 1)
            if eng_name == SP:
                for b in final_wait_bufs:
                    e.wait_ge(dsem[b.sem], b.cnt)

        with nc.Block() as block:
            @block.tensor
            def _(e):
                run(PE, e)

            @block.scalar
            def _(e):
                run(ACT, e)

            @block.vector
            def _(e):
                run(DVE, e)

            @block.gpsimd
            def _(e):
                run(POOL, e)

            @block.sync
            def _(e):
                run(SP, e)
        st.close()


def build_program(n_layers, n_seq, lam_inits, stages=("ffn1", "mixer", "ffn2"), dbg=False):
    nc = bass.Bass("TRN2", target_bir_lowering=False)
    P = Prog(nc)
    import os as _os
    if _os.environ.get("KCUT"):
        P.maxops = int(_os.environ["KCUT"])
    xin = nc.dram_tensor("xin", [n_seq, 8, 128, SEQ], F32, kind="ExternalInput").ap()
    wts = nc.dram_tensor("wts", [n_layers, 128, WL_SZ // 128], F32, kind="ExternalInput").ap()
    prm = nc.dram_tensor("prm", [128, n_layers * PC_N], F32, kind="ExternalInput").ap()
    lamrows = nc.dram_tensor("lamrows", [128, n_layers * LAMC], F32, kind="ExternalInput").ap()
    rope = nc.dram_tensor("rope", [4, 128, 2, 512], F32, kind="ExternalInput").ap()
    cmat = nc.dram_tensor("cmat", [128, 7 * 128 + 256], F32, kind="ExternalInput").ap()
    yout = nc.dram_tensor("yout", [n_seq, 8, 128, SEQ], F32, kind="ExternalOutput").ap()
    if dbg:
        dbg_out = nc.dram_tensor("dbg_out", [10, 128, SEQ], BF16, kind="ExternalOutput").ap()
        Bdbg = Buf("dbg")
    wsc = nc.dram_tensor("wsc", [n_layers, 128, WL_SZ // 128], BF16, kind="Internal").ap()
    wflat = [wsc[l].rearrange("p x -> (p x)") for l in range(n_layers)]

    xT = P.sbuf("xT", [128, 8, SEQ], F32)
    Bx = [[Buf("x%d_%d" % (c, t)) for t in range(4)] for c in range(8)]
    NSLOT, SLOT = 3, 3072
    ring = [P.sbuf("ring%d" % i, [128, SLOT], BF16) for i in range(NSLOT)]
    Bring = [Buf("ring%d" % i) for i in range(NSLOT)]
    prm_sb = P.sbuf("prm_sb", [128, n_layers * PC_N], F32)
    Bprm = Buf("prm")
    cm = P.sbuf("cm", [128, 7 * 128 + 256], BF16)
    Bcm = Buf("cm")
    identf = P.sbuf("identf", [128, 128], F32)
    Bidf = Buf("identf")
    lam_sb = P.sbuf("lam_sb", [128, 2 * n_layers], F32)
    Blam = Buf("lam")
    rstd_t = [P.sbuf("rstd%d" % i, [128, 512], F32) for i in range(2)]
    Brstd = [Buf("rstd%d" % i) for i in range(2)]
    rstd_mix = P.sbuf("rstd_mix", [128, SEQ], F32)
    Brmix = [Buf("rmix%d" % t) for t in range(4)]
    rstd_pre = [rstd_mix[:, 0:512], rstd_mix[:, 512:1024]]
    Brpre = [Brmix[0], Brmix[1]]
    SCR = 106 * 1024
    scr = P.sbuf("scr", [128, SCR // 2], BF16)

    def carve(off_bytes, shape, dt):
        n = int(np.prod(shape))
        esz = 4 if dt == F32 else 2
        assert off_bytes % 4 == 0 and off_bytes + n * esz <= SCR, (off_bytes, n * esz, SCR)
        ap = scr[:, off_bytes // 2: off_bytes // 2 + n * esz // 2]
        if dt == F32:
            ap = ap.bitcast(F32)
        if len(shape) == 2:
            ap = ap.rearrange("p (a b) -> p a b", a=shape[0])
        elif len(shape) == 3:
            ap = ap.rearrange("p (a b c) -> p a b c", a=shape[0], b=shape[1])
        return ap

    dummy = P.sbuf("dummy_bar", [128, 2], F32)
    phase = {"bar": None}

    def phase_begin(bufs):
        for b in bufs:
            b.lw = phase["bar"]
            b.rd = {}

    def phase_end(bufs):
        bufs = list(bufs)
        P.op(DVE, lambda e: e.memset(dummy[:, 0:1], 0.0), reads=bufs, writes=bufs)
        phase["bar"] = len(P.ops) - 1

    eps_vals = [float(EPS)] + [float(EPS / (1.0 - float(li)) ** 2) for li in lam_inits]
    eps_sb = P.sbuf("eps_sb", [128, len(eps_vals)], F32)
    Beps = Buf("eps")
    epsc = {}
    for i, v in enumerate(eps_vals):
        epsc[v] = eps_sb[:, i:i + 1]
        P.op(DVE, lambda e, i=i, v=v: e.memset(eps_sb[:, i:i + 1], v), writes=[Beps])
    PS = [P.psum("ps%d" % i, [128, 512], F32) for i in range(8)]
    BPS = [Buf("ps%d" % i, excl=True) for i in range(8)]

    AVG1024 = cm[:, 0:128]
    AVG512 = cm[:, 128:256]
    AVG128 = cm[:, 256:384]
    ONES = cm[:, 384:512]
    RMT = cm[:, 512:640]
    MASK = cm[:, 896:1152]

    allx = [Bx[c][t] for c in range(8) for t in range(4)]
    P.dma(POOL, xT[:], xin[0].rearrange("c p t -> p c t"), writes=allx, semb=allx[0])
    P.dma(POOL, cm[:], cmat[:, :], writes=[Bcm])
    P.dma(SP, identf[:], cmat[:, 640:768], writes=[Bidf])
    P.dma(SP, prm_sb[:], prm[:, :], writes=[Bprm])
    lr = carve(0, [n_layers * LAMC], F32)
    Blr = Buf("lr")
    P.dma(SP, lr, lamrows[:, :], writes=[Blr])
    lt = carve(n_layers * LAMC * 4, [n_layers * 128 + 4 * n_layers], F32)
    Blt = Buf("lt")
    for l in range(n_layers):
        b0 = l * LAMC
        prod = lt[:, l * 128: l * 128 + 128]
        sums = lt[:, n_layers * 128 + 4 * l: n_layers * 128 + 4 * l + 4]
        P.op(DVE, lambda e, o=prod[:, 0:64], a=lr[:, b0:b0 + 64], b=lr[:, b0 + 64:b0 + 128]:
             e.tensor_tensor(o, a, b, ALU.mult), reads=[Blr], writes=[Blt])
        P.op(DVE, lambda e, o=prod[:, 64:128], a=lr[:, b0 + 128:b0 + 192], b=lr[:, b0 + 192:b0 + 256]:
             e.tensor_tensor(o, a, b, ALU.mult), reads=[Blr], writes=[Blt])
        P.op(DVE, lambda e, o=sums[:, 0:1], i=prod[:, 0:64]:
             e.reduce_sum(o, i, mybir.AxisListType.X), reads=[Blt], writes=[Blt])
        P.op(DVE, lambda e, o=sums[:, 1:2], i=prod[:, 64:128]:
             e.reduce_sum(o, i, mybir.AxisListType.X), reads=[Blt], writes=[Blt])
        P.op(ACT, lambda e, o=sums[:, 2:4], i=sums[:, 0:2]:
             e.activation(out=o, in_=i, func=AF.Exp), reads=[Blt], writes=[Blt])
        P.op(DVE, lambda e, o=lam_sb[:, l:l + 1], a=sums[:, 2:3], b=sums[:, 3:4]:
             e.tensor_tensor(o, a, b, ALU.subtract), reads=[Blt], writes=[Blam])
        P.op(DVE, lambda e, o=lam_sb[:, l:l + 1], li=float(lam_inits[l]):
             e.tensor_scalar(o, o, li, None, ALU.add), reads=[Blam], writes=[Blam])
        P.op(DVE, lambda e, o=lam_sb[:, n_layers + l:n_layers + l + 1], i=lam_sb[:, l:l + 1]:
             e.tensor_scalar(o, i, -1.0, None, ALU.mult), reads=[Blam], writes=[Blam])

    phase_end([Blr, Blt])
    Bw = {}
    WCOLS = WL_SZ // 128
    for l in range(n_layers):
        for name, (o, s) in OFF.items():
            Bw[(l, name)] = Buf("w%d%s" % (l, name))
    wflat32 = [wts[l].rearrange("p x -> (p x)") for l in range(n_layers)]
    for l in range(n_layers):
        for name, (o, s) in OFF.items():
            cols = s // 128
            src = wflat32[l][o:o + s].rearrange("(p x) -> p x", p=128)
            dst = wflat[l][o:o + s].rearrange("(p x) -> p x", p=128)
            step = 8192
            for c0 in range(0, cols, step):
                c1 = min(cols, c0 + step)
                P.dma(POOL, dst[:, c0:c1], src[:, c0:c1], writes=[Bw[(l, name)]], semb=Bw[(l, name)], nodep=True)

    wstate = {"i": 0}

    def wload(l, name, elem_off, per_part, npanel=1):
        i = wstate["i"]
        wstate["i"] += 1
        slot = ring[i % NSLOT]
        sb = Bring[i % NSLOT]
        o, s = OFF[name]
        assert per_part <= SLOT and elem_off + 128 * per_part <= s
        if npanel == 1:
            src = wflat[l][o + elem_off: o + elem_off + 128 * per_part].rearrange("(p x) -> p x", p=128)
            dst = slot[:, 0:per_part]
        else:
            src = wflat[l][o + elem_off: o + elem_off + 128 * per_part].rearrange(
                "(f p x) -> p f x", f=npanel, p=128)
            dst = slot[:, 0:per_part].rearrange("p (f x) -> p f x", f=npanel)
        P.dma(SP, dst, src, reads=[Bw[(l, name)]], writes=[sb], semb=sb)
        return slot, sb

    sqk = {"i": 0, "r": 0, "t": 0}

    def pcol(l, col):
        return prm_sb[:, l * PC_N + col: l * PC_N + col + 1]

    def mm(out, lhsT, rhs, start, stop, reads, writes):
        P.op(PE, lambda e: e.matmul(out, lhsT, rhs, start=start, stop=stop, skip_group_check=True),
             reads=reads, writes=writes)

    def stats_accum(sqbufs, Bsq, src_ap, src_bufs, ps_stat, Bps_stat, avg, first, last, scale=1.0, N=512):
        k = sqk["i"] % 2
        sqk["i"] += 1
        sq = sqbufs[k]
        P.op(ACT, lambda e: e.activation(out=sq[:, 0:N], in_=src_ap, func=AF.Square, scale=scale),
             reads=src_bufs, writes=[Bsq[k]])
        mm(ps_stat[:, 0:N], avg, sq[:, 0:N], first, last, [Bsq[k], Bcm], [Bps_stat])

    def rstd_from(ps_stat, Bps_stat, out_ap, out_bufs, eps=EPS, N=512, lnexp=True):
        if lnexp:
            P.op(ACT, lambda e: e.activation(out=out_ap, in_=ps_stat[:, 0:N], func=AF.Ln, bias=epsc[float(eps)]),
                 reads=[Bps_stat, Beps], writes=out_bufs)
            P.op(ACT, lambda e: e.activation(out=out_ap, in_=out_ap, func=AF.Exp, scale=-0.5),
                 reads=out_bufs, writes=out_bufs)
            return
        P.op(ACT, lambda e: e.activation(out=out_ap, in_=ps_stat[:, 0:N], func=AF.Sqrt, bias=epsc[float(eps)]),
             reads=[Bps_stat, Beps], writes=out_bufs)
        P.op(DVE, lambda e: e.reciprocal(out_ap, out_ap), reads=out_bufs, writes=out_bufs)

    def ffn(l, which, lay):
        gname, dname = ("gu1", "d1") if which == 0 else ("gu2", "d2")
        ncol_pre = PC_NORM + (0 if which == 0 else 32)
        ncol_post = ncol_pre + 8
        hT, Bh, actT, Bact, yT, By, sqb, Bsq, tmpA, BtA, tmpB, BtB = lay

        def prenorm(half, ps_i=7):
            for tgi in range(2):
                tg = half * 2 + tgi
                tok = slice(tg * 512, tg * 512 + 512)
                r = sqk["r"] % 2
                sqk["r"] += 1
                for c in range(8):
                    stats_accum(sqb, Bsq, xT[:, c, tok], [Bx[c][tg]], PS[ps_i], BPS[ps_i], AVG1024, c == 0, c == 7)
                rstd_from(PS[ps_i], BPS[ps_i], rstd_pre[tgi][:], [Brpre[tgi]])
                for c in range(8):
                    P.op(DVE, lambda e, c=c, tok=tok, r=tgi, tgi=tgi: e.scalar_tensor_tensor(
                        out=hT[:, c, tgi * 512: tgi * 512 + 512], in0=xT[:, c, tok], scalar=pcol(l, ncol_pre + c),
                        in1=rstd_pre[tgi][:], op0=ALU.mult, op1=ALU.mult),
                        reads=[Bx[c][tg], Brpre[tgi], Bprm], writes=[Bh[c][tgi]])

        def gateup(half):
            for j in range(NFF):
                slot, sb = wload(l, gname, j * 128 * 2048, 2048)
                sv = slot[:, 0:2048].rearrange("p (g c n) -> p g c n", g=2, c=8)
                for tgi in range(2):
                    tg = half * 2 + tgi
                    tok = slice(tg * 512, tg * 512 + 512)
                    pg = (2 * j + tgi) % 2
                    psg, psu = PS[pg * 2], PS[pg * 2 + 1]
                    Bg, Bu = BPS[pg * 2], BPS[pg * 2 + 1]
                    for gu, (pp, Bp) in enumerate(((psg, Bg), (psu, Bu))):
                        for c in range(8):
                            mm(pp[:], sv[:, gu, c, :], hT[:, c, tgi * 512: tgi * 512 + 512], c == 0, c == 7,
                               [sb, Bh[c][tgi]], [Bp])
                    k = sqk["t"] % 2
                    sqk["t"] += 1
                    P.op(ACT, lambda e, k=k, psg=psg: e.activation(out=tmpA[k], in_=psg[:], func=AF.Silu),
                         reads=[Bg], writes=[BtA[k]])
                    P.op(DVE, lambda e, k=k, psu=psu, j=j, tgi=tgi: e.tensor_tensor(
                        actT[:, j, tgi * 512: tgi * 512 + 512], tmpA[k], psu[:], ALU.mult),
                        reads=[BtA[k], Bu], writes=[Bact[j][tgi]])

        def down(half):
            for f in range(8):
                slot, sb = wload(l, dname, f * 128 * NFF * 128, NFF * 128)
                sv = slot[:, 0:NFF * 128].rearrange("p (c n) -> p c n", c=NFF)
                for tgi in range(2):
                    pd = PS[4 + tgi]
                    Bpd = BPS[4 + tgi]
                    for cc in range(NFF):
                        mm(pd[:], sv[:, cc, :], actT[:, cc, tgi * 512: tgi * 512 + 512], cc == 0, cc == NFF - 1,
                           [sb, Bact[cc][tgi]], [Bpd])
                    P.op(DVE, lambda e, f=f, pd=pd, tgi=tgi: e.tensor_copy(yT[:, f, tgi, :], pd[:]),
                         reads=[Bpd], writes=[By[f][tgi]])
                    stats_accum(sqb, Bsq, pd[:], [Bpd], PS[6 + tgi], BPS[6 + tgi], AVG1024, f == 0, f == 7)

        def down_resid(half):
            for tgi in range(2):
                tg = half * 2 + tgi
                tok = slice(tg * 512, tg * 512 + 512)
                r = sqk["r"] % 2
                sqk["r"] += 1
                rstd_from(PS[6 + tgi], BPS[6 + tgi], rstd_t[r][:], [Brstd[r]])
                for f in range(8):
                    k = sqk["t"] % 2
                    sqk["t"] += 1
                    P.op(DVE, lambda e, f=f, r=r, k=k, tgi=tgi: e.scalar_tensor_tensor(
                        out=tmpB[k], in0=yT[:, f, tgi, :], scalar=pcol(l, ncol_post + f), in1=rstd_t[r][:],
                        op0=ALU.mult, op1=ALU.mult), reads=[By[f][tgi], Brstd[r], Bprm], writes=[BtB[k]])
                    P.op(DVE, lambda e, f=f, k=k, tok=tok: e.scalar_tensor_tensor(
                        out=xT[:, f, tok], in0=tmpB[k], scalar=0.5, in1=xT[:, f, tok],
                        op0=ALU.mult, op1=ALU.add), reads=[BtB[k], Bx[f][tg]], writes=[Bx[f][tg]])

        prenorm(0)
        gateup(0)
        down(0)
        prenorm(1, ps_i=3)
        down_resid(0)
        gateup(1)
        down(1)
        down_resid(1)

    def ffn_layout():
        o = 0
        hT = carve(o, [8, 1024], BF16); o += 8 * 1024 * 2
        actT = carve(o, [NFF, 1024], BF16); o += NFF * 1024 * 2
        yT = carve(o, [8, 2, 512], F32); o += 8 * 1024 * 4
        sqb = [carve(o + i * 1024, [512], BF16) for i in range(2)]; o += 2048
        tmpA = [carve(o + i * 2048, [512], F32) for i in range(2)]; o += 4096
        tmpB = [carve(o + i * 2048, [512], F32) for i in range(2)]; o += 4096
        return (hT, [[Buf("fh%d_%d" % (c, t)) for t in range(2)] for c in range(8)],
                actT, [[Buf("act%d_%d" % (j, t)) for t in range(2)] for j in range(NFF)],
                yT, [[Buf("y%d_%d" % (f, t)) for t in range(2)] for f in range(8)],
                sqb, [Buf("sq0"), Buf("sq1")], tmpA, [Buf("tA0"), Buf("tA1")], tmpB, [Buf("tB0"), Buf("tB1")])

    def run_pipeline(iters, skew=2):
        n = len(iters)
        for i in range(n + skew):
            if i < n:
                iters[i][0]()
            if i >= skew:
                iters[i - skew][1]()

    def mixer(l):
        hT = carve(0, [8, SEQ], BF16)
        Bh = [[Buf("mh%d_%d" % (c, t)) for t in range(4)] for c in range(8)]
        oa = carve(32768, [2, SEQ], BF16)
        Boa = [[Buf("oa%d_%d" % (c, t)) for t in range(4)] for c in range(2)]
        sqb = [carve(40960 + i * 1024, [512], BF16) for i in range(2)]
        Bsq = [Buf("msq0"), Buf("msq1")]
        ob = carve(43008, [4, SEQ], BF16)
        Bob = [[Buf("ob%d_%d" % (c, t)) for t in range(4)] for c in range(4)]
        oc = carve(59392, [4, SEQ], BF16)
        Boc = [[Buf("oc%d_%d" % (c, t)) for t in range(4)] for c in range(4)]
        flat = lambda ll: [b for row in ll for b in row]
        wide = flat(Bh) + flat(Boa) + flat(Bob) + flat(Boc) + Bsq
        phase_begin(wide)

        for tg in range(4):
            tok = slice(tg * 512, tg * 512 + 512)
            for c in range(8):
                stats_accum(sqb, Bsq, xT[:, c, tok], [Bx[c][tg]], PS[7], BPS[7], AVG1024, c == 0, c == 7)
            rstd_from(PS[7], BPS[7], rstd_mix[:, tok], [Brmix[tg]])
            for c in range(8):
                P.op(DVE, lambda e, c=c, tok=tok: e.scalar_tensor_tensor(
                    out=hT[:, c, tok], in0=xT[:, c, tok], scalar=pcol(l, PC_NORM + 16 + c),
                    in1=rstd_mix[:, tok], op0=ALU.mult, op1=ALU.mult),
                    reads=[Bx[c][tg], Brmix[tg], Bprm], writes=[Bh[c][tg]])

        ropek = {"i": 0}

        def rope_stages(wv, sb, tg, tabref, do_load, sets, dst_ap, dst_bufs, perm_d):
            k = ropek["i"] % 2
            ropek["i"] += 1
            qbf, Bqbf, t1, Bt1, t2, Bt2 = sets[k]
            tok = slice(tg * 512, tg * 512 + 512)
            pq, Bq = (PS[0], BPS[0]) if k == 0 else (PS[3], BPS[3])
            pr, Br = (PS[1], BPS[1]) if k == 0 else (PS[4], BPS[4])

            def st1():
                if do_load:
                    tabref["tb"], tabref["Btb"] = load_tab(tg)
                for c in range(8):
                    mm(pq[:], wv[:, c, :], hT[:, c, tok], c == 0, c == 7, [sb, Bh[c][tg]], [Bq])
                P.op(ACT, lambda e: e.activation(out=qbf, in_=pq[:], func=AF.Copy), reads=[Bq], writes=[Bqbf])

            def st2():
                tabs, Btab = tabref["tb"], tabref["Btb"]
                mm(pr[:], RMT, qbf, True, True, [Bqbf, Bcm], [Br])
                P.op(DVE, lambda e: e.tensor_tensor(t1, pq[:], tabs[:, 0, :], ALU.mult), reads=[Bq, Btab], writes=[Bt1])
                P.op(DVE, lambda e: e.tensor_tensor(t2, pr[:], tabs[:, 1, :], ALU.mult), reads=[Br, Btab], writes=[Bt2])
                if perm_d == 1:
                    P.op(DVE, lambda e: e.tensor_tensor(dst_ap, t1, t2, ALU.add), reads=[Bt1, Bt2], writes=dst_bufs)
                else:
                    a_ = t1.rearrange("p (i r) -> p r i", r=perm_d)
                    b_ = t2.rearrange("p (i r) -> p r i", r=perm_d)
                    P.op(DVE, lambda e: e.tensor_tensor(dst_ap, a_, b_, ALU.add), reads=[Bt1, Bt2], writes=dst_bufs)
            return [st1, st2]

        o = 43008
        tabs = [carve(o + i * 4096, [2, 512], F32) for i in range(2)]; o += 8192
        Btab = [Buf("tab0"), Buf("tab1")]
        qg = carve(o, [SEQ], BF16); o += SEQ * 2
        kg = carve(o, [SEQ], BF16); o += SEQ * 2
        vg = carve(o, [16, 128], BF16); o += 16 * 128 * 2
        Bqg, Bkg, Bvg = Buf("qg"), Buf("kg"), Buf("vg")
        Oacc = carve(o, [SEQ], F32); o += SEQ * 4
        Dacc = carve(o, [SEQ], F32); o += SEQ * 4
        BOacc, BDacc = Buf("Oacc"), Buf("Dacc")
        qbf = carve(o, [512], BF16); o += 1024
        Bqbf = Buf("qbf")
        t1 = carve(o, [512], F32); o += 2048
        t2 = carve(o, [512], F32); o += 2048
        Bt1, Bt2 = Buf("t1"), Buf("t2")
        PT = [carve(o + i * 1024, [512], BF16) for i in range(3)]; o += 3072
        BPT = [Buf("pt0"), Buf("pt1"), Buf("pt2")]
        qbf2 = carve(o, [512], BF16); o += 1024
        t1b = carve(o, [512], F32); o += 2048
        t2b = carve(o, [512], F32); o += 2048
        Bqbf2, Bt1b, Bt2b = Buf("qbf2"), Buf("t1b"), Buf("t2b")
        setsA = [(qbf, Bqbf, t1, Bt1, t2, Bt2), (qbf2, Bqbf2, t1b, Bt1b, t2b, Bt2b)]
        tabk = {"i": 0}
        ptk = {"i": 0}
        localA = Btab + [Bqg, Bkg, Bvg, BOacc, BDacc, Bqbf, Bt1, Bt2, Bqbf2, Bt1b, Bt2b] + BPT
        phase_begin(localA)

        def load_tab(tg):
            k = tabk["i"] % 2
            tabk["i"] += 1
            P.dma(SP, tabs[k], rope[tg], writes=[Btab[k]])
            return tabs[k], Btab[k]

        for jp in range(2):
            for g in range(3):
                d = DIL[g]
                L = SEQ // d
                nb = L // 128
                slot, sb = wload(l, "wa", (jp * 3 + g) * 128 * 3072, 3072)
                wv = slot[:, 0:3072].rearrange("p (q c n) -> p q c n", q=3, c=8)
                allh = lambda c: [Bh[c][t] for t in range(4)]
                rits = []
                for tg in range(4):
                    tref = {}
                    for which, (dstT, Bdst) in enumerate(((qg, Bqg), (kg, Bkg))):
                        if d == 1:
                            dst = dstT[:, tg * 512: tg * 512 + 512]
                        else:
                            i0 = tg * 512 // d
                            dst = dstT.rearrange("p (r i) -> p r i", r=d)[:, :, i0: i0 + 512 // d]
                        rits.append(rope_stages(wv[:, which], sb, tg, tref, which == 0, setsA, dst, [Bdst], d))
                run_pipeline(rits, 1)
                for b0 in range(0, 16, 4):
                    pv, Bpv = PS[2], BPS[2]
                    for bi in range(4):
                        blk = b0 + bi
                        r, kb = blk // nb, blk % nb
                        for c in range(8):
                            if d == 1:
                                lhs = hT[:, c, kb * 128: kb * 128 + 128]
                            else:
                                lhs = hT[:, c, :].rearrange("p (i r) -> p r i", r=d)[:, r, kb * 128: kb * 128 + 128]
                            mm(pv[:, bi * 128: bi * 128 + 128], lhs, wv[:, 2, c, :], c == 0, c == 7,
                               [sb] + allh(c), [Bpv])
                    P.op(ACT, lambda e, b0=b0, pv=pv: e.activation(
                        out=vg[:, b0:b0 + 4, :], in_=pv[:].rearrange("p (a n) -> p a n", a=4), func=AF.Copy),
                        reads=[Bpv], writes=[Bvg])
                units = []
                for r in range(d):
                    for q0 in range(0, nb, 4):
                        units.append((r, q0, min(nb, q0 + 4)))
                per_tile = max(1, 512 // (128 * min(nb, 4)))
                iters = []
                for ti, u0 in enumerate(range(0, len(units), per_tile)):
                    tile_units = units[u0:u0 + per_tile]
                    po, Bpo = (PS[3], BPS[3]) if ti % 2 == 0 else (PS[0], BPS[0])
                    pdn, Bpdn = (PS[4], BPS[4]) if ti % 2 == 0 else (PS[1], BPS[1])
                    colbase = 0
                    tile_iters = []
                    for (r, q0, q1) in tile_units:
                        nq = (q1 - q0) * 128
                        for hh in range(2):
                            hr = slice(hh * 64, hh * 64 + 64)
                            kbs = list(range(max(q0 - 1, 0), q1))
                            for ki, kb in enumerate(kbs):
                                qlo = max(kb, q0)
                                qhi = min(kb + 1, q1 - 1)
                                n = (qhi - qlo + 1) * 128
                                off = colbase + (qlo - q0) * 128
                                bi3 = ptk["i"] % 3
                                ptk["i"] += 1
                                pss, Bpss = PS[5 + bi3], BPS[5 + bi3]
                                pt, Bpt = PT[bi3], BPT[bi3]
                                first = (ki == 0)
                                last = (ki == len(kbs) - 1)
                                blk = r * nb + kb

                                def st1(pss=pss, Bpss=Bpss, pt=pt, Bpt=Bpt, hr=hr, r=r, kb=kb, qlo=qlo, qhi=qhi, n=n):
                                    mm(pss[:, 0:n], kg[hr, r * L + kb * 128: r * L + kb * 128 + 128],
                                       qg[hr, r * L + qlo * 128: r * L + qlo * 128 + n], True, True, [Bkg, Bqg], [Bpss])
                                    P.op(ACT, lambda e: e.activation(out=pt[:, 0:n], in_=pss[:, 0:n], func=AF.Exp, scale=0.125),
                                         reads=[Bpss], writes=[Bpt])
                                    if qlo == kb:
                                        P.op(DVE, lambda e: e.tensor_tensor(pt[:, 0:128], pt[:, 0:128], MASK[:, 0:128], ALU.mult),
                                             reads=[Bpt, Bcm], writes=[Bpt])
                                    if qhi == kb + 1:
                                        P.op(DVE, lambda e: e.tensor_tensor(pt[:, n - 128:n], pt[:, n - 128:n], MASK[:, 128:256], ALU.mult),
                                             reads=[Bpt, Bcm], writes=[Bpt])

                                def st2(pt=pt, Bpt=Bpt, hr=hr, hh=hh, off=off, n=n, first=first, last=last, blk=blk,
                                        po=po, Bpo=Bpo, pdn=pdn, Bpdn=Bpdn):
                                    mm(po[hr, off:off + n], vg[:, blk, hh * 64: hh * 64 + 64], pt[:, 0:n], first, last,
                                       [Bvg, Bpt], [Bpo])
                                    mm(pdn[hr, off:off + n], ONES[:, 0:64], pt[:, 0:n], first, last, [Bcm, Bpt], [Bpdn])
                                tile_iters.append([st1, st2])
                        colbase += nq
                    ncols = colbase
                    (r0, q00, q01) = tile_units[0]
                    if d == 1:
                        dsto = Oacc[:, q00 * 128: q00 * 128 + ncols]
                        dstd = Dacc[:, q00 * 128: q00 * 128 + ncols]
                        so, sd = po[:, 0:ncols], pdn[:, 0:ncols]
                    else:
                        nu = len(tile_units)
                        nq = ncols // nu
                        if nu == 1:
                            dsto = Oacc.rearrange("p (i r) -> p r i", r=d)[:, r0, q00 * 128: q00 * 128 + nq]
                            dstd = Dacc.rearrange("p (i r) -> p r i", r=d)[:, r0, q00 * 128: q00 * 128 + nq]
                            so, sd = po[:, 0:ncols], pdn[:, 0:ncols]
                        else:
                            dsto = Oacc.rearrange("p (i r) -> p r i", r=d)[:, r0:r0 + nu, q00 * 128: q00 * 128 + nq]
                            dstd = Dacc.rearrange("p (i r) -> p r i", r=d)[:, r0:r0 + nu, q00 * 128: q00 * 128 + nq]
                            so = po[:, 0:ncols].rearrange("p (u n) -> p u n", u=nu)
                            sd = pdn[:, 0:ncols].rearrange("p (u n) -> p u n", u=nu)

                    def evac(g=g, dsto=dsto, dstd=dstd, so=so, sd=sd, Bpo=Bpo, Bpdn=Bpdn):
                        if g == 0:
                            P.op(ACT, lambda e: e.activation(out=dsto, in_=so, func=AF.Copy), reads=[Bpo], writes=[BOacc])
                            P.op(DVE, lambda e: e.tensor_copy(dstd, sd), reads=[Bpdn], writes=[BDacc])
                        else:
                            P.op(DVE, lambda e: e.tensor_tensor(dsto, so, dsto, ALU.add), reads=[Bpo, BOacc], writes=[BOacc])
                            P.op(DVE, lambda e: e.tensor_tensor(dstd, sd, dstd, ALU.add), reads=[Bpdn, BDacc], writes=[BDacc])
                    last2 = tile_iters[-1][1]
                    tile_iters[-1][1] = (lambda f=last2, ev=evac: (f(), ev()))
                    iters.extend(tile_iters)
                run_pipeline(iters, 2)
            for tg in range(4):
                tok = slice(tg * 512, tg * 512 + 512)
                P.op(ACT, lambda e, tok=tok: e.activation(out=Dacc[:, tok], in_=Dacc[:, tok], func=AF.Ln), reads=[BDacc], writes=[BDacc])
                P.op(ACT, lambda e, tok=tok: e.activation(out=Dacc[:, tok], in_=Dacc[:, tok], func=AF.Exp, scale=-1.0), reads=[BDacc], writes=[BDacc])
                P.op(DVE, lambda e, tok=tok, jp=jp: e.tensor_tensor(oa[:, jp, tok], Oacc[:, tok], Dacc[:, tok], ALU.mult),
                     reads=[BOacc, BDacc], writes=[Boa[jp][tg]])

        phase_end(localA)
        o = 59392
        zT = carve(o, [4, 32 + SEQ], BF16); o += 4 * (32 + SEQ) * 2
        Bz = [Buf("z%d" % c) for c in range(4)]
        dg = carve(o, [31, 128], BF16); o += 31 * 128 * 2
        Bdg = Buf("dg")
        BdgA, BdgB = Buf("dgA"), Buf("dgB")
        y32 = carve(o, [4, 512], F32); o += 4 * 512 * 4
        By32 = [Buf("y32_%d" % c) for c in range(4)]
        ybf = [carve(o + i * 1024, [512], BF16) for i in range(2)]; o += 2048
        Bybf = [Buf("ybf0"), Buf("ybf1")]
        sgB = [carve(o + i * 2048, [512], F32) for i in range(2)]; o += 4096
        BsgB = [Buf("sg0"), Buf("sg1")]
        mean_sb = carve(o, [512], F32); o += 2048
        Bmean = Buf("mean")
        var_sb = carve(o, [512], F32); o += 2048
        Bvar = Buf("var")
        localB = Bz + [Bdg, BdgA, BdgB] + By32 + Bybf + BsgB + [Bmean, Bvar]
        phase_begin(localB)
        for cc in range(4):
            P.op(DVE, lambda e, cc=cc: e.memset(zT[:, cc, 0:32], 0.0), writes=[Bz[cc]])
        kk = {"i": 0}
        for cc in range(4):
            slot, sb = wload(l, "wb", cc * 128 * 2048, 2048)
            wv = slot[:, 0:2048].rearrange("p (q c n) -> p q c n", q=2, c=8)
            for tg in range(4):
                tok = slice(tg * 512, tg * 512 + 512)
                pa, Bpa = PS[0], BPS[0]
                pg_, Bpg = PS[1], BPS[1]
                for c in range(8):
                    mm(pa[:], wv[:, 0, c, :], hT[:, c, tok], c == 0, c == 7, [sb, Bh[c][tg]], [Bpa])
                for c in range(8):
                    mm(pg_[:], wv[:, 1, c, :], hT[:, c, tok], c == 0, c == 7, [sb, Bh[c][tg]], [Bpg])
                k = kk["i"] % 2
                kk["i"] += 1
                P.op(ACT, lambda e, k=k, pg_=pg_: e.activation(out=sgB[k], in_=pg_[:], func=AF.Sigmoid),
                     reads=[Bpg], writes=[BsgB[k]])
                P.op(DVE, lambda e, k=k, pa=pa, cc=cc, tg=tg: e.tensor_tensor(
                    zT[:, cc, 32 + tg * 512: 32 + tg * 512 + 512], pa[:], sgB[k], ALU.mult),
                    reads=[Bpa, BsgB[k]], writes=[Bz[cc]])
        for tg in range(4):
            tok = slice(tg * 512, tg * 512 + 512)
            pm, Bpm = PS[2], BPS[2]
            pq2, Bpq2 = PS[3], BPS[3]
            for cc in range(4):
                for j in range(16):
                    P.op(DVE, lambda e, j=j, cc=cc: e.tensor_scalar(
                        dg[:, j, :], identf[:], pcol(l, PC_CW + cc * 31 + j), None, ALU.mult),
                        reads=[Bidf, Bprm], writes=[BdgA])
                for j in range(16, 31):
                    P.op(ACT, lambda e, j=j, cc=cc: e.activation(
                        out=dg[:, j, :], in_=identf[:], func=AF.Copy, scale=pcol(l, PC_CW + cc * 31 + j)),
                        reads=[Bidf, Bprm], writes=[BdgB])
                py, Bpy = PS[4 + (cc % 2)], BPS[4 + (cc % 2)]
                for j in range(31):
                    s0 = 2 + tg * 512 + j
                    mm(py[:], dg[:, j, :], zT[:, cc, s0:s0 + 512], j == 0, j == 30,
                       [BdgA if j < 16 else BdgB, Bz[cc]], [Bpy])
                P.op(ACT, lambda e, cc=cc, py=py: e.activation(
                    out=y32[:, cc, :], in_=py[:], func=AF.Identity, bias=pcol(l, PC_CB + cc)),
                    reads=[Bpy, Bprm], writes=[By32[cc]])
                k = kk["i"] % 2
                kk["i"] += 1
                P.op(ACT, lambda e, cc=cc, k=k: e.activation(out=ybf[k], in_=y32[:, cc, :], func=AF.Copy),
                     reads=[By32[cc]], writes=[Bybf[k]])
                mm(pm[:], AVG512, ybf[k], cc == 0, cc == 3, [Bybf[k], Bcm], [Bpm])
                stats_accum(sqb, Bsq, y32[:, cc, :], [By32[cc]], pq2, Bpq2, AVG512, cc == 0, cc == 3)
            P.op(ACT, lambda e, pm=pm: e.activation(out=mean_sb, in_=pm[:], func=AF.Copy), reads=[Bpm], writes=[Bmean])
            P.op(DVE, lambda e: e.tensor_tensor(var_sb, mean_sb, mean_sb, ALU.mult), reads=[Bmean], writes=[Bvar])
            P.op(DVE, lambda e, pq2=pq2: e.tensor_tensor(var_sb, pq2[:], var_sb, ALU.subtract),
                 reads=[Bpq2, Bvar], writes=[Bvar])
            P.op(ACT, lambda e: e.activation(out=var_sb, in_=var_sb, func=AF.Ln, bias=epsc[float(EPS)]),
                 reads=[Bvar, Beps], writes=[Bvar])
            P.op(ACT, lambda e: e.activation(out=var_sb, in_=var_sb, func=AF.Exp, scale=-0.5),
                 reads=[Bvar], writes=[Bvar])
            for cc in range(4):
                k = kk["i"] % 2
                kk["i"] += 1
                P.op(DVE, lambda e, cc=cc, k=k: e.tensor_tensor(sgB[k], y32[:, cc, :], mean_sb, ALU.subtract),
                     reads=[By32[cc], Bmean], writes=[BsgB[k]])
                P.op(DVE, lambda e, k=k: e.tensor_tensor(sgB[k], sgB[k], var_sb, ALU.mult),
                     reads=[BsgB[k], Bvar], writes=[BsgB[k]])
                P.op(ACT, lambda e, cc=cc, k=k, tok=tok: e.activation(
                    out=ob[:, cc, tok], in_=sgB[k], func=AF.Silu, bias=pcol(l, PC_CNB + cc), scale=pcol(l, PC_CG + cc)),
                    reads=[BsgB[k], Bprm], writes=[Bob[cc][tg]])

        if dbg and _os.environ.get("KDBGZ"):
            P.dma(SP, dbg_out[6:10].rearrange("c p t -> p c t"), zT[:, :, 32:32 + SEQ], reads=Bz, writes=[Bdbg], semb=Bdbg)
        phase_end(localB)
        o = 75776
        tabs = [carve(o + i * 4096, [2, 512], F32) for i in range(2)]; o += 8192
        qh = carve(o, [SEQ], BF16); o += SEQ * 2
        kh = carve(o, [SEQ], BF16); o += SEQ * 2
        vh = carve(o, [16, 128], BF16); o += 16 * 128 * 2
        Bqh = [Buf("qh%d" % t) for t in range(4)]
        Bkh = [Buf("kh%d" % t) for t in range(4)]
        Bvh = Buf("vh")
        qbf = carve(o, [512], BF16); o += 1024
        t1 = carve(o, [512], F32); o += 2048
        t2 = carve(o, [512], F32); o += 2048
        PT = [carve(o + i * 1024, [512], BF16) for i in range(3)]; o += 3072
        d0 = carve(o, [512], F32); o += 2048
        d1 = carve(o, [512], F32); o += 2048
        Bd0, Bd1 = Buf("d0"), Buf("d1")
        lam_scale = 1.0 / (1.0 - float(lam_inits[l]))
        localC = Btab + Bqh + Bkh + [Bvh, Bqbf, Bt1, Bt2, Bd0, Bd1] + BPT
        phase_begin(localC)
        for h in range(4):
            slot, sb = wload(l, "wc", h * 128 * 3072, 3072)
            wv = slot[:, 0:3072].rearrange("p (q c n) -> p q c n", q=3, c=8)
            rits = []
            setsC = [(qbf, Bqbf, t1, Bt1, t2, Bt2), (PT[2], BPT[2], d0, Bd0, d1, Bd1)]
            for tg in range(4):
                tok = slice(tg * 512, tg * 512 + 512)
                tref = {}
                rits.append(rope_stages(wv[:, 0], sb, tg, tref, True, setsC, qh[:, tok], [Bqh[tg]], 1))
                rits.append(rope_stages(wv[:, 1], sb, tg, tref, False, setsC, kh[:, tok], [Bkh[tg]], 1))
            run_pipeline(rits, 1)
            for b0 in range(0, 16, 4):
                pv, Bpv = PS[2], BPS[2]
                for bi in range(4):
                    kb = b0 + bi
                    for c in range(8):
                        mm(pv[:, bi * 128: bi * 128 + 128], hT[:, c, kb * 128: kb * 128 + 128], wv[:, 2, c, :],
                           c == 0, c == 7, [sb, Bh[c][kb // 4]], [Bpv])
                P.op(ACT, lambda e, b0=b0, pv=pv: e.activation(
                    out=vh[:, b0:b0 + 4, :], in_=pv[:].rearrange("p (a n) -> p a n", a=4), func=AF.Copy),
                    reads=[Bpv], writes=[Bvh])
            iters = []
            pending = []
            for Q in range(4):
                tok = slice(Q * 512, Q * 512 + 512)
                for m in range(2):
                    hr = slice(m * 64, m * 64 + 64)
                    po, Bpo = PS[3 + m], BPS[3 + m]
                    pdn, Bpdn = PS[0 + m], BPS[0 + m]
                    nkb = 4 * Q + 4
                    for kb in range(nkb):
                        off = 0 if kb < 4 * Q else (kb - 4 * Q) * 128
                        n = 512 - off
                        bi3 = ptk["i"] % 3
                        ptk["i"] += 1
                        pss, Bpss = PS[5 + bi3], BPS[5 + bi3]
                        pt, Bpt = PT[bi3], BPT[bi3]

                        def st1(pss=pss, Bpss=Bpss, pt=pt, Bpt=Bpt, hr=hr, kb=kb, Q=Q, off=off, n=n):
                            mm(pss[:, 0:n], kh[hr, kb * 128: kb * 128 + 128], qh[hr, Q * 512 + off: Q * 512 + 512],
                               True, True, [Bkh[kb // 4], Bqh[Q]], [Bpss])
                            P.op(ACT, lambda e: e.activation(out=pt[:, 0:n], in_=pss[:, 0:n], func=AF.Exp, scale=0.125),
                                 reads=[Bpss], writes=[Bpt])
                            if kb >= 4 * Q:
                                P.op(DVE, lambda e: e.tensor_tensor(pt[:, 0:128], pt[:, 0:128], MASK[:, 0:128], ALU.mult),
                                     reads=[Bpt, Bcm], writes=[Bpt])

                        def st2(pt=pt, Bpt=Bpt, kb=kb, nkb=nkb, off=off, n=n, po=po, Bpo=Bpo, pdn=pdn, Bpdn=Bpdn):
                            mm(po[:, off:512], vh[:, kb, :], pt[:, 0:n], kb == 0, kb == nkb - 1, [Bvh, Bpt], [Bpo])
                            mm(pdn[:, off:512], ONES, pt[:, 0:n], kb == 0, kb == nkb - 1, [Bcm, Bpt], [Bpdn])
                        iters.append([st1, st2])

                def combine(h=h, tok=tok, Q=Q):
                    P.op(ACT, lambda e: e.activation(out=d0, in_=PS[0][:], func=AF.Ln), reads=[BPS[0]], writes=[Bd0])
                    P.op(DVE, lambda e: e.tensor_copy(t1, PS[3][:]), reads=[BPS[3]], writes=[Bt1])
                    P.op(ACT, lambda e: e.activation(out=d1, in_=PS[1][:], func=AF.Ln), reads=[BPS[1]], writes=[Bd1])
                    P.op(DVE, lambda e: e.tensor_copy(t2, PS[4][:]), reads=[BPS[4]], writes=[Bt2])
                    P.op(ACT, lambda e: e.activation(out=d0, in_=d0, func=AF.Exp, scale=-1.0), reads=[Bd0], writes=[Bd0])
                    P.op(ACT, lambda e: e.activation(out=d1, in_=d1, func=AF.Exp, scale=-1.0), reads=[Bd1], writes=[Bd1])
                    P.op(DVE, lambda e: e.tensor_tensor(t1, t1, d0, ALU.mult), reads=[Bt1, Bd0], writes=[Bt1])
                    P.op(DVE, lambda e: e.tensor_tensor(t2, t2, d1, ALU.mult), reads=[Bt2, Bd1], writes=[Bt2])
                    P.op(DVE, lambda e: e.scalar_tensor_tensor(
                        out=t1, in0=t2, scalar=lam_sb[:, n_layers + l: n_layers + l + 1], in1=t1,
                        op0=ALU.mult, op1=ALU.add), reads=[Bt1, Bt2, Blam], writes=[Bt1])

                def combine2(h=h, tok=tok, Q=Q):
                    stats_accum(sqb, Bsq, t1, [Bt1], PS[2], BPS[2], AVG128, True, True, scale=lam_scale)
                    rstd_from(PS[2], BPS[2], d0, [Bd0], eps=float(EPS / (1.0 - float(lam_inits[l])) ** 2), lnexp=True)
                    P.op(DVE, lambda e: e.scalar_tensor_tensor(
                        out=oc[:, h, tok], in0=t1, scalar=pcol(l, PC_SUB), in1=d0, op0=ALU.mult, op1=ALU.mult),
                        reads=[Bt1, Bd0, Bprm], writes=[Boc[h][Q]])
                last2 = iters[-1][1]
                iters[-1][1] = (lambda f=last2, cb=combine: (f(), cb()))
                pending.append((len(iters) - 1, combine2))
            for (idx, cb2) in pending:
                j = min(idx + 7, len(iters) - 1)
                f0 = iters[j][1]
                iters[j][1] = (lambda f=f0, cb=cb2: (f(), cb()))
            run_pipeline(iters, 2)

        if dbg:
            P.dma(SP, dbg_out[0:2].rearrange("c p t -> p c t"), oa, reads=flat(Boa), writes=[Bdbg], semb=Bdbg)
            P.dma(SP, dbg_out[2:6].rearrange("c p t -> p c t"), ob, reads=flat(Bob), writes=[Bdbg], semb=Bdbg)
            if not _os.environ.get("KDBGZ"):
                P.dma(SP, dbg_out[6:10].rearrange("c p t -> p c t"), oc, reads=flat(Boc), writes=[Bdbg], semb=Bdbg)
        phase_end(localC + flat(Bh))
        htg = carve(0, [8, 2, 512], BF16)
        Bhtg = [[Buf("htg%d_%d" % (c, t)) for t in range(2)] for c in range(8)]
        mg = carve(16384, [8, 2, 512], BF16)
        Bmg = [[Buf("mg%d_%d" % (c, t)) for t in range(2)] for c in range(8)]
        o = 75776
        sg = [carve(o + i * 2048, [512], F32) for i in range(2)]; o += 4096
        Bsg = [Buf("gsg0"), Buf("gsg1")]
        macc = [carve(o + i * 2048, [512], F32) for i in range(2)]; o += 4096
        Bmacc = [Buf("macc0"), Buf("macc1")]
        tt = [carve(o + i * 2048, [512], F32) for i in range(2)]; o += 4096
        Btt = [Buf("tt0"), Buf("tt1")]
        yT = carve(o, [8, 512], F32); o += 16384
        By = [Buf("my%d" % f) for f in range(8)]
        localG = flat(Bhtg) + flat(Bmg) + By + Bsg + Bmacc + Btt
        phase_begin(localG)
        obr = (oa, ob, oc)
        Bobr = (Boa, Bob, Boc)
        nkc = (2, 4, 4)
        poff = (0, 2, 6)
        for tp in range(2):
            for ti in range(2):
                tg = tp * 2 + ti
                tok = slice(tg * 512, tg * 512 + 512)
                for c in range(8):
                    P.op(DVE, lambda e, c=c, tok=tok, ti=ti: e.scalar_tensor_tensor(
                        out=htg[:, c, ti, :], in0=xT[:, c, tok], scalar=pcol(l, PC_NORM + 16 + c),
                        in1=rstd_mix[:, tok], op0=ALU.mult, op1=ALU.mult),
                        reads=[Bx[c][tg], Brmix[tg], Bprm], writes=[Bhtg[c][ti]])
            for f in range(8):
                slotg, sbg = wload(l, "gg", f * 128 * 3072, 3072)
                gv = slotg[:, 0:3072].rearrange("p (b c n) -> p b c n", b=3, c=8)
                slotp, sbp = wload(l, "gp", f * 128 * 1280, 1280)
                pv_ = slotp[:, 0:1280].rearrange("p (c n) -> p c n", c=10)
                for ti in range(2):
                    tg = tp * 2 + ti
                    tok = slice(tg * 512, tg * 512 + 512)
                    for b in range(3):
                        pgt, Bpgt = PS[(2 * b) % 6], BPS[(2 * b) % 6]
                        pyb, Bpyb = PS[(2 * b + 1) % 6], BPS[(2 * b + 1) % 6]
                        for c in range(8):
                            mm(pgt[:], gv[:, b, c, :], htg[:, c, ti, :], c == 0, c == 7, [sbg, Bhtg[c][ti]], [Bpgt])
                        for c in range(nkc[b]):
                            mm(pyb[:], pv_[:, poff[b] + c, :], obr[b][:, c, tok], c == 0, c == nkc[b] - 1,
                               [sbp, Bobr[b][c][tg]], [Bpyb])
                        k = kk["i"] % 2
                        kk["i"] += 1
                        P.op(ACT, lambda e, k=k, pgt=pgt, b=b, f=f: e.activation(
                            out=sg[k], in_=pgt[:], func=AF.Sigmoid, bias=pcol(l, PC_BG + b * 8 + f)),
                            reads=[Bpgt, Bprm], writes=[Bsg[k]])
                        if b == 0:
                            P.op(DVE, lambda e, k=k, pyb=pyb, ti=ti: e.tensor_tensor(macc[ti], sg[k], pyb[:], ALU.mult),
                                 reads=[Bsg[k], Bpyb], writes=[Bmacc[ti]])
                        else:
                            P.op(DVE, lambda e, k=k, pyb=pyb: e.tensor_tensor(tt[k], sg[k], pyb[:], ALU.mult),
                                 reads=[Bsg[k], Bpyb], writes=[Btt[k]])
                            if b == 1:
                                P.op(DVE, lambda e, k=k, ti=ti: e.tensor_tensor(macc[ti], macc[ti], tt[k], ALU.add),
                                     reads=[Bmacc[ti], Btt[k]], writes=[Bmacc[ti]])
                            else:
                                P.op(DVE, lambda e, k=k, f=f, ti=ti: e.tensor_tensor(mg[:, f, ti, :], macc[ti], tt[k], ALU.add),
                                     reads=[Bmacc[ti], Btt[k]], writes=[Bmg[f][ti]])
            for ti in range(2):
                tg = tp * 2 + ti
                tok = slice(tg * 512, tg * 512 + 512)
                for f0, nf in ((0, 3), (3, 3), (6, 2)):
                    slot, sb = wload(l, "wo", f0 * 128 * 1024, nf * 1024, npanel=nf)
                    wv = slot[:, 0:nf * 1024].rearrange("p (f c n) -> p f c n", f=nf, c=8)
                    for fi in range(nf):
                        f = f0 + fi
                        pw, Bpw = PS[f % 2], BPS[f % 2]
                        for c in range(8):
                            mm(pw[:], wv[:, fi, c, :], mg[:, c, ti, :], c == 0, c == 7, [sb, Bmg[c][ti]], [Bpw])
                        P.op(DVE, lambda e, f=f, pw=pw: e.tensor_copy(yT[:, f, :], pw[:]), reads=[Bpw], writes=[By[f]])
                        stats_accum(sqb, Bsq, pw[:], [Bpw], PS[6], BPS[6], AVG1024, f == 0, f == 7)
                r = sqk["r"] % 2
                sqk["r"] += 1
                rstd_from(PS[6], BPS[6], rstd_t[r][:], [Brstd[r]])
                for f in range(8):
                    k = kk["i"] % 2
                    kk["i"] += 1
                    P.op(DVE, lambda e, f=f, r=r, k=k: e.scalar_tensor_tensor(
                        out=tt[k], in0=yT[:, f, :], scalar=pcol(l, PC_NORM + 24 + f), in1=rstd_t[r][:],
                        op0=ALU.mult, op1=ALU.mult), reads=[By[f], Brstd[r], Bprm], writes=[Btt[k]])
                    P.op(DVE, lambda e, f=f, k=k, tok=tok: e.tensor_tensor(xT[:, f, tok], xT[:, f, tok], tt[k], ALU.add),
                         reads=[Btt[k], Bx[f][tg]], writes=[Bx[f][tg]])
        phase_end(localG + flat(Boa) + flat(Bob) + flat(Boc) + Bsq)

    Bout = Buf("yout")
    flay = ffn_layout()
    fbufs = []
    for item in flay[1::2]:
        for b in item:
            fbufs.extend(b if isinstance(b, list) else [b])
    for s in range(n_seq):
        if s > 0:
            P.dma(POOL, xT[:], xin[s].rearrange("c p t -> p c t"), writes=allx, semb=allx[0])
        for l in range(n_layers):
            if "ffn1" in stages:
                phase_begin(fbufs)
                ffn(l, 0, flay)
                phase_end(fbufs)
            if "mixer" in stages:
                mixer(l)
            if "ffn2" in stages:
                phase_begin(fbufs)
                ffn(l, 1, flay)
                phase_end(fbufs)
        P.force = True
        P.dma(POOL, yout[s].rearrange("c p t -> p c t"), xT[:], reads=allx, writes=[Bout], semb=Bout)
        P.force = False
    P.emit(final_wait_bufs=[Bout] + ([Bdbg] if dbg else []))
    return nc, P


def _panels(W, col0, ncols):
    K = W.shape[0]
    sub = W[:, col0:col0 + ncols].reshape(K // 128, 128, ncols // 128, 128)
    return np.ascontiguousarray(sub.transpose(2, 1, 0, 3))


def pack_layer_weights(inp, l):
    out = np.empty(WL_SZ, np.float32)

    def put(name, arr):
        o, s = OFF[name]
        assert arr.size == s, (name, arr.size, s)
        out[o:o + s] = arr.reshape(-1)

    for idx, pre in ((1, "ffn1"), (2, "ffn2")):
        g = _panels(inp[pre + "_w_gate"][l], 0, DFF)
        u = _panels(inp[pre + "_w_up"][l], 0, DFF)
        put("gu%d" % idx, np.stack([g, u], axis=2))
        put("d%d" % idx, _panels(inp[pre + "_w_down"][l], 0, D))
    win = inp["w_in"][l]
    wp = _panels(win, 0, 7936)
    wa = np.empty((6, 128, 3, 8, 128), np.float32)
    for jp in range(2):
        for g in range(3):
            for q in range(3):
                wa[jp * 3 + g, :, q] = wp[q * 6 + g * 2 + jp]
    put("wa", wa)
    wb = np.empty((4, 128, 2, 8, 128), np.float32)
    for cc in range(4):
        wb[cc, :, 0] = wp[30 + cc]
        wb[cc, :, 1] = wp[34 + cc]
    put("wb", wb)
    wc = np.empty((4, 128, 3, 8, 128), np.float32)
    for h in range(4):
        for q in range(3):
            wc[h, :, q] = wp[18 + q * 4 + h]
    put("wc", wc)
    gg = np.empty((8, 128, 3, 8, 128), np.float32)
    for f in range(8):
        for b in range(3):
            gg[f, :, b] = wp[38 + b * 8 + f]
    put("gg", gg)
    pa = _panels(inp["w_proj_a"][l], 0, D)
    pb = _panels(inp["w_proj_b"][l], 0, D)
    pc = _panels(inp["w_proj_c"][l], 0, D)
    put("gp", np.concatenate([pa, pb, pc], axis=2))
    put("wo", _panels(inp["w_out"][l], 0, D))
    return out.reshape(128, WL_SZ // 128)


def pack_params(inp, layers):
    nl = len(layers)
    prm = np.zeros((128, nl * PC_N), np.float32)
    lam = np.zeros((128, nl * LAMC), np.float32)
    for i, l in enumerate(layers):
        b = i * PC_N
        for k, name in enumerate(("ffn1_norm_pre", "ffn1_norm_post", "mix_norm_pre", "mix_norm_post",
                                  "ffn2_norm_pre", "ffn2_norm_post")):
            prm[:, b + PC_NORM + 8 * k: b + PC_NORM + 8 * k + 8] = inp[name][l].reshape(8, 128).T
        prm[:, b + PC_BG: b + PC_BG + 24] = inp["b_gate"][l].reshape(24, 128).T
        cw = inp["conv_w"][l]
        for cc in range(4):
            prm[:, b + PC_CW + cc * 31: b + PC_CW + cc * 31 + 31] = cw[:, cc * 128:(cc + 1) * 128].T
        prm[:, b + PC_CB: b + PC_CB + 4] = inp["conv_b"][l].reshape(4, 128).T
        prm[:, b + PC_CG: b + PC_CG + 4] = inp["conv_norm_g"][l].reshape(4, 128).T
        prm[:, b + PC_CNB: b + PC_CNB + 4] = inp["conv_norm_b"][l].reshape(4, 128).T
        prm[:, b + PC_SUB] = inp["diff_subln"][l]
        for k, name in enumerate(("lambda_q1", "lambda_k1", "lambda_q2", "lambda_k2")):
            lam[:, i * LAMC + 64 * k: i * LAMC + 64 * k + 64] = inp[name][l][None, :]
    return prm, lam


def const_tables():
    inv = (10000.0 ** (-np.arange(0, 64, 2, dtype=np.float32) / 64)).astype(np.float32)
    ang = np.arange(SEQ, dtype=np.float32)[None, :] * inv[:, None]
    cos = np.cos(ang).astype(np.float32)
    sin = np.sin(ang).astype(np.float32)
    C = np.tile(cos, (4, 1))
    S_ = np.tile(sin, (4, 1))
    rope = np.stack([C.reshape(128, 4, 512), S_.reshape(128, 4, 512)], axis=2)
    rope = np.ascontiguousarray(rope.transpose(1, 0, 2, 3))
    cm = np.zeros((128, 7 * 128 + 256), np.float32)
    cm[:, 0:128] = 1.0 / 1024
    cm[:, 128:256] = 1.0 / 512
    cm[:, 256:384] = 1.0 / 128
    cm[:, 384:512] = 1.0
    R = np.zeros((128, 128), np.float32)
    for blk in (0, 64):
        for dd in range(32):
            R[blk + dd + 32, blk + dd] = -1.0
            R[blk + dd, blk + dd + 32] = 1.0
    cm[:, 512:640] = R
    cm[:, 640:768] = np.eye(128, dtype=np.float32)
    i = np.arange(128)[:, None]
    j = np.arange(128)[None, :]
    cm[:, 896:1024] = (j >= i).astype(np.float32)
    cm[:, 1024:1152] = (j <= i).astype(np.float32)
    return rope, cm


_CACHE = {}


def _get_prog(n_layers, n_seq, lam_inits):
    key = (n_layers, n_seq, tuple(lam_inits))
    if key not in _CACHE:
        _CACHE[key] = build_program(n_layers, n_seq, lam_inits)[0]
    return _CACHE[key]


MODE = "fused"


def kernel(**inputs):
    inp = {k: np.asarray(v) for k, v in inputs.items()}
    x = inp["x"]
    B = x.shape[0]
    xT = np.ascontiguousarray(x.reshape(NCORES, SEQ_PER_CORE, SEQ, 8, 128).transpose(0, 1, 3, 4, 2))
    layers = list(range(DEPTH))
    rope, cm = const_tables()
    if MODE == "fused":
        wts = np.stack([pack_layer_weights(inp, l) for l in layers], axis=0)
        prm, lam = pack_params(inp, layers)
        nc = _get_prog(DEPTH, SEQ_PER_CORE, [_lam_init(l) for l in layers])
        in_maps = [{"xin": xT[c], "wts": wts, "prm": prm, "lamrows": lam, "rope": rope, "cmat": cm}
                   for c in range(NCORES)]
        res = run_bass_kernel_spmd(nc, in_maps, core_ids=list(range(NCORES)))
        yT = np.stack([res.results[c]["yout"] for c in range(NCORES)], axis=0)
    else:
        yT = xT
        for l in layers:
            wts = pack_layer_weights(inp, l)[None]
            prm, lam = pack_params(inp, [l])
            nc = _get_prog(1, 1, [_lam_init(l)])
            nxt = np.empty_like(yT)
            for h in range(SEQ_PER_CORE):
                in_maps = [{"xin": np.ascontiguousarray(yT[c, h:h + 1]), "wts": wts, "prm": prm,
                            "lamrows": lam, "rope": rope, "cmat": cm} for c in range(NCORES)]
                res = run_bass_kernel_spmd(nc, in_maps, core_ids=list(range(NCORES)))
                for c in range(NCORES):
                    nxt[c, h:h + 1] = res.results[c]["yout"]
            yT = nxt
    y = yT.transpose(0, 1, 4, 2, 3).reshape(B, SEQ, D)
    return np.ascontiguousarray(y.astype(np.float32))
```
